# Optimizing a Trainium2 kernel written in Bass

```python
import jax
import jax.numpy as jnp
from jax import lax
import numpy as np

D_MODEL = 1024
BATCH = 16
SEQ = 2048
DEPTH = 2

GRID_W = 64
CTX_LEN = 256
HEAD_DIM = 64
ATT_HEADS = 8
ATT_KV_HEADS = 2
RWKV_HEADS = 8
NA_HEADS = 8
D_BRANCH = 512
D_KV = ATT_KV_HEADS * HEAD_DIM
DECAY_LORA = 64
ICLR_LORA = 64
GATE_LORA = 128
D_FF = 2816
N_BRANCH = 3
N_MOD = 9
Q_BLOCK = 128
NA_KH_MAX = 8
NA_KW = 16
ROPE_THETA = 10000.0
LN_EPS = 1e-5
RMS_EPS = 1e-6
GN_EPS = 64e-5
ALPHA = (2 * DEPTH) ** 0.25
BETA = (8 * DEPTH) ** -0.25

D_RWKV_IN = 3 * D_BRANCH + 2 * DECAY_LORA + 2 * ICLR_LORA + GATE_LORA
RWKV_SPLITS = (D_BRANCH, 2 * D_BRANCH, 3 * D_BRANCH, 3 * D_BRANCH + 2 * DECAY_LORA, 3 * D_BRANCH + 2 * DECAY_LORA + 2 * ICLR_LORA)
IN_SPLITS = (D_BRANCH, D_BRANCH + D_KV, D_BRANCH + 2 * D_KV, D_BRANCH + 2 * D_KV + D_RWKV_IN, 2 * D_BRANCH + 2 * D_KV + D_RWKV_IN, 3 * D_BRANCH + 2 * D_KV + D_RWKV_IN, 4 * D_BRANCH + 2 * D_KV + D_RWKV_IN)
D_IN = IN_SPLITS[-1] + N_BRANCH * D_MODEL

kernel_name = 'hybrid_gated_mixer_diffusion_block'


def layer_norm(x, g, b):
    xf = x.astype(jnp.float32)
    mu = jnp.mean(xf, -1, keepdims=True)
    var = jnp.mean(jnp.square(xf - mu), -1, keepdims=True)
    return ((xf - mu) * lax.rsqrt(var + LN_EPS)).astype(x.dtype) * g + b


def rms_norm(x, g):
    xf = x.astype(jnp.float32)
    return (xf * lax.rsqrt(jnp.mean(jnp.square(xf), -1, keepdims=True) + RMS_EPS)).astype(x.dtype) * g


def modulate(x, shift, scale):
    return x * (1 + scale) + shift


def post_norm(x, y, g, b):
    return layer_norm(ALPHA * x + y, g, b)


def swiglu(u, wi, wo):
    gate, up = jnp.split(u @ wi, 2, axis=-1)
    return (jax.nn.silu(gate) * up) @ wo


def heads(t, n):
    return t.reshape(t.shape[0], t.shape[1], n, HEAD_DIM)


def axial_rope_tables(n_tok):
    t = jnp.arange(n_tok)
    row = (t // GRID_W).astype(jnp.float32)
    col = (t % GRID_W).astype(jnp.float32)
    half = HEAD_DIM // 2
    inv = ROPE_THETA ** (-jnp.arange(0, half, 2, dtype=jnp.float32) / half)
    ang = jnp.stack([row[:, None] * inv, col[:, None] * inv], 1)
    return jnp.cos(ang), jnp.sin(ang)


def apply_axial_rope(x, cos, sin):
    B, T, H, d = x.shape
    xr = x.reshape(B, T, H, 2, 2, d // 4)
    x1, x2 = xr[..., 0, :], xr[..., 1, :]
    c = cos[:, None].astype(x.dtype)
    s = sin[:, None].astype(x.dtype)
    return jnp.stack([x1 * c - x2 * s, x2 * c + x1 * s], axis=-2).reshape(B, T, H, d)


def attend_grouped(q, k, v):
    s = jnp.einsum('blkgd,bskd->bkgls', q, k).astype(jnp.float32) * (HEAD_DIM ** -0.5)
    p = jax.nn.softmax(s, axis=-1).astype(v.dtype)
    return jnp.einsum('bkgls,bskd->blkgd', p, v)


def gqa_branch(q, k, v, qc, kc, vc, q_gain, k_gain, need_ctx):
    B, T = q.shape[:2]
    C = qc.shape[1]
    G = ATT_HEADS // ATT_KV_HEADS
    cos, sin = axial_rope_tables(T)
    q = apply_axial_rope(rms_norm(q, q_gain), cos, sin)
    k = apply_axial_rope(rms_norm(k, k_gain), cos, sin)
    kc = rms_norm(kc, k_gain)
    keys = jnp.concatenate([k, kc], 1)
    vals = jnp.concatenate([v, vc], 1)
    n_blk = T // Q_BLOCK
    qb = q.reshape(B, n_blk, Q_BLOCK, ATT_KV_HEADS, G, HEAD_DIM).transpose(1, 0, 2, 3, 4, 5)
    ob = lax.map(lambda blk: attend_grouped(blk, keys, vals), qb)
    y = ob.transpose(1, 0, 2, 3, 4, 5).reshape(B, T, D_BRANCH)
    y_c = None
    if need_ctx:
        qc = rms_norm(qc, q_gain).reshape(B, C, ATT_KV_HEADS, G, HEAD_DIM)
        y_c = attend_grouped(qc, kc, vc).reshape(B, C, D_BRANCH)
    return y, y_c


def na_branch(q, k, v, qc, kc, vc, rpb, need_ctx):
    B, T = q.shape[:2]
    C = qc.shape[1]
    rows = T // GRID_W
    kh = min(NA_KH_MAX, rows)
    scale = HEAD_DIM ** -0.5
    qg = jnp.moveaxis(q.reshape(B, rows, GRID_W, NA_HEADS, HEAD_DIM), 1, 0)
    kg = k.reshape(B, rows, GRID_W, NA_HEADS, HEAD_DIM)
    vg = v.reshape(B, rows, GRID_W, NA_HEADS, HEAD_DIM)
    cols = jnp.arange(GRID_W)
    c0 = jnp.clip(cols - NA_KW // 2, 0, GRID_W - NA_KW)
    col_idx = c0[:, None] + jnp.arange(NA_KW)[None, :]
    rpb_cols = rpb[:, :, col_idx - cols[:, None] + (NA_KW - 1)]

    def one_row(args):
        r, q_row = args
        r0 = jnp.clip(r - kh // 2, 0, rows - kh)
        k_win = lax.dynamic_slice_in_dim(kg, r0, kh, axis=1)[:, :, col_idx]
        v_win = lax.dynamic_slice_in_dim(vg, r0, kh, axis=1)[:, :, col_idx]
        bias = rpb_cols[:, r0 + jnp.arange(kh) - r + (NA_KH_MAX - 1)]
        s_win = jnp.einsum('bqhd,bkqjhd->bhqkj', q_row, k_win).astype(jnp.float32) * scale + jnp.transpose(bias, (0, 2, 1, 3))
        s_ctx = jnp.einsum('bqhd,bshd->bhqs', q_row, kc).astype(jnp.float32) * scale
        s = jnp.concatenate([s_win.reshape(B, NA_HEADS, GRID_W, kh * NA_KW), s_ctx], -1)
        p = jax.nn.softmax(s, axis=-1).astype(v.dtype)
        p_win = p[..., :kh * NA_KW].reshape(B, NA_HEADS, GRID_W, kh, NA_KW)
        return jnp.einsum('bhqkj,bkqjhd->bqhd', p_win, v_win) + jnp.einsum('bhqs,bshd->bqhd', p[..., kh * NA_KW:], vc)

    out = lax.map(one_row, (jnp.arange(rows), qg))
    y = jnp.moveaxis(out, 0, 1).reshape(B, T, D_BRANCH)
    y_c = None
    if need_ctx:
        y_c = attend_grouped(qc.reshape(B, C, NA_HEADS, 1, HEAD_DIM), kc, vc).reshape(B, C, D_BRANCH)
    return y, y_c


def token_shift_centred(y, mu):
    prev = jnp.pad(y, ((0, 0), (1, 0), (0, 0)))[:, :-1]
    nxt = jnp.pad(y, ((0, 0), (0, 1), (0, 0)))[:, 1:]
    return y + mu[0] * (prev - y) + mu[1] * (nxt - y)


def to_scan_layout(t):
    t = jnp.stack([t[:, :, 0], jnp.flip(t[:, :, 1], 1)], 0)
    Z, B, T, _ = t.shape
    return t.reshape(Z, B, T, RWKV_HEADS, HEAD_DIM).transpose(2, 0, 1, 3, 4)


def from_scan_layout(o):
    o = o.transpose(1, 2, 0, 3, 4)
    return o[0] + jnp.flip(o[1], 1)


def rwkv_inputs(y, mu, w0, w2, a0, a2, g2, k_k, k_a):
    B, T, _ = y.shape
    y = token_shift_centred(y, mu)
    r, k, v, wd, ad, gd = jnp.split(y, RWKV_SPLITS, axis=-1)
    wd = wd.reshape(B, T, 2, DECAY_LORA)
    ad = ad.reshape(B, T, 2, ICLR_LORA)
    w_log = -jax.nn.softplus(-(w0 + jnp.einsum('btzr,zrc->btzc', jnp.tanh(wd), w2))) - 0.5
    decay = jnp.exp(-jnp.exp(w_log.astype(jnp.float32)))
    iclr = jax.nn.sigmoid(a0 + jnp.einsum('btzr,zrc->btzc', ad, a2))
    g = jax.nn.sigmoid(gd) @ g2
    kk = (k * k_k).reshape(B, T, RWKV_HEADS, HEAD_DIM)
    kkf = kk.astype(jnp.float32)
    kk = (kkf * lax.rsqrt(jnp.sum(jnp.square(kkf), -1, keepdims=True) + 1e-12)).astype(k.dtype).reshape(B, T, D_BRANCH)
    k_dir = k[:, :, None] * (1 + (iclr - 1) * k_a)
    a_dir = jnp.broadcast_to(-kk[:, :, None], iclr.shape)
    b_dir = kk[:, :, None] * iclr
    r_dir = jnp.broadcast_to(r[:, :, None], iclr.shape)
    v_dir = jnp.broadcast_to(v[:, :, None], iclr.shape)
    xs = (to_scan_layout(r_dir), to_scan_layout(decay), to_scan_layout(k_dir),
          to_scan_layout(v_dir), to_scan_layout(a_dir), to_scan_layout(b_dir))
    return xs, r, k, v, g


def rwkv_scan(state0, xs, emit):
    def step(state, inp):
        r_t, w_t, k_t, v_t, a_t, b_t = inp
        sa = jnp.einsum('zbhvk,zbhk->zbhv', state, a_t)
        state = state * w_t[..., None, :] + sa[..., :, None] * b_t[..., None, :] + v_t[..., :, None] * k_t[..., None, :]
        out = jnp.einsum('zbhvk,zbhk->zbhv', state, r_t) if emit else None
        return state, out
    return lax.scan(step, state0, xs)


def rwkv_readout(o, r, k, v, g, r_k, gn_g, gn_b):
    B, T, _ = r.shape
    o = from_scan_layout(o)
    mu = jnp.mean(o, -1, keepdims=True)
    var = jnp.mean(jnp.square(o - mu), -1, keepdims=True)
    o = ((o - mu) * lax.rsqrt(var + GN_EPS)).reshape(B, T, D_BRANCH).astype(r.dtype) * gn_g + gn_b
    rh, kh, vh = heads(r, RWKV_HEADS), heads(k, RWKV_HEADS), heads(v, RWKV_HEADS)
    bonus = (jnp.sum(rh * kh * r_k, -1, keepdims=True) * vh).reshape(B, T, D_BRANCH)
    return (o + bonus) * g


def rwkv_branch(y_lat, y_ctx, mu, w0, w2, a0, a2, g2, k_k, k_a, r_k, gn_g, gn_b, need_ctx):
    B = y_lat.shape[0]
    xs_l, r_l, k_l, v_l, g_l = rwkv_inputs(y_lat, mu, w0, w2, a0, a2, g2, k_k, k_a)
    xs_c, r_c, k_c, v_c, g_c = rwkv_inputs(y_ctx, mu, w0, w2, a0, a2, g2, k_k, k_a)
    state0 = jnp.zeros((2, B, RWKV_HEADS, HEAD_DIM, HEAD_DIM), jnp.float32)
    state_ctx, o_c = rwkv_scan(state0, xs_c, need_ctx)
    _, o_l = rwkv_scan(state_ctx, xs_l, True)
    y = rwkv_readout(o_l, r_l, k_l, v_l, g_l, r_k, gn_g, gn_b).astype(y_lat.dtype)
    y_c = None
    if need_ctx:
        y_c = rwkv_readout(o_c, r_c, k_c, v_c, g_c, r_k, gn_g, gn_b).astype(y_ctx.dtype)
    return y, y_c


def merge_branches(ya, yr, yn, gate_logits, w_branch, w_out):
    B, T, _ = ya.shape
    ys = jnp.stack([ya, yr, yn], 2)
    proj = jnp.einsum('btzc,zcd->btzd', ys, w_branch)
    gates = jax.nn.sigmoid(gate_logits.reshape(B, T, N_BRANCH, D_MODEL))
    return jnp.sum(gates * proj, 2) @ w_out


def hybrid_mixer(ux, uc, w_in, q_gain, k_gain, mu, w0, w2, a0, a2, g2, k_k, k_a, r_k, gn_g, gn_b, rpb, w_branch, w_out, need_ctx):
    px = jnp.split(ux @ w_in, IN_SPLITS, axis=-1)
    pc = jnp.split(uc @ w_in, IN_SPLITS, axis=-1)
    ya, ya_c = gqa_branch(heads(px[0], ATT_HEADS), heads(px[1], ATT_KV_HEADS), heads(px[2], ATT_KV_HEADS),
                          heads(pc[0], ATT_HEADS), heads(pc[1], ATT_KV_HEADS), heads(pc[2], ATT_KV_HEADS),
                          q_gain, k_gain, need_ctx)
    yr, yr_c = rwkv_branch(px[3], pc[3], mu, w0, w2, a0, a2, g2, k_k, k_a, r_k, gn_g, gn_b, need_ctx)
    yn, yn_c = na_branch(heads(px[4], NA_HEADS), heads(px[5], NA_HEADS), heads(px[6], NA_HEADS),
                         heads(pc[4], NA_HEADS), heads(pc[5], NA_HEADS), heads(pc[6], NA_HEADS), rpb, need_ctx)
    y_x = merge_branches(ya, yr, yn, px[7], w_branch, w_out)
    y_c = merge_branches(ya_c, yr_c, yn_c, pc[7], w_branch, w_out) if need_ctx else None
    return y_x, y_c


def setup_inputs(seed: int = 0) -> dict:
    key = jax.random.key(seed)
    ks = iter(jax.random.split(key, 40))
    L = DEPTH

    def nrm(shape, s):
        return s * jax.random.normal(next(ks), shape, jnp.float32)

    def unif(shape, lo, hi):
        return jax.random.uniform(next(ks), shape, jnp.float32, lo, hi)

    return {
        'x': nrm((BATCH, SEQ, D_MODEL), 1.0),
        'c': nrm((BATCH, D_MODEL), 1.0),
        'ctx': nrm((BATCH, CTX_LEN, D_MODEL), 1.0),
        'c_ctx': nrm((D_MODEL,), 1.0),
        'w_ada': nrm((L, D_MODEL, N_MOD * D_MODEL), 0.5 * D_MODEL ** -0.5),
        'b_ada': nrm((L, N_MOD * D_MODEL), 0.01),
        'ffn1_wi': nrm((L, D_MODEL, 2 * D_FF), D_MODEL ** -0.5),
        'ffn1_wo': nrm((L, D_FF, D_MODEL), BETA * D_FF ** -0.5),
        'ffn2_wi': nrm((L, D_MODEL, 2 * D_FF), D_MODEL ** -0.5),
        'ffn2_wo': nrm((L, D_FF, D_MODEL), BETA * D_FF ** -0.5),
        'ln_g': 1.0 + nrm((L, 3, D_MODEL), 0.02),
        'ln_b': nrm((L, 3, D_MODEL), 0.02),
        'w_in': nrm((L, D_MODEL, D_IN), D_MODEL ** -0.5),
        'att_q_gain': 1.0 + nrm((L, HEAD_DIM), 0.02),
        'att_k_gain': 1.0 + nrm((L, HEAD_DIM), 0.02),
        'rwkv_mu': unif((L, 2, D_RWKV_IN), 0.0, 0.5),
        'rwkv_w0': unif((L, 2, D_BRANCH), -6.0, 0.0),
        'rwkv_w2': nrm((L, 2, DECAY_LORA, D_BRANCH), 0.5 * DECAY_LORA ** -0.5),
        'rwkv_a0': nrm((L, 2, D_BRANCH), 0.1),
        'rwkv_a2': nrm((L, 2, ICLR_LORA, D_BRANCH), 0.5 * ICLR_LORA ** -0.5),
        'rwkv_g2': nrm((L, GATE_LORA, D_BRANCH), GATE_LORA ** -0.5),
        'rwkv_k_k': 0.85 + nrm((L, D_BRANCH), 0.02),
        'rwkv_k_a': 1.0 + nrm((L, D_BRANCH), 0.02),
        'rwkv_r_k': nrm((L, RWKV_HEADS, HEAD_DIM), 0.1),
        'rwkv_gn_g': 1.0 + nrm((L, D_BRANCH), 0.02),
        'rwkv_gn_b': nrm((L, D_BRANCH), 0.02),
        'na_rpb': nrm((L, NA_HEADS, 2 * NA_KH_MAX - 1, 2 * NA_KW - 1), 0.05),
        'w_branch': nrm((L, N_BRANCH, D_BRANCH, D_MODEL), D_BRANCH ** -0.5),
        'w_out': nrm((L, D_MODEL, D_MODEL), BETA * D_MODEL ** -0.5),
    }


def reference(x, c, ctx, c_ctx, w_ada, b_ada, ffn1_wi, ffn1_wo, ffn2_wi, ffn2_wo, ln_g, ln_b, w_in,
              att_q_gain, att_k_gain, rwkv_mu, rwkv_w0, rwkv_w2, rwkv_a0, rwkv_a2, rwkv_g2, rwkv_k_k,
              rwkv_k_a, rwkv_r_k, rwkv_gn_g, rwkv_gn_b, na_rpb, w_branch, w_out):
    for l in range(DEPTH):
        last = l == DEPTH - 1
        m_x = jnp.split((jax.nn.silu(c) @ w_ada[l] + b_ada[l])[:, None, :], N_MOD, axis=-1)
        m_c = jnp.split((jax.nn.silu(c_ctx) @ w_ada[l] + b_ada[l])[None, None, :], N_MOD, axis=-1)

        x = post_norm(x, 0.5 * m_x[2] * swiglu(modulate(x, m_x[0], m_x[1]), ffn1_wi[l], ffn1_wo[l]), ln_g[l, 0], ln_b[l, 0])
        ctx = post_norm(ctx, 0.5 * m_c[2] * swiglu(modulate(ctx, m_c[0], m_c[1]), ffn1_wi[l], ffn1_wo[l]), ln_g[l, 0], ln_b[l, 0])

        y_x, y_c = hybrid_mixer(modulate(x, m_x[3], m_x[4]), modulate(ctx, m_c[3], m_c[4]), w_in[l],
                                att_q_gain[l], att_k_gain[l], rwkv_mu[l], rwkv_w0[l], rwkv_w2[l], rwkv_a0[l],
                                rwkv_a2[l], rwkv_g2[l], rwkv_k_k[l], rwkv_k_a[l], rwkv_r_k[l], rwkv_gn_g[l],
                                rwkv_gn_b[l], na_rpb[l], w_branch[l], w_out[l], not last)
        x = post_norm(x, m_x[5] * y_x, ln_g[l, 1], ln_b[l, 1])
        if not last:
            ctx = post_norm(ctx, m_c[5] * y_c, ln_g[l, 1], ln_b[l, 1])
            ctx = post_norm(ctx, 0.5 * m_c[8] * swiglu(modulate(ctx, m_c[6], m_c[7]), ffn2_wi[l], ffn2_wo[l]), ln_g[l, 2], ln_b[l, 2])

        x = post_norm(x, 0.5 * m_x[8] * swiglu(modulate(x, m_x[6], m_x[7]), ffn2_wi[l], ffn2_wo[l]), ln_g[l, 2], ln_b[l, 2])
    return x
```

```python
import numpy as np
import concourse.bass as bass
import concourse.mybir as mybir
from concourse.bass_utils import run_bass_kernel_spmd

F32 = mybir.dt.float32
BF16 = mybir.dt.bfloat16
AF = mybir.ActivationFunctionType
ALU = mybir.AluOpType
AX = mybir.AxisListType


class Tok:
    __slots__ = ("w", "r", "dsem", "dcount", "name")

    def __init__(self, name=""):
        self.w = None
        self.r = {}
        self.dsem = None
        self.dcount = 0
        self.name = name


class T:
    __slots__ = ("ap", "tok")

    def __init__(self, ap, tok):
        self.ap = ap
        self.tok = tok

    def __getitem__(self, idx):
        return T(self.ap[idx], self.tok)

    def re(self, pat, **kw):
        return T(self.ap.rearrange(pat, **kw), self.tok)

    def bc(self, shape):
        return T(self.ap.to_broadcast(shape), self.tok)

    def bitcast(self, dt):
        return T(self.ap.bitcast(dt), self.tok)

    def wt(self, tok):
        return T(self.ap, tok)

    def r(self):
        return T(self.ap.bitcast(mybir.dt.float32r), self.tok)

    @property
    def shape(self):
        return self.ap.shape


class Prog:
    def __init__(self, nc):
        self.nc = nc
        self.eng = {"pe": nc.tensor, "dve": nc.vector, "act": nc.scalar, "pool": nc.gpsimd, "sp": nc.sync}
        self.sem = {k: nc.alloc_semaphore("sem_" + k) for k in self.eng}
        self.cnt = {k: 0 for k in self.eng}
        self.seen = {k: {} for k in self.eng}
        self.dsems = []
        self.ninst = 0
        self.dpool = []
        self.phase_stack = None

    def dram(self, name, shape, dt, kind="Internal"):
        h = self.nc.dram_tensor(name, list(shape), dt, kind=kind)
        return T(h.ap(), Tok(name))

    def sb(self, name, shape, dt):
        if self.phase_stack is None:
            h = self.nc.alloc_sbuf_tensor(name, list(shape), dt)
            return T(h.ap(), Tok(name))
        self.uid = getattr(self, "uid", 0) + 1
        g = self.nc.sbuf_tensor("%s_%d" % (name, self.uid), list(shape), dt)
        h = g.__enter__()
        t = T(h.ap(), Tok(name))
        self.phase_stack.append((g, t.tok))
        return t

    def phase_begin(self):
        assert self.phase_stack is None
        self.phase_stack = []

    def phase_end(self):
        self.barrier()
        for g, tok in reversed(self.phase_stack):
            if tok.dsem is not None:
                self.dpool.append((tok.dsem, tok.dcount))
                self.dsems.remove(tok)
                tok.dsem = None
            g.__exit__(None, None, None)
        self.phase_stack = None

    def ps(self, name, shape, dt=F32):
        h = self.nc.alloc_psum_tensor(name, list(shape), dt)
        return T(h.ap(), Tok(name))

    def _wait(self, eng, events):
        best = {}
        for ev in events:
            if ev is None:
                continue
            s, v = ev
            if s.num not in best or best[s.num][1] < v:
                best[s.num] = (s, v)
        seen = self.seen[eng]
        e = self.eng[eng]
        for num, (s, v) in best.items():
            if eng == "pe" and s is self.sem["pe"]:
                continue
            if seen.get(num, 0) < v:
                e.wait_ge(s, v)
                seen[num] = v
                self.ninst += 1

    def _deps(self, w, r):
        evs = []
        for t in r:
            evs.append(t.tok.w)
        for t in w:
            evs.append(t.tok.w)
            evs.extend(t.tok.r.values())
        return evs

    def op(self, eng, fn, w, r):
        self._wait(eng, self._deps(w, r))
        inst = fn(self.eng[eng])
        self.cnt[eng] += 1
        self.ninst += 1
        s = self.sem[eng]
        inst.then_inc(s, 1)
        ev = (s, self.cnt[eng])
        for t in r:
            t.tok.r[s.num] = ev
        for t in w:
            t.tok.w = ev
            t.tok.r = {}
        return ev

    def dma(self, eng, out, in_, **kw):
        tok = out.tok
        evs = [in_.tok.w] + list(tok.r.values())
        if tok.w is not None and tok.w[0] is not tok.dsem:
            evs.append(tok.w)
        self._wait(eng, evs)
        if tok.dsem is None:
            if self.dpool:
                tok.dsem, tok.dcount = self.dpool.pop()
            else:
                tok.dsem = self.nc.alloc_semaphore("d%d_%s" % (len(self.dsems), tok.name))
            self.dsems.append(tok)
        inst = self.eng[eng].dma_start(out=out.ap, in_=in_.ap, **kw)
        inst.then_inc(tok.dsem, 16)
        self.ninst += 1
        tok.dcount += 16
        ev = (tok.dsem, tok.dcount)
        in_.tok.r[tok.dsem.num] = ev
        tok.w = ev
        tok.r = {}
        return ev

    def barrier(self):
        evs = [(self.sem[k], self.cnt[k]) for k in self.eng if self.cnt[k] > 0]
        evs += [(t.dsem, t.dcount) for t in self.dsems]
        for k in self.eng:
            self._wait(k, evs)

    def finish(self):
        self.barrier()

    def mm(self, out, lhsT, rhs, start=True, stop=True, r32=False):
        la, ra = lhsT.ap, rhs.ap
        if r32:
            la, ra = la.bitcast(mybir.dt.float32r), ra.bitcast(mybir.dt.float32r)
        return self.op("pe", lambda e: e.matmul(out.ap, la, ra, start=start, stop=stop),
                       [out], [lhsT, rhs])

    def tr(self, out, in_, ident):
        return self.op("pe", lambda e: e.transpose(out.ap, in_.ap, ident.ap), [out], [in_, ident])

    def act(self, out, in_, func, bias=None, scale=None, accum=None, eng="act"):
        kw = {}
        r = [in_]
        w = [out]
        if bias is not None:
            if isinstance(bias, T):
                kw["bias"] = bias.ap
                r.append(bias)
            else:
                kw["bias"] = bias
        if scale is not None:
            if isinstance(scale, T):
                kw["scale"] = scale.ap
                r.append(scale)
            else:
                kw["scale"] = scale
        if accum is not None:
            kw["accum_out"] = accum.ap
            w.append(accum)
        return self.op("act", lambda e: e.activation(out.ap, in_.ap, func, **kw), w, r)

    def copy(self, eng, out, in_):
        if eng == "act":
            return self.op("act", lambda e: e.copy(out.ap, in_.ap), [out], [in_])
        return self.op(eng, lambda e: e.tensor_copy(out.ap, in_.ap), [out], [in_])

    def tt(self, eng, out, a, b, op):
        return self.op(eng, lambda e: e.tensor_tensor(out.ap, a.ap, b.ap, op), [out], [a, b])

    def ts(self, eng, out, a, s1, s2=None, op0=ALU.mult, op1=None, accum=None):
        r = [a]
        w = [out]
        v1 = s1
        v2 = s2
        if isinstance(s1, T):
            r.append(s1)
            v1 = s1.ap
        if isinstance(s2, T):
            r.append(s2)
            v2 = s2.ap
        kw = {}
        if op1 is not None:
            kw["op1"] = op1
        if accum is not None:
            kw["accum_out"] = accum.ap
            w.append(accum)
        return self.op(eng, lambda e: e.tensor_scalar(out.ap, a.ap, v1, v2, op0, **kw), w, r)

    def stt(self, out, in0, scalar, in1, op0, op1, eng="dve"):
        r = [in0, in1]
        v = scalar
        if isinstance(scalar, T):
            r.append(scalar)
            v = scalar.ap
        return self.op(eng, lambda e: e.scalar_tensor_tensor(out.ap, in0.ap, v, in1.ap, op0, op1), [out], r)

    def memset(self, eng, out, val):
        return self.op(eng, lambda e: e.memset(out.ap, val), [out], [])

    def reduce(self, out, in_, op, axis=None, eng="dve"):
        ax = axis if axis is not None else AX.X
        return self.op(eng, lambda e: e.tensor_reduce(out.ap, in_.ap, ax, op), [out], [in_])

    def recip(self, out, in_):
        return self.op("dve", lambda e: e.reciprocal(out.ap, in_.ap), [out], [in_])


NB = 2
TL = 2048
TC = 256
TT = TL + TC
NT = TT // 128
D = 1024
DFF = 2816
DIN = 7296
DEPTH = 2
ALPHA = (2 * DEPTH) ** 0.25
LN_EPS = 1e-5
RMS_EPS = 1e-6
GN_EPS = 64e-5
MASKV = -30000.0
C_AQ, C_AK, C_AV, C_RW, C_NQ, C_NK, C_NV, C_G = 0, 512, 640, 768, 2688, 3200, 3712, 4224


NB = 2
TL = 2048
TC = 256
TT = TL + TC
NT = TT // 128
D = 1024
DFF = 2816
DIN = 7296
DEPTH = 2
ALPHA = (2 * DEPTH) ** 0.25
LN_EPS = 1e-5
RMS_EPS = 1e-6
GN_EPS = 64e-5
MASKV = -30000.0
C_AQ, C_AK, C_AV, C_RW, C_NQ, C_NK, C_NV, C_G = 0, 512, 640, 768, 2688, 3200, 3712, 4224
DRW = 1920

WNAMES = ["w_ada", "b_ada", "ffn1_wi", "ffn1_wo", "ffn2_wi", "ffn2_wo", "ln_g", "ln_b", "w_in",
          "att_q_gain", "att_k_gain", "rwkv_mu", "rwkv_w0", "rwkv_w2", "rwkv_a0", "rwkv_a2", "rwkv_g2",
          "rwkv_k_k", "rwkv_k_a", "rwkv_r_k", "rwkv_gn_g", "rwkv_gn_b", "w_branch", "w_out"]
WSHAPES = {
    "w_ada": [DEPTH, D, 9 * D], "b_ada": [DEPTH, 9 * D], "ffn1_wi": [DEPTH, D, 2 * DFF], "ffn1_wo": [DEPTH, DFF, D],
    "ffn2_wi": [DEPTH, D, 2 * DFF], "ffn2_wo": [DEPTH, DFF, D], "ln_g": [DEPTH, 3, D], "ln_b": [DEPTH, 3, D],
    "w_in": [DEPTH, D, DIN], "att_q_gain": [DEPTH, 64], "att_k_gain": [DEPTH, 64], "rwkv_mu": [DEPTH, 2, DRW],
    "rwkv_w0": [DEPTH, 2, 512], "rwkv_w2": [DEPTH, 2, 64, 512], "rwkv_a0": [DEPTH, 2, 512],
    "rwkv_a2": [DEPTH, 2, 64, 512], "rwkv_g2": [DEPTH, 128, 512], "rwkv_k_k": [DEPTH, 512], "rwkv_k_a": [DEPTH, 512],
    "rwkv_r_k": [DEPTH, 512], "rwkv_gn_g": [DEPTH, 512], "rwkv_gn_b": [DEPTH, 512],
    "w_branch": [DEPTH, 3, 512, D], "w_out": [DEPTH, D, D],
}


def na_tiles(i):
    if 2 <= i <= 13:
        return [(i - 2 + p, p) for p in range(5)]
    if i == 0:
        return [(j, 5 + j) for j in range(4)]
    if i == 1:
        return [(j, 9 + j) for j in range(4)]
    if i == 14:
        return [(12 + j, 13 + j) for j in range(4)]
    return [(12 + j, 17 + j) for j in range(4)]


NPAT = 21


class Ctx:
    pass


def build(dbg=False, stop=None, nlayers=DEPTH):
    nc = bass.Bass("TRN2", target_bir_lowering=False)
    P = Prog(nc)
    S = Ctx()
    S.P = P
    S.dbg = dbg
    import os
    S.nsteps = int(os.environ.get('RWKV_STEPS', NT))
    okind = "ExternalOutput" if dbg else "Internal"
    S.x_in = P.dram("x", [NB, TL, D], F32, kind="ExternalInput")
    S.ctx_in = P.dram("ctx", [NB, TC, D], F32, kind="ExternalInput")
    S.c3_in = P.dram("c3", [3, D], F32, kind="ExternalInput")
    S.W = {n: P.dram(n, WSHAPES[n], F32, kind="ExternalInput") for n in WNAMES}
    S.nab = P.dram("nab", [DEPTH, NPAT, 128, 8, 128], F32, kind="ExternalInput")
    S.cst = P.dram("cst", [128, 6, 128], F32, kind="ExternalInput")
    S.rope = P.dram("rope", [TL, 64], F32, kind="ExternalInput")
    S.out = P.dram("out", [NB, TL, D], F32, kind="ExternalOutput")
    S.XA = [P.dram("XA%d" % b, [TT, D], F32, kind=okind) for b in range(NB)]
    S.XB = [P.dram("XB%d" % b, [TT, D], F32, kind=okind) for b in range(NB)]
    S.mod = P.dram("mod", [3, 9 * D], F32, kind=okind)
    S.QA = [P.dram("QA%d" % b, [TT, 512], BF16) for b in range(NB)]
    S.KA = [P.dram("KA%d" % b, [TT, 128], BF16) for b in range(NB)]
    S.VA = [P.dram("VA%d" % b, [TT, 128], BF16) for b in range(NB)]
    S.QN = [P.dram("QN%d" % b, [TT, 512], BF16) for b in range(NB)]
    S.KN = [P.dram("KN%d" % b, [TT, 512], BF16) for b in range(NB)]
    S.VN = [P.dram("VN%d" % b, [TT, 512], BF16) for b in range(NB)]
    S.GT = [P.dram("GT%d" % b, [TT, 3 * D], BF16) for b in range(NB)]
    S.YRI = [P.dram("YRI%d" % b, [TT, DRW], F32, kind=okind) for b in range(NB)]
    S.YA = [P.dram("YA%d" % b, [TT, 512], F32, kind=okind) for b in range(NB)]
    S.YR = [P.dram("YR%d" % b, [TT, 512], F32, kind=okind) for b in range(NB)]
    S.YN = [P.dram("YN%d" % b, [TT, 512], F32, kind=okind) for b in range(NB)]
    S.RS = {n: P.dram("RS_" + n, [TT, 512], F32, kind=okind) for n in
            ["R", "K", "V", "G", "A", "LW0", "LW1", "KD0", "KD1", "B0", "B1", "O0", "O1"]}
    S.cf = P.sb("cst_f", [128, 6, 128], F32)
    P.dma("sp", S.cf, S.cst)
    S.identb = P.sb("identb", [128, 128], BF16)
    P.copy("dve", S.identb, S.cf[:, 0, :])
    S.identf = S.cf[:, 0, :]
    S.banks = [P.ps("bank%d" % i, [128, 512], F32) for i in range(8)]
    S.bank_i = 0

    def bank():
        b = S.banks[S.bank_i % 8]
        S.bank_i += 1
        return b
    S.bank = bank
    S.eps = P.sb("eps", [128, 4], F32)
    for i, v in enumerate([LN_EPS, RMS_EPS, GN_EPS, 1e-12]):
        P.memset("dve", S.eps[:, i:i + 1], v)

    xsrc = [None] * NB
    for l in range(nlayers):
        last = l == DEPTH - 1
        mod_phase(S, l)
        if stop == ("mod", l):
            break
        src = (lambda b, i: (S.x_in[b, i * 128:(i + 1) * 128, :] if i < 16 else S.ctx_in[b, (i - 16) * 128:(i - 15) * 128, :])) \
            if l == 0 else (lambda b, i: S.XB[b][i * 128:(i + 1) * 128, :])
        dst = lambda b, i: S.XA[b][i * 128:(i + 1) * 128, :]
        ffn_phase(S, l, 0, src, dst, list(range(NT)))
        if stop == ("ffn1", l):
            break
        mixpre_phase(S, l)
        if stop == ("mixpre", l):
            break
        if stop == ("att", l):
            for b in range(NB):
                att_phase(S, l, b, need_ctx=not last)
                na_phase(S, l, b, need_ctx=not last)
            break
        if stop == ("rwkvscan", l):
            for _ in rwkv_prep(S, l, 0):
                pass
            rwkv_scan(S, l, 0, nsteps=S.nsteps)
            break
        for b in range(NB):
            att_phase(S, l, b, need_ctx=not last,
                      side=lambda b=b: rwkv_prep(S, l, b, own_phase=False, dbuf=False))
            rwkv_scan(S, l, b)
            na_phase(S, l, b, need_ctx=not last,
                     side=lambda b=b: rwkv_readout(S, l, b, not last, own_phase=False))
        if stop == ("rwkv", l):
            break
        merge_phase(S, l, list(range(NT)) if not last else list(range(16)))
        if stop == ("merge", l):
            break
        src = lambda b, i: S.XB[b][i * 128:(i + 1) * 128, :]
        if last:
            dst = lambda b, i: S.out[b, i * 128:(i + 1) * 128, :]
            ffn_phase(S, l, 1, src, dst, list(range(16)))
        else:
            dst = lambda b, i: S.XA[b][i * 128:(i + 1) * 128, :]
            ffn_phase(S, l, 1, src, dst, list(range(NT)))
            S.XA, S.XB = S.XB, S.XA
    P.finish()
    S.nc = nc
    return S


def bc3(t, shape):
    return T(t.ap.unsqueeze(2).to_broadcast(shape), t.tok)


def transpose_bf(S, src, dst, nch, eng_cycle=("act", "dve")):
    P = S.P
    gi = 0
    for g0 in range(0, nch, 8):
        n = min(8, nch - g0)
        pb = S.bank().bitcast(BF16)
        for k in range(n):
            P.tr(pb[:, k * 128:(k + 1) * 128], src[:, (g0 + k) * 128:(g0 + k + 1) * 128], S.identb)
        P.copy(eng_cycle[gi % len(eng_cycle)], dst[:, g0 * 128:(g0 + n) * 128], pb[:, 0:n * 128])
        gi += 1


def layernorm(S, z, g_bc, b_bc, out, wk):
    P = S.P
    stats, mv, rstd, xn = wk
    for j in range(2):
        P.op("dve", lambda e: e.bn_stats(stats[:, j, :].ap, z[:, j * 512:(j + 1) * 512].ap), [stats], [z])
    P.op("dve", lambda e: e.bn_aggr(mv.ap, stats.re("p a b -> p (a b)").ap), [mv], [stats])
    P.act(rstd, mv[:, 1:2], AF.Sqrt, bias=S.eps[:, 0:1], scale=1.0)
    P.recip(rstd, rstd)
    P.ts("dve", xn, z, mv[:, 0:1], rstd, op0=ALU.subtract, op1=ALU.mult)
    P.tt("pool", xn, xn, g_bc, ALU.mult)
    P.tt("dve", out, xn, b_bc, ALU.add)


def ln_work(P):
    return (P.sb("ln_stats", [128, 2, 6], F32), P.sb("ln_mv", [128, 2], F32),
            P.sb("ln_rstd", [128, 1], F32), P.sb("ln_xn", [128, D], F32))


def row_tiles(row, tiles):
    if row < 2:
        return [(row, i) for i in tiles if i < 16]
    return [(b, i) for b in range(NB) for i in tiles if i >= 16]


def load_mod(S, row, slots, dsts):
    for s, d in zip(slots, dsts):
        S.P.dma("sp", d, S.mod[row:row + 1, s * D:(s + 1) * D].bc([128, D]))


def mod_phase(S, l):
    P = S.P
    P.phase_begin()
    c3 = P.sb("c3", [3, D], F32)
    P.dma("sp", c3, S.c3_in)
    sc = P.sb("sc", [3, D], F32)
    P.act(sc, c3, AF.Silu)
    scT = P.sb("scT", [128, 8, 3], F32)
    pb = S.bank()
    for k in range(8):
        P.tr(pb[:, k * 4:k * 4 + 3], sc[:, k * 128:(k + 1) * 128], S.identf[0:3, 0:3])
    P.copy("dve", scT, pb[:, 0:32].re("p (k c) -> p k c", c=4)[:, :, 0:3])
    ba = P.sb("ba", [3, 9 * D], F32)
    P.dma("sp", ba, S.W["b_ada"][l:l + 1, :].bc([3, 9 * D]))
    m_sb = P.sb("m_sb", [3, 9 * D], F32)
    wch = [P.sb("wch%d" % i, [128, 8, 512], F32) for i in range(2)]
    wsrc = S.W["w_ada"][l].re("(k p) n -> p k n", p=128)
    for j in range(18):
        w = wch[j % 2]
        P.dma("sp", w, wsrc[:, :, j * 512:(j + 1) * 512])
        pb = S.bank()
        for k in range(8):
            P.mm(pb[0:3, :], scT[:, k, :], w[:, k, :], start=(k == 0), stop=(k == 7))
        P.tt("dve", m_sb[:, j * 512:(j + 1) * 512], pb[0:3, :], ba[:, j * 512:(j + 1) * 512], ALU.add)
    P.dma("sp", S.mod, m_sb)
    P.phase_end()


def ffn_phase(S, l, f, src, dst, tiles):
    P = S.P
    P.phase_begin()
    wi_d = S.W["ffn%d_wi" % (f + 1)]
    wo_d = S.W["ffn%d_wo" % (f + 1)]
    wi = P.sb("wi", [128, 8, 2 * DFF], BF16)
    wo = P.sb("wo", [128, 22, D], BF16)
    for k in range(8):
        P.dma("pool", wi[:, k, :], wi_d[l, k * 128:(k + 1) * 128, :])
    for k in range(22):
        P.dma("pool", wo[:, k, :], wo_d[l, k * 128:(k + 1) * 128, :])
    lni = 0 if f == 0 else 2
    lng = P.sb("lng", [128, D], F32)
    lnb = P.sb("lnb", [128, D], F32)
    P.dma("sp", lng, S.W["ln_g"][l, lni:lni + 1, :].bc([128, D]))
    P.dma("sp", lnb, S.W["ln_b"][l, lni:lni + 1, :].bc([128, D]))
    s0 = 0 if f == 0 else 6
    sh = P.sb("sh", [128, D], F32)
    sc = P.sb("sc", [128, D], F32)
    gt = P.sb("gt", [128, D], F32)
    xb = [P.sb("xb%d" % i, [128, D], F32) for i in range(2)]
    u32 = P.sb("u32", [128, D], F32)
    ub = P.sb("ub", [128, D], BF16)
    uT = P.sb("uT", [128, D], BF16)
    sg = [P.sb("sg%d" % i, [128, 512], F32) for i in range(2)]
    act = P.sb("act", [128, DFF], BF16)
    actT = P.sb("actT", [128, DFF], BF16)
    tmp = P.sb("tmp", [128, D], F32)
    z = P.sb("z", [128, D], F32)
    xo = P.sb("xo", [128, D], F32)
    wk = ln_work(P)
    for row in range(3):
        rt = row_tiles(row, tiles)
        if not rt:
            continue
        load_mod(S, row, [s0, s0 + 1, s0 + 2], [sh, sc, gt])
        P.ts("pool", sc, sc, 1.0, op0=ALU.add)
        P.ts("pool", gt, gt, 0.5, op0=ALU.mult)
        P.dma("sp", xb[0], src(*rt[0]))

        def front(n):
            P.tt("pool", u32, xb[n % 2], sc, ALU.mult)
            P.tt("dve", ub, u32, sh, ALU.add)
            transpose_bf(S, ub, uT, 8)

        front(0)
        for n, (b, i) in enumerate(rt):
            xt = xb[n % 2]
            if n + 1 < len(rt):
                P.dma("sp", xb[(n + 1) % 2], src(*rt[n + 1]))
            for ci, c0 in enumerate(range(0, DFF, 512)):
                w = min(512, DFF - c0)
                pg = S.bank()
                pu = S.bank()
                for k in range(8):
                    P.mm(pg[:, 0:w], uT[:, k * 128:(k + 1) * 128], wi[:, k, c0:c0 + w], start=(k == 0), stop=(k == 7))
                for k in range(8):
                    P.mm(pu[:, 0:w], uT[:, k * 128:(k + 1) * 128], wi[:, k, DFF + c0:DFF + c0 + w], start=(k == 0), stop=(k == 7))
                s_ = sg[ci % 2]
                P.act(s_[:, 0:w], pg[:, 0:w], AF.Silu)
                P.tt("dve", act[:, c0:c0 + w], s_[:, 0:w], pu[:, 0:w], ALU.mult)
            transpose_bf(S, act, actT, 22)
            pos = []
            for n2 in range(2):
                po = S.bank()
                for hc in range(22):
                    P.mm(po, actT[:, hc * 128:(hc + 1) * 128], wo[:, hc, n2 * 512:(n2 + 1) * 512], start=(hc == 0), stop=(hc == 21))
                pos.append(po)
            if n + 1 < len(rt):
                front(n + 1)
            for n2 in range(2):
                P.tt("dve", tmp[:, n2 * 512:(n2 + 1) * 512], pos[n2], gt[:, n2 * 512:(n2 + 1) * 512], ALU.mult)
            P.stt(z, xt, ALPHA, tmp, ALU.mult, ALU.add)
            layernorm(S, z, lng, lnb, xo, wk)
            P.dma("sp", dst(b, i), xo)
    P.phase_end()


def _consts():
    p = np.arange(128)[:, None]
    f = np.arange(128)[None, :]
    cst = np.stack([(p == f), (p > f), (p < f), (p >= f), (p <= f), np.ones((128, 128), bool)], 1).astype(np.float32)
    t = np.arange(TL)
    row = (t // 64).astype(np.float32)
    col = (t % 64).astype(np.float32)
    inv = (np.float32(10000.0) ** (-np.arange(0, 32, 2, dtype=np.float32) / np.float32(32))).astype(np.float32)
    ang = np.stack([row[:, None] * inv, col[:, None] * inv], 1).astype(np.float32)
    rope = np.concatenate([np.cos(ang).reshape(TL, 32), np.sin(ang).reshape(TL, 32)], 1).astype(np.float32)
    return np.ascontiguousarray(cst), np.ascontiguousarray(rope)


def _na_bias(rpb):
    L = rpb.shape[0]
    out = np.full((L, NPAT, 128, 8, 128), MASKV, np.float32)
    reps = {}
    for i in [2, 0, 1, 14, 15]:
        for (j, pat) in na_tiles(i):
            if pat not in reps:
                reps[pat] = (i, j)
    kk, kc = np.divmod(np.arange(128), 64)
    qq, qc = np.divmod(np.arange(128), 64)
    for pat, (i, j) in reps.items():
        r = 2 * i + qq
        kr = 2 * j + kk
        r0 = np.clip(r - 4, 0, 32 - 8)
        c0 = np.clip(qc - 8, 0, 64 - 16)
        ok = (kr[:, None] >= r0[None, :]) & (kr[:, None] < r0[None, :] + 8) & \
             (kc[:, None] >= c0[None, :]) & (kc[:, None] < c0[None, :] + 16)
        dr = np.clip(kr[:, None] - r[None, :] + 7, 0, 14)
        dc = np.clip(kc[:, None] - qc[None, :] + 15, 0, 30)
        g = rpb[:, :, dr, dc]
        g = np.where(ok[None, None], g, np.float32(MASKV))
        out[:, pat] = np.transpose(g, (0, 2, 1, 3))
    return out


def make_in_maps(inp, cores=range(8)):
    cst, rope = _consts()
    shared = {n: np.ascontiguousarray(np.asarray(inp[n], np.float32).reshape(WSHAPES[n])) for n in WNAMES}
    shared["nab"] = _na_bias(np.asarray(inp["na_rpb"], np.float32))
    shared["cst"] = cst
    shared["rope"] = rope
    maps = []
    for c in cores:
        m = dict(shared)
        m["x"] = np.ascontiguousarray(inp["x"][c * NB:(c + 1) * NB])
        m["ctx"] = np.ascontiguousarray(inp["ctx"][c * NB:(c + 1) * NB])
        m["c3"] = np.ascontiguousarray(np.concatenate([inp["c"][c * NB:(c + 1) * NB], inp["c_ctx"][None, :]], 0))
        maps.append(m)
    return maps


_CACHE = {}


def kernel(**inputs):
    if "S" not in _CACHE:
        _CACHE["S"] = build()
    S = _CACHE["S"]
    maps = make_in_maps(inputs)
    res = run_bass_kernel_spmd(S.nc, maps, core_ids=list(range(8)))
    return np.concatenate([np.asarray(r["out"], np.float32) for r in res.results], 0)


def mixpre_phase(S, l):
    P = S.P
    P.phase_begin()
    win = P.sb("win", [128, 8, DIN], BF16)
    for k in range(8):
        P.dma("pool", win[:, k, :], S.W["w_in"][l, k * 128:(k + 1) * 128, :])
    gain = P.sb("gain", [128, 10, 64], F32)
    for h in range(10):
        gsrc = S.W["att_q_gain"] if h < 8 else S.W["att_k_gain"]
        P.dma("sp", gain[:, h, :], gsrc[l:l + 1, :].bc([128, 64]))
    sh = P.sb("sh", [128, D], F32)
    sc = P.sb("sc", [128, D], F32)
    xb = [P.sb("xb%d" % i, [128, D], F32) for i in range(2)]
    u32 = P.sb("u32", [128, D], F32)
    ub = P.sb("ub", [128, D], BF16)
    uT = P.sb("uT", [128, D], BF16)
    NPX = C_NQ
    pxs = [P.sb("px%d" % i, [128, NPX], F32) for i in range(2)]
    gtss = [P.sb("gts%d" % i, [128, 3 * D], BF16) for i in range(2)]
    nbs = [P.sb("nb%d" % i, [128, 1536], BF16) for i in range(2)]
    sq = P.sb("sq", [128, 640], F32)
    ss = P.sb("ss", [128, 10], F32)
    rs = P.sb("rs", [128, 10], F32)
    qn = P.sb("qn", [128, 640], F32)
    rw = P.sb("rw", [128, 4, 320], F32)
    qrs = [P.sb("qr%d" % i, [128, 640], BF16) for i in range(2)]
    vas = [P.sb("va%d" % i, [128, 128], BF16) for i in range(2)]
    cs = P.sb("cs", [128, 64], F32)
    chunks = [(c0, 512) for c0 in range(0, 2560, 512)] + [(2560, 128)] + [(c0, 512) for c0 in range(C_NQ, DIN, 512)]
    cnt = 0
    for row in range(3):
        rt = row_tiles(row, list(range(NT)))
        load_mod(S, row, [3, 4], [sh, sc])
        P.ts("pool", sc, sc, 1.0, op0=ALU.add)
        P.dma("sp", xb[0], S.XA[rt[0][0]][rt[0][1] * 128:(rt[0][1] + 1) * 128, :])

        def front(n):
            P.tt("pool", u32, xb[n % 2], sc, ALU.mult)
            P.tt("dve", ub, u32, sh, ALU.add)
            transpose_bf(S, ub, uT, 8)

        front(0)
        for n, (b, i) in enumerate(rt):
            rows = slice(i * 128, (i + 1) * 128)
            if n + 1 < len(rt):
                b2, i2 = rt[n + 1]
                P.dma("sp", xb[(n + 1) % 2], S.XA[b2][i2 * 128:(i2 + 1) * 128, :])
            px = pxs[cnt % 2]
            gts = gtss[cnt % 2]
            qr = qrs[cnt % 2]
            va = vas[cnt % 2]
            nb = nbs[cnt % 2]
            cnt += 1
            for ci, (c0, w) in enumerate(chunks):
                pb = S.bank()
                for k in range(8):
                    P.mm(pb[:, 0:w], uT[:, k * 128:(k + 1) * 128], win[:, k, c0:c0 + w], start=(k == 0), stop=(k == 7))
                if c0 < C_NQ:
                    P.copy("act" if ci % 2 == 0 else "dve", px[:, c0:c0 + w], pb[:, 0:w])
                elif c0 < C_G:
                    P.copy("dve" if ci % 2 == 0 else "act", nb[:, c0 - C_NQ:c0 - C_NQ + w], pb[:, 0:w])
                else:
                    P.act(gts[:, c0 - C_G:c0 - C_G + w], pb[:, 0:w], AF.Sigmoid)
            if n + 1 < len(rt):
                front(n + 1)
            P.tt("pool", sq, px[:, 0:640], px[:, 0:640], ALU.mult)
            P.reduce(ss, sq.re("p (h d) -> p h d", d=64), ALU.add)
            P.act(rs, ss, AF.Sqrt, bias=S.eps[:, 1:2], scale=1.0 / 64)
            P.recip(rs, rs)
            qn3 = qn.re("p (h d) -> p h d", d=64)
            P.tt("dve", qn3, px[:, 0:640].re("p (h d) -> p h d", d=64), bc3(rs, [128, 10, 64]), ALU.mult)
            P.tt("pool", qn3, qn3, gain, ALU.mult)
            if i < 16:
                P.dma("sp", cs, S.rope[rows, :])
                q5 = qn.re("p (h a j d) -> p h a j d", h=10, a=2, j=2, d=16)
                o5 = qr.re("p (h a j d) -> p h a j d", h=10, a=2, j=2, d=16)
                x1, x2 = q5[:, :, :, 0, :], q5[:, :, :, 1, :]
                cosb = T(cs[:, 0:32].re("p (a d) -> p a d", a=2).ap.unsqueeze(1).to_broadcast([128, 10, 2, 16]), cs.tok)
                sinb = T(cs[:, 32:64].re("p (a d) -> p a d", a=2).ap.unsqueeze(1).to_broadcast([128, 10, 2, 16]), cs.tok)
                tw = [rw[:, t, :].re("p (h a d) -> p h a d", h=10, a=2) for t in range(4)]
                P.tt("dve", tw[0], x1, cosb, ALU.mult)
                P.tt("pool", tw[1], x2, sinb, ALU.mult)
                P.tt("dve", tw[2], x2, cosb, ALU.mult)
                P.tt("pool", tw[3], x1, sinb, ALU.mult)
                P.tt("dve", o5[:, :, :, 0, :], tw[0], tw[1], ALU.subtract)
                P.tt("pool", o5[:, :, :, 1, :], tw[2], tw[3], ALU.add)
            else:
                P.copy("dve", qr, qn)
            P.dma("sp", S.QA[b][rows, :], qr[:, 0:512])
            P.dma("sp", S.KA[b][rows, :], qr[:, 512:640])
            P.copy("pool", va, px[:, C_AV:C_AV + 128])
            P.dma("sp", S.VA[b][rows, :], va)
            P.dma("sp", S.YRI[b][rows, :], px[:, C_RW:C_RW + DRW])
            P.dma("sp", S.QN[b][rows, :], nb[:, 0:512])
            P.dma("sp", S.KN[b][rows, :], nb[:, 512:1024])
            P.dma("sp", S.VN[b][rows, :], nb[:, 1024:1536])
            P.dma("sp", S.GT[b][rows, :], gts)
    P.phase_end()


def load_kT(S, ksrc, KT, nh):
    P = S.P
    ktok = P.sb("ktok", [128, NT, nh * 64], BF16)
    P.dma("sp", ktok, ksrc.re("(i p) c -> p i c", p=128))
    per = 8 // nh
    for i0 in range(0, NT, per):
        n = min(per, NT - i0)
        pb = S.bank().bitcast(BF16)
        for ii in range(n):
            for h in range(nh):
                slot = h * per + ii
                P.tr(pb[0:64, slot * 128:(slot + 1) * 128], ktok[:, i0 + ii, h * 64:(h + 1) * 64], S.identb)
        src = pb[0:64, :].re("p (h i t) -> p h i t", h=nh, i=per)[:, :, 0:n, :]
        dst = KT[:, :, i0 * 128:(i0 + n) * 128].re("p h (i t) -> p h i t", t=128)
        P.copy("act", dst, src)


def load_v1(S, vsrc, V1, nh):
    P = S.P
    P.memset("pool", V1, 1.0)
    for i in range(NT):
        P.dma("sp", V1[:, i, :, 0:64], vsrc[i * 128:(i + 1) * 128, :].re("p (g d) -> p g d", g=nh))


def pv_and_store(S, PT, V1, keyl, kvh_of, ya, dstrows, pt_of=None):
    P = S.P
    nk = len(keyl)
    for hg in range(2):
        po = S.bank()
        for hh in range(4):
            h = hg * 4 + hh
            for idx, j in enumerate(keyl):
                pt = pt_of(idx, h) if pt_of is not None else PT[:, idx, h * 128:(h + 1) * 128]
                P.mm(po[:, hh * 65:(hh + 1) * 65], pt, V1[:, j, kvh_of(h), :],
                     start=(idx == 0), stop=(idx == nk - 1))
        po3 = po[:, 0:260].re("p (h e) -> p h e", e=65)
        rec = S.rec
        P.recip(rec, po3[:, :, 64])
        P.tt("dve", ya[:, hg * 256:(hg + 1) * 256].re("p (h d) -> p h d", d=64), po3[:, :, 0:64],
             bc3(rec, [128, 4, 64]), ALU.mult)
    P.dma("sp", dstrows, ya)


def load_qT(S, qsrc_rows, qtok, QT):
    P = S.P
    P.dma("sp", qtok, qsrc_rows)
    pb = S.bank().bitcast(BF16)
    for h in range(8):
        P.tr(pb[0:64, h * 128:(h + 1) * 128], qtok[:, h * 64:(h + 1) * 64], S.identb)
    P.copy("dve", QT, pb[0:64, :])


def att_phase(S, l, b, need_ctx, side=None):
    P = S.P
    P.phase_begin()
    KT = P.sb("KT", [64, 2, TT], BF16)
    load_kT(S, S.KA[b], KT, 2)
    V1 = P.sb("V1", [128, NT, 2, 65], BF16)
    load_v1(S, S.VA[b], V1, 2)
    S.rec = P.sb("rec", [128, 4], F32)
    qtoks = [P.sb("qtok%d" % i, [128, 512], BF16) for i in range(2)]
    QTs = [P.sb("QT%d" % i, [64, 1024], BF16) for i in range(2)]
    PTg = [P.sb("PTg%d" % g, [128, NT, 512], BF16) for g in range(2)]
    yas = [P.sb("ya%d" % i, [128, 512], F32) for i in range(2)]
    qtiles = list(range(16)) + ([16, 17] if need_ctx else [])
    sidegen = side() if side is not None else None
    for n, i in enumerate(qtiles):
        keyl = list(range(NT)) if i < 16 else [16, 17]
        QT, ya = QTs[n % 2], yas[n % 2]
        rows = slice(i * 128, (i + 1) * 128)
        load_qT(S, S.QA[b][rows, :], qtoks[n % 2], QT)
        for idx, j in enumerate(keyl):
            for g in range(2):
                ps = S.bank()
                P.mm(ps, KT[:, g, j * 128:(j + 1) * 128], QT[:, g * 512:(g + 1) * 512])
                P.act(PTg[g][:, idx, :], ps, AF.Exp, scale=0.125)
        pv_and_store(S, None, V1, keyl, lambda h: h // 4, ya, S.YA[b][rows, :],
                     pt_of=lambda idx, h: PTg[h // 4][:, idx, (h % 4) * 128:(h % 4 + 1) * 128])
        if sidegen is not None:
            next(sidegen, None)
    if sidegen is not None:
        for _ in sidegen:
            pass
    P.phase_end()


def na_phase(S, l, b, need_ctx, side=None):
    P = S.P
    P.phase_begin()
    KT = P.sb("KTn", [64, 8, TT], BF16)
    load_kT(S, S.KN[b], KT, 8)
    V1 = P.sb("V1n", [128, NT, 8, 65], BF16)
    load_v1(S, S.VN[b], V1, 8)
    S.rec = P.sb("rec", [128, 4], F32)
    qtoks = [P.sb("qtok%d" % i, [128, 512], BF16) for i in range(2)]
    QTs = [P.sb("QT%d" % i, [64, 1024], BF16) for i in range(2)]
    PTs = [P.sb("PT%d" % i, [128, 7, 1024], BF16) for i in range(2)]
    yas = [P.sb("ya%d" % i, [128, 512], F32) for i in range(2)]
    nbs = [P.sb("nab%d" % i, [128, 5, 8, 128], F32) for i in range(2)]
    tmps = [P.sb("natmp%d" % i, [128, 512], F32) for i in range(2)]
    qtiles = list(range(16)) + ([16, 17] if need_ctx else [])
    sidegen = side() if side is not None else None
    tc = 0
    for n, i in enumerate(qtiles):
        kl = (na_tiles(i) if i < 16 else []) + [(16, None), (17, None)]
        QT, PT, ya, nb = QTs[n % 2], PTs[n % 2], yas[n % 2], nbs[n % 2]
        rows = slice(i * 128, (i + 1) * 128)
        for idx, (j, pat) in enumerate(kl):
            if pat is not None:
                P.dma("sp", nb[:, idx], S.nab[l, pat])
        load_qT(S, S.QN[b][rows, :], qtoks[n % 2], QT)
        for idx, (j, pat) in enumerate(kl):
            for hg in range(2):
                ps = S.bank()
                for hh in range(4):
                    h = hg * 4 + hh
                    P.mm(ps[:, hh * 128:(hh + 1) * 128], KT[:, h, j * 128:(j + 1) * 128], QT[:, h * 128:(h + 1) * 128])
                if pat is not None:
                    tmp = tmps[tc % 2]
                    tc += 1
                    P.stt(tmp, ps, 0.125, nb[:, idx, hg * 4:(hg + 1) * 4, :].re("p h q -> p (h q)"), ALU.mult, ALU.add)
                    P.act(PT[:, idx, hg * 512:(hg + 1) * 512], tmp, AF.Exp)
                else:
                    P.act(PT[:, idx, hg * 512:(hg + 1) * 512], ps, AF.Exp, scale=0.125)
        pv_and_store(S, PT, V1, [j for j, _ in kl], lambda h: h, ya, S.YN[b][rows, :])
        if sidegen is not None:
            next(sidegen, None)
    if sidegen is not None:
        for _ in sidegen:
            pass
    P.phase_end()


def merge_phase(S, l, tiles):
    P = S.P
    P.phase_begin()
    wb = P.sb("wbr", [128, 12, D], BF16)
    for zz in range(3):
        for k in range(4):
            P.dma("pool", wb[:, zz * 4 + k, :], S.W["w_branch"][l, zz, k * 128:(k + 1) * 128, :])
    wout = P.sb("wout", [128, 8, D], BF16)
    for k in range(8):
        P.dma("pool", wout[:, k, :], S.W["w_out"][l, k * 128:(k + 1) * 128, :])
    lng = P.sb("lng", [128, D], F32)
    lnb = P.sb("lnb", [128, D], F32)
    P.dma("sp", lng, S.W["ln_g"][l, 1:2, :].bc([128, D]))
    P.dma("sp", lnb, S.W["ln_b"][l, 1:2, :].bc([128, D]))
    gt = P.sb("gt", [128, D], F32)
    xb = [P.sb("xb%d" % i, [128, D], F32) for i in range(2)]
    y32 = [P.sb("y32_%d" % i, [128, 1536], F32) for i in range(2)]
    gts = [P.sb("gtl%d" % i, [128, 3 * D], BF16) for i in range(2)]
    yb = P.sb("yb", [128, 1536], BF16)
    yT = P.sb("yT", [128, 1536], BF16)
    mg = P.sb("mg", [128, D], F32)
    tmp = P.sb("tmp", [128, 512], F32)
    mgb = P.sb("mgb", [128, D], BF16)
    mT = P.sb("mT", [128, D], BF16)
    tmp2 = P.sb("tmp2", [128, D], F32)
    z = P.sb("z", [128, D], F32)
    xo = P.sb("xo", [128, D], F32)
    wk = ln_work(P)

    def loads(n, b, i):
        rows = slice(i * 128, (i + 1) * 128)
        P.dma("sp", xb[n % 2], S.XA[b][rows, :])
        P.dma("sp", y32[n % 2][:, 0:512], S.YA[b][rows, :])
        P.dma("sp", y32[n % 2][:, 512:1024], S.YR[b][rows, :])
        P.dma("sp", y32[n % 2][:, 1024:1536], S.YN[b][rows, :])
        P.dma("sp", gts[n % 2], S.GT[b][rows, :])

    for row in range(3):
        rt = row_tiles(row, tiles)
        if not rt:
            continue
        load_mod(S, row, [5], [gt])
        loads(0, *rt[0])
        for n, (b, i) in enumerate(rt):
            rows = slice(i * 128, (i + 1) * 128)
            if n + 1 < len(rt):
                loads(n + 1, *rt[n + 1])
            xt, yy, gg = xb[n % 2], y32[n % 2], gts[n % 2]
            P.copy("pool", yb, yy)
            transpose_bf(S, yb, yT, 12)
            for n2 in range(2):
                cs_ = slice(n2 * 512, (n2 + 1) * 512)
                for zz in range(3):
                    ps = S.bank()
                    for k in range(4):
                        P.mm(ps, yT[:, (zz * 4 + k) * 128:(zz * 4 + k + 1) * 128], wb[:, zz * 4 + k, cs_], start=(k == 0), stop=(k == 3))
                    gsl = gg[:, zz * D + n2 * 512: zz * D + (n2 + 1) * 512]
                    if zz == 0:
                        P.tt("dve", mg[:, cs_], ps, gsl, ALU.mult)
                    else:
                        P.tt("dve", tmp, ps, gsl, ALU.mult)
                        P.tt("pool", mg[:, cs_], mg[:, cs_], tmp, ALU.add)
            P.copy("act", mgb, mg)
            transpose_bf(S, mgb, mT, 8)
            for n2 in range(2):
                cs_ = slice(n2 * 512, (n2 + 1) * 512)
                po = S.bank()
                for k in range(8):
                    P.mm(po, mT[:, k * 128:(k + 1) * 128], wout[:, k, cs_], start=(k == 0), stop=(k == 7))
                P.tt("dve", tmp2[:, cs_], po, gt[:, cs_], ALU.mult)
            P.stt(z, xt, ALPHA, tmp2, ALU.mult, ALU.add)
            layernorm(S, z, lng, lnb, xo, wk)
            P.dma("sp", S.XB[b][rows, :], xo)
    P.phase_end()


def rwkv_phase(S, l, b, need_ctx):
    for _ in rwkv_prep(S, l, b):
        pass
    rwkv_scan(S, l, b)
    for _ in rwkv_readout(S, l, b, need_ctx):
        pass


def rwkv_prep(S, l, b, own_phase=True, dbuf=True):
    P = S.P
    W = S.W
    RS = S.RS
    if own_phase:
        P.phase_begin()
    mu0 = P.sb("mu0", [128, DRW], F32)
    mu1 = P.sb("mu1", [128, DRW], F32)
    P.dma("sp", mu0, W["rwkv_mu"][l, 0:1, :].bc([128, DRW]))
    P.dma("sp", mu1, W["rwkv_mu"][l, 1:2, :].bc([128, DRW]))
    w2 = P.sb("w2", [128, 512], F32)
    a2 = P.sb("a2", [128, 512], F32)
    g2 = P.sb("g2", [128, 512], F32)
    P.dma("sp", w2, W["rwkv_w2"][l].re("z r c -> (z r) c"))
    P.dma("sp", a2, W["rwkv_a2"][l].re("z r c -> (z r) c"))
    P.dma("sp", g2, W["rwkv_g2"][l])
    w0b = P.sb("w0b", [128, 2, 512], F32)
    a0b = P.sb("a0b", [128, 2, 512], F32)
    for z in range(2):
        P.dma("sp", w0b[:, z, :], W["rwkv_w0"][l, z:z + 1, :].bc([128, 512]))
        P.dma("sp", a0b[:, z, :], W["rwkv_a0"][l, z:z + 1, :].bc([128, 512]))
    kkb = P.sb("kkb", [128, 512], F32)
    kab = P.sb("kab", [128, 512], F32)
    P.dma("sp", kkb, W["rwkv_k_k"][l:l + 1, :].bc([128, 512]))
    P.dma("sp", kab, W["rwkv_k_a"][l:l + 1, :].bc([128, 512]))
    def mkset(q):
        d = Ctx()
        d.yc = P.sb("yc%d" % q, [128, DRW], F32)
        d.yp = P.sb("yp%d" % q, [128, DRW], F32)
        d.yn = P.sb("yn%d" % q, [128, DRW], F32)
        d.lo = P.sb("lo%d" % q, [128, 3, 128], F32)
        d.loT = P.sb("loT%d" % q, [128, 3, 128], F32)
        d.t1 = P.sb("t1%d" % q, [128, 512], F32)
        d.lw = [P.sb("lw%d_%d" % (z, q), [128, 512], F32) for z in range(2)]
        d.ic = [P.sb("ic%d_%d" % (z, q), [128, 512], F32) for z in range(2)]
        d.kd = [P.sb("kd%d_%d" % (z, q), [128, 512], F32) for z in range(2)]
        d.bz = [P.sb("bz%d_%d" % (z, q), [128, 512], F32) for z in range(2)]
        for n in ["gg", "kx", "sq", "kk", "aa"]:
            setattr(d, n, P.sb("%s%d" % (n, q), [128, 512], F32))
        d.ss8 = P.sb("ss8%d" % q, [128, 8], F32)
        return d

    sets = [mkset(0), mkset(1)] if dbuf else [mkset(0)] * 2
    Y = S.YRI[b]

    def loads(i):
        d = sets[i % 2]
        r0 = i * 128
        rows = slice(r0, r0 + 128)
        P.dma("sp", d.yc, Y[rows, :])
        if i == 0 or i == 16:
            P.memset("pool", d.yp, 0.0)
            P.dma("sp", d.yp[1:128, :], Y[r0:r0 + 127, :])
        else:
            P.dma("sp", d.yp, Y[r0 - 1:r0 + 127, :])
        if i == 15 or i == 17:
            P.memset("pool", d.yn, 0.0)
            P.dma("sp", d.yn[0:127, :], Y[r0 + 1:r0 + 128, :])
        else:
            P.dma("sp", d.yn, Y[r0 + 1:r0 + 129, :])

    loads(0)
    for i in range(NT):
        d = sets[i % 2]
        yc, yp, yn, lo, loT, t1, lw, ic, kd, bz = d.yc, d.yp, d.yn, d.lo, d.loT, d.t1, d.lw, d.ic, d.kd, d.bz
        gg, kx, sq, kk, aa, ss8 = d.gg, d.kx, d.sq, d.kk, d.aa, d.ss8
        r0 = i * 128
        rows = slice(r0, r0 + 128)
        if dbuf and i + 1 < NT:
            loads(i + 1)
        if not dbuf and i > 0:
            loads(i)
        P.tt("dve", yp, yp, yc, ALU.subtract)
        P.tt("dve", yn, yn, yc, ALU.subtract)
        P.tt("dve", yp, yp, mu0, ALU.mult)
        P.tt("dve", yn, yn, mu1, ALU.mult)
        P.tt("dve", yc, yc, yp, ALU.add)
        P.tt("dve", yc, yc, yn, ALU.add)
        r_, k_, v_ = yc[:, 0:512], yc[:, 512:1024], yc[:, 1024:1536]
        P.act(lo[:, 0, :], yc[:, 1536:1664], AF.Tanh)
        P.copy("pool", lo[:, 1, :], yc[:, 1664:1792])
        P.act(lo[:, 2, :], yc[:, 1792:1920], AF.Sigmoid)
        pb = S.bank()
        for q in range(3):
            P.tr(pb[:, q * 128:(q + 1) * 128], lo[:, q, :], S.identf)
        P.copy("act", loT.re("p a t -> p (a t)"), pb[:, 0:384])
        for z in range(2):
            ps = S.bank()
            P.mm(ps, loT[z * 64:(z + 1) * 64, 0, :], w2[z * 64:(z + 1) * 64, :])
            P.tt("dve", t1, ps, w0b[:, z, :], ALU.add)
            P.act(t1, t1, AF.Sigmoid)
            P.ts("pool", lw[z], t1, -float(np.exp(-0.5)), op0=ALU.mult)
            P.dma("sp", RS["LW%d" % z][rows, :], lw[z])
            ps = S.bank()
            P.mm(ps, loT[z * 64:(z + 1) * 64, 1, :], a2[z * 64:(z + 1) * 64, :])
            P.tt("dve", ic[z], ps, a0b[:, z, :], ALU.add)
            P.act(ic[z], ic[z], AF.Sigmoid)
        ps = S.bank()
        P.mm(ps, loT[:, 2, :], g2)
        P.copy("act", gg, ps)
        P.dma("sp", RS["G"][rows, :], gg)
        P.tt("dve", kx, k_, kkb, ALU.mult)
        P.tt("pool", sq, kx, kx, ALU.mult)
        P.reduce(ss8, sq.re("p (h d) -> p h d", d=64), ALU.add)
        P.act(ss8, ss8, AF.Sqrt, bias=S.eps[:, 3:4], scale=1.0)
        P.recip(ss8, ss8)
        P.tt("dve", kk.re("p (h d) -> p h d", d=64), kx.re("p (h d) -> p h d", d=64), bc3(ss8, [128, 8, 64]), ALU.mult)
        P.ts("pool", aa, kk, -1.0, op0=ALU.mult)
        P.dma("sp", RS["A"][rows, :], aa)
        for z in range(2):
            P.stt(t1, ic[z], -1.0, kab, ALU.add, ALU.mult)
            P.stt(kd[z], t1, 1.0, k_, ALU.add, ALU.mult)
            P.dma("sp", RS["KD%d" % z][rows, :], kd[z])
            P.tt("pool", bz[z], kk, ic[z], ALU.mult)
            P.dma("sp", RS["B%d" % z][rows, :], bz[z])
        P.dma("sp", RS["R"][rows, :], r_)
        P.dma("sp", RS["K"][rows, :], k_)
        P.dma("sp", RS["V"][rows, :], v_)
        yield
    if own_phase:
        P.phase_end()


def rwkv_scan(S, l, b, nsteps=NT):
    P = S.P
    RS = S.RS
    P.phase_begin()
    ident, SL, SU, IL, IU, ones = [S.cf[:, q, :] for q in range(6)]
    order = [[16, 17] + list(range(16)), [17, 16] + list(range(15, -1, -1))]
    mN = [SL, SU]
    mTs = [SU, SL]
    mTi = [IU, IL]
    mC = [IU, IL]

    def mk(z):
        d = Ctx()
        for n in ["lw", "kd", "bb", "aa", "rr", "vv", "cum", "d1", "d2", "e1", "e2", "At", "Bt", "Kt", "Rt", "Bg", "Kg",
                  "Zs", "Us", "Os"]:
            setattr(d, n, P.sb("%s%d" % (n, z), [128, 512], F32))
        for n in ["AtT", "BtT", "KtT", "RtT"]:
            setattr(d, n, P.sb("%s%d" % (n, z), [64, 8, 128], F32))
        for n in ["Aak", "Arb", "Ark"]:
            setattr(d, n, [P.sb("%s%d_%d" % (n, z, hg), [128, 4, 128], F32) for hg in range(2)])
        for n in ["M", "Mt", "Pt"]:
            setattr(d, n, [[P.sb("%s%d_%d_%d" % (n, z, hg, q), [128, 4, 128], F32) for q in range(2)] for hg in range(2)])
        d.gL = P.sb("gL%d" % z, [64, 8], F32)
        d.H = P.sb("H%d" % z, [64, 8, 64], F32)
        P.memset("pool", d.H, 0.0)
        return d

    Dz = [mk(0), mk(1)]

    def mb(m):
        return T(m.ap.unsqueeze(1).to_broadcast([128, 4, 128]), m.tok)

    def b4(ps):
        return ps.re("p (h t) -> p h t", h=4)

    def opd(XT, h):
        return XT[:, h, :]

    def pre(z, c):
        d = Dz[z]
        rows = slice(c * 128, (c + 1) * 128)
        P.dma("sp", d.lw, RS["LW%d" % z][rows, :])
        P.dma("sp", d.kd, RS["KD%d" % z][rows, :])
        P.dma("sp", d.bb, RS["B%d" % z][rows, :])
        P.dma("sp", d.aa, RS["A"][rows, :])
        P.dma("sp", d.rr, RS["R"][rows, :])
        P.dma("sp", d.vv, RS["V"][rows, :])
        pc = S.bank()
        P.mm(pc, mC[z], d.lw)
        pl = S.bank()
        P.mm(pl, ones, d.lw)
        pg = S.bank()
        for h in range(8):
            P.mm(pg[0:64, 2 * h:2 * h + 2], d.lw[:, h * 64:(h + 1) * 64], ones[:, 0:2])
        P.act(d.gL, pg[0:64, 0:16].re("p (h two) -> p h two", two=2)[:, :, 0], AF.Exp)
        P.copy("act", d.cum, pc)
        P.tt("pool", d.d1, d.cum, d.lw, ALU.subtract)
        P.tt("dve", d.d2, pl, d.cum, ALU.subtract)
        P.act(d.e1, d.cum, AF.Exp)
        P.act(d.e2, d.cum, AF.Exp, scale=-1.0)
        P.act(d.d1, d.d1, AF.Exp)
        P.act(d.d2, d.d2, AF.Exp)
        P.tt("dve", d.At, d.aa, d.d1, ALU.mult)
        P.tt("pool", d.Bt, d.bb, d.e2, ALU.mult)
        P.tt("dve", d.Kt, d.kd, d.e2, ALU.mult)
        P.tt("pool", d.Rt, d.rr, d.e1, ALU.mult)
        P.tt("dve", d.Bg, d.bb, d.d2, ALU.mult)
        P.tt("pool", d.Kg, d.kd, d.d2, ALU.mult)
        qi = 0
        for (src, dst) in [(d.At, d.AtT), (d.Bt, d.BtT), (d.Kt, d.KtT), (d.Rt, d.RtT)]:
            for hg in range(2):
                pb = S.bank()
                for hh in range(4):
                    h = hg * 4 + hh
                    P.tr(pb[0:64, hh * 128:(hh + 1) * 128], src[:, h * 64:(h + 1) * 64], ident)
                P.copy("act" if qi % 2 == 0 else "dve", dst[:, hg * 4:(hg + 1) * 4, :].re("p a t -> p (a t)"), pb[0:64, :])
                qi += 1
        for hg in range(2):
            M, Mt, Pt = d.M[hg], d.Mt[hg], d.Pt[hg]
            specs = [(d.AtT, d.BtT, mN[z], M[0]), (d.BtT, d.AtT, mTs[z], Mt[0]), (d.KtT, d.AtT, mTs[z], d.Aak[hg]),
                     (d.BtT, d.RtT, mTi[z], d.Arb[hg]), (d.KtT, d.RtT, mTi[z], d.Ark[hg])]
            for (LT, RT_, msk, dst) in specs:
                ps = S.bank()
                for hh in range(4):
                    h = hg * 4 + hh
                    P.mm(ps[:, hh * 128:(hh + 1) * 128], opd(LT, h), opd(RT_, h))
                P.tt("dve", dst, b4(ps), mb(msk), ALU.mult)
            P.tt("pool", Pt[0], Mt[0], mb(ident), ALU.add)
            cur = 0
            for j in range(1, 7):
                nxt = 1 - cur
                if j < 6:
                    ps1 = S.bank()
                    for hh in range(4):
                        P.mm(ps1[:, hh * 128:(hh + 1) * 128], M[cur][:, hh, :], Mt[cur][:, hh, :])
                    P.copy("act", Mt[nxt], b4(ps1))
                ps2 = S.bank()
                for hh in range(4):
                    P.mm(ps2[:, hh * 128:(hh + 1) * 128], Mt[cur][:, hh, :], M[cur][:, hh, :])
                P.copy("dve", M[nxt], b4(ps2))
                ps3 = S.bank()
                for hh in range(4):
                    P.mm(ps3[:, hh * 128:(hh + 1) * 128], M[nxt][:, hh, :], Pt[cur][:, hh, :])
                P.tt("dve", Pt[nxt], b4(ps3), Pt[cur], ALU.add)
                cur = nxt
            assert cur == 0

    def seq_stages(z, c):
        d = Dz[z]
        rows = slice(c * 128, (c + 1) * 128)
        H = d.H

        def st_z():
            pz = S.bank()
            for h in range(8):
                hc = slice(h * 64, (h + 1) * 64)
                P.mm(pz[:, hc], d.AtT[:, h, :], H[:, h, :], start=True, stop=False)
                P.mm(pz[:, hc], d.Aak[h // 4][:, h % 4, :], d.vv[:, hc], start=False, stop=True)
            P.copy("act", d.Zs, pz)

        def st_u():
            pu = S.bank()
            for h in range(8):
                hc = slice(h * 64, (h + 1) * 64)
                P.mm(pu[:, hc], d.Pt[h // 4][0][:, h % 4, :], d.Zs[:, hc])
            P.copy("dve", d.Us, pu)

        def st_o():
            po = S.bank()
            for h in range(8):
                hc = slice(h * 64, (h + 1) * 64)
                P.mm(po[:, hc], d.RtT[:, h, :], H[:, h, :], start=True, stop=False)
                P.mm(po[:, hc], d.Arb[h // 4][:, h % 4, :], d.Us[:, hc], start=False, stop=False)
                P.mm(po[:, hc], d.Ark[h // 4][:, h % 4, :], d.vv[:, hc], start=False, stop=True)
            P.copy("act", d.Os, po)
            P.dma("sp", RS["O%d" % z][rows, :], d.Os)

        def st_h():
            ph = S.bank()
            for h in range(8):
                hc = slice(h * 64, (h + 1) * 64)
                P.mm(ph[0:64, hc], d.Bg[:, hc], d.Us[:, hc], start=True, stop=False)
                P.mm(ph[0:64, hc], d.Kg[:, hc], d.vv[:, hc], start=False, stop=True)
            P.tt("dve", H, H, bc3(d.gL, [64, 8, 64]), ALU.mult)
            P.tt("dve", H, H, ph[0:64, :].re("p (h v) -> p h v", v=64), ALU.add)

        return [st_z, st_u, st_o, st_h]

    for step in range(nsteps):
        for z in range(2):
            pre(z, order[z][step])
        stg = [seq_stages(z, order[z][step]) for z in range(2)]
        for k in range(4):
            for z in range(2):
                stg[z][k]()
    P.phase_end()


def rwkv_readout(S, l, b, need_ctx, own_phase=True):
    P = S.P
    RS = S.RS
    W = S.W
    if own_phase:
        P.phase_begin()
    gng = P.sb("gng", [128, 512], F32)
    gnb = P.sb("gnb", [128, 512], F32)
    rkb = P.sb("rkb", [128, 512], F32)
    P.dma("sp", gng, W["rwkv_gn_g"][l:l + 1, :].bc([128, 512]))
    P.dma("sp", gnb, W["rwkv_gn_b"][l:l + 1, :].bc([128, 512]))
    P.dma("sp", rkb, W["rwkv_r_k"][l:l + 1, :].bc([128, 512]))
    names = ["O0", "O1", "R", "K", "V", "G"]
    bufs = [{n: P.sb("ro_%s%d" % (n, q), [128, 512], F32) for n in names} for q in range(2)]
    o = P.sb("o", [128, 512], F32)
    sq = P.sb("sq", [128, 512], F32)
    s1 = P.sb("s1", [128, 8], F32)
    s2 = P.sb("s2", [128, 8], F32)
    m2 = P.sb("m2", [128, 8], F32)
    rk = P.sb("rk", [128, 512], F32)
    sb8 = P.sb("sb8", [128, 8], F32)
    yo = [P.sb("yo%d" % q, [128, 512], F32) for q in range(2)]
    tiles = list(range(16)) + ([16, 17] if need_ctx else [])

    def v3(t):
        return t.re("p (h d) -> p h d", d=64)

    def loads(n, i):
        for nm in names:
            P.dma("sp", bufs[n % 2][nm], RS[nm][i * 128:(i + 1) * 128, :])

    loads(0, tiles[0])
    for n, i in enumerate(tiles):
        if n + 1 < len(tiles):
            loads(n + 1, tiles[n + 1])
        B = bufs[n % 2]
        y = yo[n % 2]
        P.tt("dve", o, B["O0"], B["O1"], ALU.add)
        P.reduce(s1, v3(o), ALU.add)
        P.tt("pool", sq, o, o, ALU.mult)
        P.reduce(s2, v3(sq), ALU.add)
        P.ts("dve", s1, s1, 1.0 / 64, op0=ALU.mult)
        P.tt("dve", m2, s1, s1, ALU.mult)
        P.stt(s2, s2, 1.0 / 64, m2, ALU.mult, ALU.subtract)
        P.act(s2, s2, AF.Sqrt, bias=S.eps[:, 2:3], scale=1.0)
        P.recip(s2, s2)
        P.tt("dve", v3(o), v3(o), bc3(s1, [128, 8, 64]), ALU.subtract)
        P.tt("dve", v3(o), v3(o), bc3(s2, [128, 8, 64]), ALU.mult)
        P.tt("dve", o, o, gng, ALU.mult)
        P.tt("pool", o, o, gnb, ALU.add)
        P.tt("dve", rk, B["R"], B["K"], ALU.mult)
        P.tt("pool", rk, rk, rkb, ALU.mult)
        P.reduce(sb8, v3(rk), ALU.add)
        P.tt("dve", v3(rk), v3(B["V"]), bc3(sb8, [128, 8, 64]), ALU.mult)
        P.tt("dve", o, o, rk, ALU.add)
        P.tt("dve", y, o, B["G"], ALU.mult)
        P.dma("sp", S.YR[b][i * 128:(i + 1) * 128, :], y)
        yield
    if own_phase:
        P.phase_end()
```

```python
import numpy as np
import concourse.bass as bass
import concourse.mybir as mybir
from concourse.bass_utils import run_bass_kernel_spmd

F32 = mybir.dt.float32
BF16 = mybir.dt.bfloat16
AF = mybir.ActivationFunctionType
ALU = mybir.AluOpType
AX = mybir.AxisListType


class Tok:
    __slots__ = ("w", "r", "dsem", "dcount", "name")

    def __init__(self, name=""):
        self.w = None
        self.r = {}
        self.dsem = None
        self.dcount = 0
        self.name = name


class T:
    __slots__ = ("ap", "tok")

    def __init__(self, ap, tok):
        self.ap = ap
        self.tok = tok

    def __getitem__(self, idx):
        return T(self.ap[idx], self.tok)

    def re(self, pat, **kw):
        return T(self.ap.rearrange(pat, **kw), self.tok)

    def bc(self, shape):
        return T(self.ap.to_broadcast(shape), self.tok)

    def bitcast(self, dt):
        return T(self.ap.bitcast(dt), self.tok)

    def wt(self, tok):
        return T(self.ap, tok)

    def r(self):
        return T(self.ap.bitcast(mybir.dt.float32r), self.tok)

    @property
    def shape(self):
        return self.ap.shape


class Prog:
    def __init__(self, nc):
        self.nc = nc
        self.eng = {"pe": nc.tensor, "dve": nc.vector, "act": nc.scalar, "pool": nc.gpsimd, "sp": nc.sync}
        self.sem = {k: nc.alloc_semaphore("sem_" + k) for k in self.eng}
        self.cnt = {k: 0 for k in self.eng}
        self.seen = {k: {} for k in self.eng}
        self.dsems = []
        self.ninst = 0
        self.dpool = []
        self.phase_stack = None

    def dram(self, name, shape, dt, kind="Internal"):
        h = self.nc.dram_tensor(name, list(shape), dt, kind=kind)
        return T(h.ap(), Tok(name))

    def sb(self, name, shape, dt):
        if self.phase_stack is None:
            h = self.nc.alloc_sbuf_tensor(name, list(shape), dt)
            return T(h.ap(), Tok(name))
        self.uid = getattr(self, "uid", 0) + 1
        g = self.nc.sbuf_tensor("%s_%d" % (name, self.uid), list(shape), dt)
        h = g.__enter__()
        t = T(h.ap(), Tok(name))
        self.phase_stack.append((g, t.tok))
        return t

    def phase_begin(self):
        assert self.phase_stack is None
        self.phase_stack = []

    def phase_end(self):
        self.barrier()
        for g, tok in reversed(self.phase_stack):
            if tok.dsem is not None:
                self.dpool.append((tok.dsem, tok.dcount))
                self.dsems.remove(tok)
                tok.dsem = None
            g.__exit__(None, None, None)
        self.phase_stack = None

    def ps(self, name, shape, dt=F32):
        h = self.nc.alloc_psum_tensor(name, list(shape), dt)
        return T(h.ap(), Tok(name))

    def _wait(self, eng, events):
        best = {}
        for ev in events:
            if ev is None:
                continue
            s, v = ev
            if s.num not in best or best[s.num][1] < v:
                best[s.num] = (s, v)
        seen = self.seen[eng]
        e = self.eng[eng]
        for num, (s, v) in best.items():
            if eng == "pe" and s is self.sem["pe"]:
                continue
            if seen.get(num, 0) < v:
                e.wait_ge(s, v)
                seen[num] = v
                self.ninst += 1

    def _deps(self, w, r):
        evs = []
        for t in r:
            evs.append(t.tok.w)
        for t in w:
            evs.append(t.tok.w)
            evs.extend(t.tok.r.values())
        return evs

    def op(self, eng, fn, w, r):
        self._wait(eng, self._deps(w, r))
        inst = fn(self.eng[eng])
        self.cnt[eng] += 1
        self.ninst += 1
        s = self.sem[eng]
        inst.then_inc(s, 1)
        ev = (s, self.cnt[eng])
        for t in r:
            t.tok.r[s.num] = ev
        for t in w:
            t.tok.w = ev
            t.tok.r = {}
        return ev

    def dma(self, eng, out, in_, **kw):
        tok = out.tok
        evs = [in_.tok.w] + list(tok.r.values())
        if tok.w is not None and tok.w[0] is not tok.dsem:
            evs.append(tok.w)
        self._wait(eng, evs)
        if tok.dsem is None:
            if self.dpool:
                tok.dsem, tok.dcount = self.dpool.pop()
            else:
                tok.dsem = self.nc.alloc_semaphore("d%d_%s" % (len(self.dsems), tok.name))
            self.dsems.append(tok)
        inst = self.eng[eng].dma_start(out=out.ap, in_=in_.ap, **kw)
        inst.then_inc(tok.dsem, 16)
        self.ninst += 1
        tok.dcount += 16
        ev = (tok.dsem, tok.dcount)
        in_.tok.r[tok.dsem.num] = ev
        tok.w = ev
        tok.r = {}
        return ev

    def barrier(self):
        evs = [(self.sem[k], self.cnt[k]) for k in self.eng if self.cnt[k] > 0]
        evs += [(t.dsem, t.dcount) for t in self.dsems]
        for k in self.eng:
            self._wait(k, evs)

    def finish(self):
        self.barrier()

    def mm(self, out, lhsT, rhs, start=True, stop=True, r32=False):
        la, ra = lhsT.ap, rhs.ap
        if r32:
            la, ra = la.bitcast(mybir.dt.float32r), ra.bitcast(mybir.dt.float32r)
        return self.op("pe", lambda e: e.matmul(out.ap, la, ra, start=start, stop=stop),
                       [out], [lhsT, rhs])

    def tr(self, out, in_, ident):
        return self.op("pe", lambda e: e.transpose(out.ap, in_.ap, ident.ap), [out], [in_, ident])

    def act(self, out, in_, func, bias=None, scale=None, accum=None, eng="act"):
        kw = {}
        r = [in_]
        w = [out]
        if bias is not None:
            if isinstance(bias, T):
                kw["bias"] = bias.ap
                r.append(bias)
            else:
                kw["bias"] = bias
        if scale is not None:
            if isinstance(scale, T):
                kw["scale"] = scale.ap
                r.append(scale)
            else:
                kw["scale"] = scale
        if accum is not None:
            kw["accum_out"] = accum.ap
            w.append(accum)
        return self.op("act", lambda e: e.activation(out.ap, in_.ap, func, **kw), w, r)

    def copy(self, eng, out, in_):
        if eng == "act":
            return self.op("act", lambda e: e.copy(out.ap, in_.ap), [out], [in_])
        return self.op(eng, lambda e: e.tensor_copy(out.ap, in_.ap), [out], [in_])

    def tt(self, eng, out, a, b, op):
        return self.op(eng, lambda e: e.tensor_tensor(out.ap, a.ap, b.ap, op), [out], [a, b])

    def ts(self, eng, out, a, s1, s2=None, op0=ALU.mult, op1=None, accum=None):
        r = [a]
        w = [out]
        v1 = s1
        v2 = s2
        if isinstance(s1, T):
            r.append(s1)
            v1 = s1.ap
        if isinstance(s2, T):
            r.append(s2)
            v2 = s2.ap
        kw = {}
        if op1 is not None:
            kw["op1"] = op1
        if accum is not None:
            kw["accum_out"] = accum.ap
            w.append(accum)
        return self.op(eng, lambda e: e.tensor_scalar(out.ap, a.ap, v1, v2, op0, **kw), w, r)

    def stt(self, out, in0, scalar, in1, op0, op1, eng="dve"):
        r = [in0, in1]
        v = scalar
        if isinstance(scalar, T):
            r.append(scalar)
            v = scalar.ap
        return self.op(eng, lambda e: e.scalar_tensor_tensor(out.ap, in0.ap, v, in1.ap, op0, op1), [out], r)

    def memset(self, eng, out, val):
        return self.op(eng, lambda e: e.memset(out.ap, val), [out], [])

    def reduce(self, out, in_, op, axis=None, eng="dve"):
        ax = axis if axis is not None else AX.X
        return self.op(eng, lambda e: e.tensor_reduce(out.ap, in_.ap, ax, op), [out], [in_])

    def recip(self, out, in_):
        return self.op("dve", lambda e: e.reciprocal(out.ap, in_.ap), [out], [in_])


NB = 2
TL = 2048
TC = 256
TT = TL + TC
NT = TT // 128
D = 1024
DFF = 2816
DIN = 7296
DEPTH = 2
ALPHA = (2 * DEPTH) ** 0.25
LN_EPS = 1e-5
RMS_EPS = 1e-6
GN_EPS = 64e-5
MASKV = -30000.0
C_AQ, C_AK, C_AV, C_RW, C_NQ, C_NK, C_NV, C_G = 0, 512, 640, 768, 2688, 3200, 3712, 4224


NB = 2
TL = 2048
TC = 256
TT = TL + TC
NT = TT // 128
D = 1024
DFF = 2816
DIN = 7296
DEPTH = 2
ALPHA = (2 * DEPTH) ** 0.25
LN_EPS = 1e-5
RMS_EPS = 1e-6
GN_EPS = 64e-5
MASKV = -30000.0
C_AQ, C_AK, C_AV, C_RW, C_NQ, C_NK, C_NV, C_G = 0, 512, 640, 768, 2688, 3200, 3712, 4224
DRW = 1920

WNAMES = ["w_ada", "b_ada", "ffn1_wi", "ffn1_wo", "ffn2_wi", "ffn2_wo", "ln_g", "ln_b", "w_in",
          "att_q_gain", "att_k_gain", "rwkv_mu", "rwkv_w0", "rwkv_w2", "rwkv_a0", "rwkv_a2", "rwkv_g2",
          "rwkv_k_k", "rwkv_k_a", "rwkv_r_k", "rwkv_gn_g", "rwkv_gn_b", "w_branch", "w_out"]
WSHAPES = {
    "w_ada": [DEPTH, D, 9 * D], "b_ada": [DEPTH, 9 * D], "ffn1_wi": [DEPTH, D, 2 * DFF], "ffn1_wo": [DEPTH, DFF, D],
    "ffn2_wi": [DEPTH, D, 2 * DFF], "ffn2_wo": [DEPTH, DFF, D], "ln_g": [DEPTH, 3, D], "ln_b": [DEPTH, 3, D],
    "w_in": [DEPTH, D, DIN], "att_q_gain": [DEPTH, 64], "att_k_gain": [DEPTH, 64], "rwkv_mu": [DEPTH, 2, DRW],
    "rwkv_w0": [DEPTH, 2, 512], "rwkv_w2": [DEPTH, 2, 64, 512], "rwkv_a0": [DEPTH, 2, 512],
    "rwkv_a2": [DEPTH, 2, 64, 512], "rwkv_g2": [DEPTH, 128, 512], "rwkv_k_k": [DEPTH, 512], "rwkv_k_a": [DEPTH, 512],
    "rwkv_r_k": [DEPTH, 512], "rwkv_gn_g": [DEPTH, 512], "rwkv_gn_b": [DEPTH, 512],
    "w_branch": [DEPTH, 3, 512, D], "w_out": [DEPTH, D, D],
}


def na_tiles(i):
    if 2 <= i <= 13:
        return [(i - 2 + p, p) for p in range(5)]
    if i == 0:
        return [(j, 5 + j) for j in range(4)]
    if i == 1:
        return [(j, 9 + j) for j in range(4)]
    if i == 14:
        return [(12 + j, 13 + j) for j in range(4)]
    return [(12 + j, 17 + j) for j in range(4)]


NPAT = 21


class Ctx:
    pass


def build(dbg=False, stop=None, nlayers=DEPTH):
    nc = bass.Bass("TRN2", target_bir_lowering=False)
    P = Prog(nc)
    S = Ctx()
    S.P = P
    S.dbg = dbg
    import os
    S.nsteps = int(os.environ.get('RWKV_STEPS', NT))
    okind = "ExternalOutput" if dbg else "Internal"
    S.x_in = P.dram("x", [NB, TL, D], F32, kind="ExternalInput")
    S.ctx_in = P.dram("ctx", [NB, TC, D], F32, kind="ExternalInput")
    S.c3_in = P.dram("c3", [3, D], F32, kind="ExternalInput")
    S.W = {n: P.dram(n, WSHAPES[n], F32, kind="ExternalInput") for n in WNAMES}
    S.nab = P.dram("nab", [DEPTH, NPAT, 128, 8, 128], F32, kind="ExternalInput")
    S.cst = P.dram("cst", [128, 6, 128], F32, kind="ExternalInput")
    S.rope = P.dram("rope", [TL, 64], F32, kind="ExternalInput")
    S.out = P.dram("out", [NB, TL, D], F32, kind="ExternalOutput")
    S.XA = [P.dram("XA%d" % b, [TT, D], F32, kind=okind) for b in range(NB)]
    S.XB = [P.dram("XB%d" % b, [TT, D], F32, kind=okind) for b in range(NB)]
    S.mod = P.dram("mod", [3, 9 * D], F32, kind=okind)
    S.QA = [P.dram("QA%d" % b, [TT, 512], BF16) for b in range(NB)]
    S.KA = [P.dram("KA%d" % b, [TT, 128], BF16) for b in range(NB)]
    S.VA = [P.dram("VA%d" % b, [TT, 128], BF16) for b in range(NB)]
    S.QN = [P.dram("QN%d" % b, [TT, 512], BF16) for b in range(NB)]
    S.KN = [P.dram("KN%d" % b, [TT, 512], BF16) for b in range(NB)]
    S.VN = [P.dram("VN%d" % b, [TT, 512], BF16) for b in range(NB)]
    S.GT = [P.dram("GT%d" % b, [TT, 3 * D], BF16) for b in range(NB)]
    S.YRI = [P.dram("YRI%d" % b, [TT, DRW], F32, kind=okind) for b in range(NB)]
    S.YA = [P.dram("YA%d" % b, [TT, 512], F32, kind=okind) for b in range(NB)]
    S.YR = [P.dram("YR%d" % b, [TT, 512], F32, kind=okind) for b in range(NB)]
    S.YN = [P.dram("YN%d" % b, [TT, 512], F32, kind=okind) for b in range(NB)]
    S.RS = {n: P.dram("RS_" + n, [TT, 512], F32, kind=okind) for n in
            ["R", "K", "V", "G", "A", "LW0", "LW1", "KD0", "KD1", "B0", "B1", "O0", "O1"]}
    S.cf = P.sb("cst_f", [128, 6, 128], F32)
    P.dma("sp", S.cf, S.cst)
    S.identb = P.sb("identb", [128, 128], BF16)
    P.copy("dve", S.identb, S.cf[:, 0, :])
    S.identf = S.cf[:, 0, :]
    S.banks = [P.ps("bank%d" % i, [128, 512], F32) for i in range(8)]
    S.bank_i = 0

    def bank():
        b = S.banks[S.bank_i % 8]
        S.bank_i += 1
        return b
    S.bank = bank
    S.eps = P.sb("eps", [128, 4], F32)
    for i, v in enumerate([LN_EPS, RMS_EPS, GN_EPS, 1e-12]):
        P.memset("dve", S.eps[:, i:i + 1], v)

    xsrc = [None] * NB
    for l in range(nlayers):
        last = l == DEPTH - 1
        mod_phase(S, l)
        if stop == ("mod", l):
            break
        src = (lambda b, i: (S.x_in[b, i * 128:(i + 1) * 128, :] if i < 16 else S.ctx_in[b, (i - 16) * 128:(i - 15) * 128, :])) \
            if l == 0 else (lambda b, i: S.XB[b][i * 128:(i + 1) * 128, :])
        dst = lambda b, i: S.XA[b][i * 128:(i + 1) * 128, :]
        ffn_phase(S, l, 0, src, dst, list(range(NT)))
        if stop == ("ffn1", l):
            break
        mixpre_phase(S, l)
        if stop == ("mixpre", l):
            break
        if stop == ("att", l):
            for b in range(NB):
                att_phase(S, l, b, need_ctx=not last)
                na_phase(S, l, b, need_ctx=not last)
            break
        if stop == ("rwkvscan", l):
            for _ in rwkv_prep(S, l, 0):
                pass
            rwkv_scan(S, l, 0, nsteps=S.nsteps)
            break
        for b in range(NB):
            att_phase(S, l, b, need_ctx=not last)
            na_phase(S, l, b, need_ctx=not last)
        for b in range(NB):
            rwkv_phase(S, l, b, need_ctx=not last)
        if stop == ("rwkv", l):
            break
        merge_phase(S, l, list(range(NT)) if not last else list(range(16)))
        if stop == ("merge", l):
            break
        src = lambda b, i: S.XB[b][i * 128:(i + 1) * 128, :]
        if last:
            dst = lambda b, i: S.out[b, i * 128:(i + 1) * 128, :]
            ffn_phase(S, l, 1, src, dst, list(range(16)))
        else:
            dst = lambda b, i: S.XA[b][i * 128:(i + 1) * 128, :]
            ffn_phase(S, l, 1, src, dst, list(range(NT)))
            S.XA, S.XB = S.XB, S.XA
    P.finish()
    S.nc = nc
    return S


def bc3(t, shape):
    return T(t.ap.unsqueeze(2).to_broadcast(shape), t.tok)


def transpose_bf(S, src, dst, nch, eng_cycle=("act", "dve")):
    P = S.P
    gi = 0
    for g0 in range(0, nch, 8):
        n = min(8, nch - g0)
        pb = S.bank().bitcast(BF16)
        for k in range(n):
            P.tr(pb[:, k * 128:(k + 1) * 128], src[:, (g0 + k) * 128:(g0 + k + 1) * 128], S.identb)
        P.copy(eng_cycle[gi % len(eng_cycle)], dst[:, g0 * 128:(g0 + n) * 128], pb[:, 0:n * 128])
        gi += 1


def layernorm(S, z, g_bc, b_bc, out, wk):
    P = S.P
    stats, mv, rstd, xn = wk
    for j in range(2):
        P.op("dve", lambda e: e.bn_stats(stats[:, j, :].ap, z[:, j * 512:(j + 1) * 512].ap), [stats], [z])
    P.op("dve", lambda e: e.bn_aggr(mv.ap, stats.re("p a b -> p (a b)").ap), [mv], [stats])
    P.act(rstd, mv[:, 1:2], AF.Sqrt, bias=S.eps[:, 0:1], scale=1.0)
    P.recip(rstd, rstd)
    P.ts("dve", xn, z, mv[:, 0:1], rstd, op0=ALU.subtract, op1=ALU.mult)
    P.tt("pool", xn, xn, g_bc, ALU.mult)
    P.tt("dve", out, xn, b_bc, ALU.add)


def ln_work(P):
    return (P.sb("ln_stats", [128, 2, 6], F32), P.sb("ln_mv", [128, 2], F32),
            P.sb("ln_rstd", [128, 1], F32), P.sb("ln_xn", [128, D], F32))


def row_tiles(row, tiles):
    if row < 2:
        return [(row, i) for i in tiles if i < 16]
    return [(b, i) for b in range(NB) for i in tiles if i >= 16]


def load_mod(S, row, slots, dsts):
    for s, d in zip(slots, dsts):
        S.P.dma("sp", d, S.mod[row:row + 1, s * D:(s + 1) * D].bc([128, D]))


def mod_phase(S, l):
    P = S.P
    P.phase_begin()
    c3 = P.sb("c3", [3, D], F32)
    P.dma("sp", c3, S.c3_in)
    sc = P.sb("sc", [3, D], F32)
    P.act(sc, c3, AF.Silu)
    scT = P.sb("scT", [128, 8, 3], F32)
    pb = S.bank()
    for k in range(8):
        P.tr(pb[:, k * 4:k * 4 + 3], sc[:, k * 128:(k + 1) * 128], S.identf[0:3, 0:3])
    P.copy("dve", scT, pb[:, 0:32].re("p (k c) -> p k c", c=4)[:, :, 0:3])
    ba = P.sb("ba", [3, 9 * D], F32)
    P.dma("sp", ba, S.W["b_ada"][l:l + 1, :].bc([3, 9 * D]))
    m_sb = P.sb("m_sb", [3, 9 * D], F32)
    wch = [P.sb("wch%d" % i, [128, 8, 512], F32) for i in range(2)]
    wsrc = S.W["w_ada"][l].re("(k p) n -> p k n", p=128)
    for j in range(18):
        w = wch[j % 2]
        P.dma("sp", w, wsrc[:, :, j * 512:(j + 1) * 512])
        pb = S.bank()
        for k in range(8):
            P.mm(pb[0:3, :], scT[:, k, :], w[:, k, :], start=(k == 0), stop=(k == 7))
        P.tt("dve", m_sb[:, j * 512:(j + 1) * 512], pb[0:3, :], ba[:, j * 512:(j + 1) * 512], ALU.add)
    P.dma("sp", S.mod, m_sb)
    P.phase_end()


def ffn_phase(S, l, f, src, dst, tiles):
    P = S.P
    P.phase_begin()
    wi_d = S.W["ffn%d_wi" % (f + 1)]
    wo_d = S.W["ffn%d_wo" % (f + 1)]
    wi = P.sb("wi", [128, 8, 2 * DFF], BF16)
    wo = P.sb("wo", [128, 22, D], BF16)
    for k in range(8):
        P.dma("pool", wi[:, k, :], wi_d[l, k * 128:(k + 1) * 128, :])
    for k in range(22):
        P.dma("pool", wo[:, k, :], wo_d[l, k * 128:(k + 1) * 128, :])
    lni = 0 if f == 0 else 2
    lng = P.sb("lng", [128, D], F32)
    lnb = P.sb("lnb", [128, D], F32)
    P.dma("sp", lng, S.W["ln_g"][l, lni:lni + 1, :].bc([128, D]))
    P.dma("sp", lnb, S.W["ln_b"][l, lni:lni + 1, :].bc([128, D]))
    s0 = 0 if f == 0 else 6
    sh = P.sb("sh", [128, D], F32)
    sc = P.sb("sc", [128, D], F32)
    gt = P.sb("gt", [128, D], F32)
    xb = [P.sb("xb%d" % i, [128, D], F32) for i in range(2)]
    u32 = P.sb("u32", [128, D], F32)
    ub = P.sb("ub", [128, D], BF16)
    uT = P.sb("uT", [128, D], BF16)
    sg = [P.sb("sg%d" % i, [128, 512], F32) for i in range(2)]
    act = P.sb("act", [128, DFF], BF16)
    actT = P.sb("actT", [128, DFF], BF16)
    tmp = P.sb("tmp", [128, D], F32)
    z = P.sb("z", [128, D], F32)
    xo = P.sb("xo", [128, D], F32)
    wk = ln_work(P)
    for row in range(3):
        rt = row_tiles(row, tiles)
        if not rt:
            continue
        load_mod(S, row, [s0, s0 + 1, s0 + 2], [sh, sc, gt])
        P.ts("pool", sc, sc, 1.0, op0=ALU.add)
        P.ts("pool", gt, gt, 0.5, op0=ALU.mult)
        P.dma("sp", xb[0], src(*rt[0]))

        def front(n):
            P.tt("pool", u32, xb[n % 2], sc, ALU.mult)
            P.tt("dve", ub, u32, sh, ALU.add)
            transpose_bf(S, ub, uT, 8)

        front(0)
        for n, (b, i) in enumerate(rt):
            xt = xb[n % 2]
            if n + 1 < len(rt):
                P.dma("sp", xb[(n + 1) % 2], src(*rt[n + 1]))
            for ci, c0 in enumerate(range(0, DFF, 512)):
                w = min(512, DFF - c0)
                pg = S.bank()
                pu = S.bank()
                for k in range(8):
                    P.mm(pg[:, 0:w], uT[:, k * 128:(k + 1) * 128], wi[:, k, c0:c0 + w], start=(k == 0), stop=(k == 7))
                for k in range(8):
                    P.mm(pu[:, 0:w], uT[:, k * 128:(k + 1) * 128], wi[:, k, DFF + c0:DFF + c0 + w], start=(k == 0), stop=(k == 7))
                s_ = sg[ci % 2]
                P.act(s_[:, 0:w], pg[:, 0:w], AF.Silu)
                P.tt("dve", act[:, c0:c0 + w], s_[:, 0:w], pu[:, 0:w], ALU.mult)
            transpose_bf(S, act, actT, 22)
            pos = []
            for n2 in range(2):
                po = S.bank()
                for hc in range(22):
                    P.mm(po, actT[:, hc * 128:(hc + 1) * 128], wo[:, hc, n2 * 512:(n2 + 1) * 512], start=(hc == 0), stop=(hc == 21))
                pos.append(po)
            if n + 1 < len(rt):
                front(n + 1)
            for n2 in range(2):
                P.tt("dve", tmp[:, n2 * 512:(n2 + 1) * 512], pos[n2], gt[:, n2 * 512:(n2 + 1) * 512], ALU.mult)
            P.stt(z, xt, ALPHA, tmp, ALU.mult, ALU.add)
            layernorm(S, z, lng, lnb, xo, wk)
            P.dma("sp", dst(b, i), xo)
    P.phase_end()


def _consts():
    p = np.arange(128)[:, None]
    f = np.arange(128)[None, :]
    cst = np.stack([(p == f), (p > f), (p < f), (p >= f), (p <= f), np.ones((128, 128), bool)], 1).astype(np.float32)
    t = np.arange(TL)
    row = (t // 64).astype(np.float32)
    col = (t % 64).astype(np.float32)
    inv = (np.float32(10000.0) ** (-np.arange(0, 32, 2, dtype=np.float32) / np.float32(32))).astype(np.float32)
    ang = np.stack([row[:, None] * inv, col[:, None] * inv], 1).astype(np.float32)
    rope = np.concatenate([np.cos(ang).reshape(TL, 32), np.sin(ang).reshape(TL, 32)], 1).astype(np.float32)
    return np.ascontiguousarray(cst), np.ascontiguousarray(rope)


def _na_bias(rpb):
    L = rpb.shape[0]
    out = np.full((L, NPAT, 128, 8, 128), MASKV, np.float32)
    reps = {}
    for i in [2, 0, 1, 14, 15]:
        for (j, pat) in na_tiles(i):
            if pat not in reps:
                reps[pat] = (i, j)
    kk, kc = np.divmod(np.arange(128), 64)
    qq, qc = np.divmod(np.arange(128), 64)
    for pat, (i, j) in reps.items():
        r = 2 * i + qq
        kr = 2 * j + kk
        r0 = np.clip(r - 4, 0, 32 - 8)
        c0 = np.clip(qc - 8, 0, 64 - 16)
        ok = (kr[:, None] >= r0[None, :]) & (kr[:, None] < r0[None, :] + 8) & \
             (kc[:, None] >= c0[None, :]) & (kc[:, None] < c0[None, :] + 16)
        dr = np.clip(kr[:, None] - r[None, :] + 7, 0, 14)
        dc = np.clip(kc[:, None] - qc[None, :] + 15, 0, 30)
        g = rpb[:, :, dr, dc]
        g = np.where(ok[None, None], g, np.float32(MASKV))
        out[:, pat] = np.transpose(g, (0, 2, 1, 3))
    return out


def make_in_maps(inp, cores=range(8)):
    cst, rope = _consts()
    shared = {n: np.ascontiguousarray(np.asarray(inp[n], np.float32).reshape(WSHAPES[n])) for n in WNAMES}
    shared["nab"] = _na_bias(np.asarray(inp["na_rpb"], np.float32))
    shared["cst"] = cst
    shared["rope"] = rope
    maps = []
    for c in cores:
        m = dict(shared)
        m["x"] = np.ascontiguousarray(inp["x"][c * NB:(c + 1) * NB])
        m["ctx"] = np.ascontiguousarray(inp["ctx"][c * NB:(c + 1) * NB])
        m["c3"] = np.ascontiguousarray(np.concatenate([inp["c"][c * NB:(c + 1) * NB], inp["c_ctx"][None, :]], 0))
        maps.append(m)
    return maps


_CACHE = {}


def kernel(**inputs):
    if "S" not in _CACHE:
        _CACHE["S"] = build()
    S = _CACHE["S"]
    maps = make_in_maps(inputs)
    res = run_bass_kernel_spmd(S.nc, maps, core_ids=list(range(8)))
    return np.concatenate([np.asarray(r["out"], np.float32) for r in res.results], 0)


def mixpre_phase(S, l):
    P = S.P
    P.phase_begin()
    win = P.sb("win", [128, 8, DIN], BF16)
    for k in range(8):
        P.dma("pool", win[:, k, :], S.W["w_in"][l, k * 128:(k + 1) * 128, :])
    gain = P.sb("gain", [128, 10, 64], F32)
    for h in range(10):
        gsrc = S.W["att_q_gain"] if h < 8 else S.W["att_k_gain"]
        P.dma("sp", gain[:, h, :], gsrc[l:l + 1, :].bc([128, 64]))
    sh = P.sb("sh", [128, D], F32)
    sc = P.sb("sc", [128, D], F32)
    xb = [P.sb("xb%d" % i, [128, D], F32) for i in range(2)]
    u32 = P.sb("u32", [128, D], F32)
    ub = P.sb("ub", [128, D], BF16)
    uT = P.sb("uT", [128, D], BF16)
    NPX = C_NQ
    pxs = [P.sb("px%d" % i, [128, NPX], F32) for i in range(2)]
    gtss = [P.sb("gts%d" % i, [128, 3 * D], BF16) for i in range(2)]
    nbs = [P.sb("nb%d" % i, [128, 1536], BF16) for i in range(2)]
    sq = P.sb("sq", [128, 640], F32)
    ss = P.sb("ss", [128, 10], F32)
    rs = P.sb("rs", [128, 10], F32)
    qn = P.sb("qn", [128, 640], F32)
    rw = P.sb("rw", [128, 4, 320], F32)
    qrs = [P.sb("qr%d" % i, [128, 640], BF16) for i in range(2)]
    vas = [P.sb("va%d" % i, [128, 128], BF16) for i in range(2)]
    cs = P.sb("cs", [128, 64], F32)
    chunks = [(c0, 512) for c0 in range(0, 2560, 512)] + [(2560, 128)] + [(c0, 512) for c0 in range(C_NQ, DIN, 512)]
    cnt = 0
    for row in range(3):
        rt = row_tiles(row, list(range(NT)))
        load_mod(S, row, [3, 4], [sh, sc])
        P.ts("pool", sc, sc, 1.0, op0=ALU.add)
        P.dma("sp", xb[0], S.XA[rt[0][0]][rt[0][1] * 128:(rt[0][1] + 1) * 128, :])

        def front(n):
            P.tt("pool", u32, xb[n % 2], sc, ALU.mult)
            P.tt("dve", ub, u32, sh, ALU.add)
            transpose_bf(S, ub, uT, 8)

        front(0)
        for n, (b, i) in enumerate(rt):
            rows = slice(i * 128, (i + 1) * 128)
            if n + 1 < len(rt):
                b2, i2 = rt[n + 1]
                P.dma("sp", xb[(n + 1) % 2], S.XA[b2][i2 * 128:(i2 + 1) * 128, :])
            px = pxs[cnt % 2]
            gts = gtss[cnt % 2]
            qr = qrs[cnt % 2]
            va = vas[cnt % 2]
            nb = nbs[cnt % 2]
            cnt += 1
            for ci, (c0, w) in enumerate(chunks):
                pb = S.bank()
                for k in range(8):
                    P.mm(pb[:, 0:w], uT[:, k * 128:(k + 1) * 128], win[:, k, c0:c0 + w], start=(k == 0), stop=(k == 7))
                if c0 < C_NQ:
                    P.copy("act" if ci % 2 == 0 else "dve", px[:, c0:c0 + w], pb[:, 0:w])
                elif c0 < C_G:
                    P.copy("dve" if ci % 2 == 0 else "act", nb[:, c0 - C_NQ:c0 - C_NQ + w], pb[:, 0:w])
                else:
                    P.act(gts[:, c0 - C_G:c0 - C_G + w], pb[:, 0:w], AF.Sigmoid)
            if n + 1 < len(rt):
                front(n + 1)
            P.tt("pool", sq, px[:, 0:640], px[:, 0:640], ALU.mult)
            P.reduce(ss, sq.re("p (h d) -> p h d", d=64), ALU.add)
            P.act(rs, ss, AF.Sqrt, bias=S.eps[:, 1:2], scale=1.0 / 64)
            P.recip(rs, rs)
            qn3 = qn.re("p (h d) -> p h d", d=64)
            P.tt("dve", qn3, px[:, 0:640].re("p (h d) -> p h d", d=64), bc3(rs, [128, 10, 64]), ALU.mult)
            P.tt("pool", qn3, qn3, gain, ALU.mult)
            if i < 16:
                P.dma("sp", cs, S.rope[rows, :])
                q5 = qn.re("p (h a j d) -> p h a j d", h=10, a=2, j=2, d=16)
                o5 = qr.re("p (h a j d) -> p h a j d", h=10, a=2, j=2, d=16)
                x1, x2 = q5[:, :, :, 0, :], q5[:, :, :, 1, :]
                cosb = T(cs[:, 0:32].re("p (a d) -> p a d", a=2).ap.unsqueeze(1).to_broadcast([128, 10, 2, 16]), cs.tok)
                sinb = T(cs[:, 32:64].re("p (a d) -> p a d", a=2).ap.unsqueeze(1).to_broadcast([128, 10, 2, 16]), cs.tok)
                tw = [rw[:, t, :].re("p (h a d) -> p h a d", h=10, a=2) for t in range(4)]
                P.tt("dve", tw[0], x1, cosb, ALU.mult)
                P.tt("pool", tw[1], x2, sinb, ALU.mult)
                P.tt("dve", tw[2], x2, cosb, ALU.mult)
                P.tt("pool", tw[3], x1, sinb, ALU.mult)
                P.tt("dve", o5[:, :, :, 0, :], tw[0], tw[1], ALU.subtract)
                P.tt("pool", o5[:, :, :, 1, :], tw[2], tw[3], ALU.add)
            else:
                P.copy("dve", qr, qn)
            P.dma("sp", S.QA[b][rows, :], qr[:, 0:512])
            P.dma("sp", S.KA[b][rows, :], qr[:, 512:640])
            P.copy("pool", va, px[:, C_AV:C_AV + 128])
            P.dma("sp", S.VA[b][rows, :], va)
            P.dma("sp", S.YRI[b][rows, :], px[:, C_RW:C_RW + DRW])
            P.dma("sp", S.QN[b][rows, :], nb[:, 0:512])
            P.dma("sp", S.KN[b][rows, :], nb[:, 512:1024])
            P.dma("sp", S.VN[b][rows, :], nb[:, 1024:1536])
            P.dma("sp", S.GT[b][rows, :], gts)
    P.phase_end()


def load_kT(S, ksrc, KT, nh):
    P = S.P
    ktok = P.sb("ktok", [128, NT, nh * 64], BF16)
    P.dma("sp", ktok, ksrc.re("(i p) c -> p i c", p=128))
    per = 8 // nh
    for i0 in range(0, NT, per):
        n = min(per, NT - i0)
        pb = S.bank().bitcast(BF16)
        for ii in range(n):
            for h in range(nh):
                slot = h * per + ii
                P.tr(pb[0:64, slot * 128:(slot + 1) * 128], ktok[:, i0 + ii, h * 64:(h + 1) * 64], S.identb)
        src = pb[0:64, :].re("p (h i t) -> p h i t", h=nh, i=per)[:, :, 0:n, :]
        dst = KT[:, :, i0 * 128:(i0 + n) * 128].re("p h (i t) -> p h i t", t=128)
        P.copy("act", dst, src)


def load_v1(S, vsrc, V1, nh):
    P = S.P
    P.memset("pool", V1, 1.0)
    for i in range(NT):
        P.dma("sp", V1[:, i, :, 0:64], vsrc[i * 128:(i + 1) * 128, :].re("p (g d) -> p g d", g=nh))


def pv_and_store(S, PT, V1, keyl, kvh_of, ya, dstrows, pt_of=None):
    P = S.P
    nk = len(keyl)
    for hg in range(2):
        po = S.bank()
        for hh in range(4):
            h = hg * 4 + hh
            for idx, j in enumerate(keyl):
                pt = pt_of(idx, h) if pt_of is not None else PT[:, idx, h * 128:(h + 1) * 128]
                P.mm(po[:, hh * 65:(hh + 1) * 65], pt, V1[:, j, kvh_of(h), :],
                     start=(idx == 0), stop=(idx == nk - 1))
        po3 = po[:, 0:260].re("p (h e) -> p h e", e=65)
        rec = S.rec
        P.recip(rec, po3[:, :, 64])
        P.tt("dve", ya[:, hg * 256:(hg + 1) * 256].re("p (h d) -> p h d", d=64), po3[:, :, 0:64],
             bc3(rec, [128, 4, 64]), ALU.mult)
    P.dma("sp", dstrows, ya)


def load_qT(S, qsrc_rows, qtok, QT):
    P = S.P
    P.dma("sp", qtok, qsrc_rows)
    pb = S.bank().bitcast(BF16)
    for h in range(8):
        P.tr(pb[0:64, h * 128:(h + 1) * 128], qtok[:, h * 64:(h + 1) * 64], S.identb)
    P.copy("dve", QT, pb[0:64, :])


def att_phase(S, l, b, need_ctx, side=None):
    P = S.P
    P.phase_begin()
    KT = P.sb("KT", [64, 2, TT], BF16)
    load_kT(S, S.KA[b], KT, 2)
    V1 = P.sb("V1", [128, NT, 2, 65], BF16)
    load_v1(S, S.VA[b], V1, 2)
    S.rec = P.sb("rec", [128, 4], F32)
    qtoks = [P.sb("qtok%d" % i, [128, 512], BF16) for i in range(2)]
    QTs = [P.sb("QT%d" % i, [64, 1024], BF16) for i in range(2)]
    PTgs = [[P.sb("PTg%d_%d" % (g, q), [128, NT, 512], BF16) for g in range(2)] for q in range(2)]
    yas = [P.sb("ya%d" % i, [128, 512], F32) for i in range(2)]
    qtiles = list(range(16)) + ([16, 17] if need_ctx else [])
    sidegen = side() if side is not None else None
    for n, i in enumerate(qtiles):
        keyl = list(range(NT)) if i < 16 else [16, 17]
        QT, ya, PTg = QTs[n % 2], yas[n % 2], PTgs[n % 2]
        rows = slice(i * 128, (i + 1) * 128)
        load_qT(S, S.QA[b][rows, :], qtoks[n % 2], QT)
        for idx, j in enumerate(keyl):
            for g in range(2):
                ps = S.bank()
                P.mm(ps, KT[:, g, j * 128:(j + 1) * 128], QT[:, g * 512:(g + 1) * 512])
                P.act(PTg[g][:, idx, :], ps, AF.Exp, scale=0.125)
        pv_and_store(S, None, V1, keyl, lambda h: h // 4, ya, S.YA[b][rows, :],
                     pt_of=lambda idx, h: PTg[h // 4][:, idx, (h % 4) * 128:(h % 4 + 1) * 128])
        if sidegen is not None:
            next(sidegen, None)
    if sidegen is not None:
        for _ in sidegen:
            pass
    P.phase_end()


def na_phase(S, l, b, need_ctx, side=None):
    P = S.P
    P.phase_begin()
    KT = P.sb("KTn", [64, 8, TT], BF16)
    load_kT(S, S.KN[b], KT, 8)
    V1 = P.sb("V1n", [128, NT, 8, 65], BF16)
    load_v1(S, S.VN[b], V1, 8)
    S.rec = P.sb("rec", [128, 4], F32)
    qtoks = [P.sb("qtok%d" % i, [128, 512], BF16) for i in range(2)]
    QTs = [P.sb("QT%d" % i, [64, 1024], BF16) for i in range(2)]
    PTs = [P.sb("PT%d" % i, [128, 7, 1024], BF16) for i in range(2)]
    yas = [P.sb("ya%d" % i, [128, 512], F32) for i in range(2)]
    nbs = [P.sb("nab%d" % i, [128, 5, 8, 128], F32) for i in range(2)]
    tmps = [P.sb("natmp%d" % i, [128, 512], F32) for i in range(2)]
    qtiles = list(range(16)) + ([16, 17] if need_ctx else [])
    sidegen = side() if side is not None else None
    tc = 0
    for n, i in enumerate(qtiles):
        kl = (na_tiles(i) if i < 16 else []) + [(16, None), (17, None)]
        QT, PT, ya, nb = QTs[n % 2], PTs[n % 2], yas[n % 2], nbs[n % 2]
        rows = slice(i * 128, (i + 1) * 128)
        for idx, (j, pat) in enumerate(kl):
            if pat is not None:
                P.dma("sp", nb[:, idx], S.nab[l, pat])
        load_qT(S, S.QN[b][rows, :], qtoks[n % 2], QT)
        for idx, (j, pat) in enumerate(kl):
            for hg in range(2):
                ps = S.bank()
                for hh in range(4):
                    h = hg * 4 + hh
                    P.mm(ps[:, hh * 128:(hh + 1) * 128], KT[:, h, j * 128:(j + 1) * 128], QT[:, h * 128:(h + 1) * 128])
                if pat is not None:
                    tmp = tmps[tc % 2]
                    tc += 1
                    P.stt(tmp, ps, 0.125, nb[:, idx, hg * 4:(hg + 1) * 4, :].re("p h q -> p (h q)"), ALU.mult, ALU.add)
                    P.act(PT[:, idx, hg * 512:(hg + 1) * 512], tmp, AF.Exp)
                else:
                    P.act(PT[:, idx, hg * 512:(hg + 1) * 512], ps, AF.Exp, scale=0.125)
        pv_and_store(S, PT, V1, [j for j, _ in kl], lambda h: h, ya, S.YN[b][rows, :])
        if sidegen is not None:
            next(sidegen, None)
    if sidegen is not None:
        for _ in sidegen:
            pass
    P.phase_end()


def merge_phase(S, l, tiles):
    P = S.P
    P.phase_begin()
    wb = P.sb("wbr", [128, 12, D], BF16)
    for zz in range(3):
        for k in range(4):
            P.dma("pool", wb[:, zz * 4 + k, :], S.W["w_branch"][l, zz, k * 128:(k + 1) * 128, :])
    wout = P.sb("wout", [128, 8, D], BF16)
    for k in range(8):
        P.dma("pool", wout[:, k, :], S.W["w_out"][l, k * 128:(k + 1) * 128, :])
    lng = P.sb("lng", [128, D], F32)
    lnb = P.sb("lnb", [128, D], F32)
    P.dma("sp", lng, S.W["ln_g"][l, 1:2, :].bc([128, D]))
    P.dma("sp", lnb, S.W["ln_b"][l, 1:2, :].bc([128, D]))
    gt = P.sb("gt", [128, D], F32)
    xb = [P.sb("xb%d" % i, [128, D], F32) for i in range(2)]
    y32 = [P.sb("y32_%d" % i, [128, 1536], F32) for i in range(2)]
    gts = [P.sb("gtl%d" % i, [128, 3 * D], BF16) for i in range(2)]
    yb = P.sb("yb", [128, 1536], BF16)
    yT = P.sb("yT", [128, 1536], BF16)
    mg = P.sb("mg", [128, D], F32)
    tmp = P.sb("tmp", [128, 512], F32)
    mgb = P.sb("mgb", [128, D], BF16)
    mT = P.sb("mT", [128, D], BF16)
    tmp2 = P.sb("tmp2", [128, D], F32)
    z = P.sb("z", [128, D], F32)
    xo = P.sb("xo", [128, D], F32)
    wk = ln_work(P)

    def loads(n, b, i):
        rows = slice(i * 128, (i + 1) * 128)
        P.dma("sp", xb[n % 2], S.XA[b][rows, :])
        P.dma("sp", y32[n % 2][:, 0:512], S.YA[b][rows, :])
        P.dma("sp", y32[n % 2][:, 512:1024], S.YR[b][rows, :])
        P.dma("sp", y32[n % 2][:, 1024:1536], S.YN[b][rows, :])
        P.dma("sp", gts[n % 2], S.GT[b][rows, :])

    for row in range(3):
        rt = row_tiles(row, tiles)
        if not rt:
            continue
        load_mod(S, row, [5], [gt])
        loads(0, *rt[0])
        for n, (b, i) in enumerate(rt):
            rows = slice(i * 128, (i + 1) * 128)
            if n + 1 < len(rt):
                loads(n + 1, *rt[n + 1])
            xt, yy, gg = xb[n % 2], y32[n % 2], gts[n % 2]
            P.copy("pool", yb, yy)
            transpose_bf(S, yb, yT, 12)
            for n2 in range(2):
                cs_ = slice(n2 * 512, (n2 + 1) * 512)
                for zz in range(3):
                    ps = S.bank()
                    for k in range(4):
                        P.mm(ps, yT[:, (zz * 4 + k) * 128:(zz * 4 + k + 1) * 128], wb[:, zz * 4 + k, cs_], start=(k == 0), stop=(k == 3))
                    gsl = gg[:, zz * D + n2 * 512: zz * D + (n2 + 1) * 512]
                    if zz == 0:
                        P.tt("dve", mg[:, cs_], ps, gsl, ALU.mult)
                    else:
                        P.tt("dve", tmp, ps, gsl, ALU.mult)
                        P.tt("pool", mg[:, cs_], mg[:, cs_], tmp, ALU.add)
            P.copy("act", mgb, mg)
            transpose_bf(S, mgb, mT, 8)
            for n2 in range(2):
                cs_ = slice(n2 * 512, (n2 + 1) * 512)
                po = S.bank()
                for k in range(8):
                    P.mm(po, mT[:, k * 128:(k + 1) * 128], wout[:, k, cs_], start=(k == 0), stop=(k == 7))
                P.tt("dve", tmp2[:, cs_], po, gt[:, cs_], ALU.mult)
            P.stt(z, xt, ALPHA, tmp2, ALU.mult, ALU.add)
            layernorm(S, z, lng, lnb, xo, wk)
            P.dma("sp", S.XB[b][rows, :], xo)
    P.phase_end()


def rwkv_phase(S, l, b, need_ctx):
    for _ in rwkv_prep(S, l, b):
        pass
    rwkv_scan(S, l, b)
    for _ in rwkv_readout(S, l, b, need_ctx):
        pass


def rwkv_prep(S, l, b, own_phase=True, dbuf=True):
    P = S.P
    W = S.W
    RS = S.RS
    if own_phase:
        P.phase_begin()
    mu0 = P.sb("mu0", [128, DRW], F32)
    mu1 = P.sb("mu1", [128, DRW], F32)
    P.dma("sp", mu0, W["rwkv_mu"][l, 0:1, :].bc([128, DRW]))
    P.dma("sp", mu1, W["rwkv_mu"][l, 1:2, :].bc([128, DRW]))
    w2 = P.sb("w2", [128, 512], F32)
    a2 = P.sb("a2", [128, 512], F32)
    g2 = P.sb("g2", [128, 512], F32)
    P.dma("sp", w2, W["rwkv_w2"][l].re("z r c -> (z r) c"))
    P.dma("sp", a2, W["rwkv_a2"][l].re("z r c -> (z r) c"))
    P.dma("sp", g2, W["rwkv_g2"][l])
    w0b = P.sb("w0b", [128, 2, 512], F32)
    a0b = P.sb("a0b", [128, 2, 512], F32)
    for z in range(2):
        P.dma("sp", w0b[:, z, :], W["rwkv_w0"][l, z:z + 1, :].bc([128, 512]))
        P.dma("sp", a0b[:, z, :], W["rwkv_a0"][l, z:z + 1, :].bc([128, 512]))
    kkb = P.sb("kkb", [128, 512], F32)
    kab = P.sb("kab", [128, 512], F32)
    P.dma("sp", kkb, W["rwkv_k_k"][l:l + 1, :].bc([128, 512]))
    P.dma("sp", kab, W["rwkv_k_a"][l:l + 1, :].bc([128, 512]))
    def mkset(q):
        d = Ctx()
        d.yc = P.sb("yc%d" % q, [128, DRW], F32)
        d.yp = P.sb("yp%d" % q, [128, DRW], F32)
        d.yn = P.sb("yn%d" % q, [128, DRW], F32)
        d.lo = P.sb("lo%d" % q, [128, 3, 128], F32)
        d.loT = P.sb("loT%d" % q, [128, 3, 128], F32)
        d.t1 = P.sb("t1%d" % q, [128, 512], F32)
        d.lw = [P.sb("lw%d_%d" % (z, q), [128, 512], F32) for z in range(2)]
        d.ic = [P.sb("ic%d_%d" % (z, q), [128, 512], F32) for z in range(2)]
        d.kd = [P.sb("kd%d_%d" % (z, q), [128, 512], F32) for z in range(2)]
        d.bz = [P.sb("bz%d_%d" % (z, q), [128, 512], F32) for z in range(2)]
        for n in ["gg", "kx", "sq", "kk", "aa"]:
            setattr(d, n, P.sb("%s%d" % (n, q), [128, 512], F32))
        d.ss8 = P.sb("ss8%d" % q, [128, 8], F32)
        return d

    sets = [mkset(0), mkset(1)] if dbuf else [mkset(0)] * 2
    Y = S.YRI[b]

    def loads(i):
        d = sets[i % 2]
        r0 = i * 128
        rows = slice(r0, r0 + 128)
        P.dma("sp", d.yc, Y[rows, :])
        if i == 0 or i == 16:
            P.memset("pool", d.yp, 0.0)
            P.dma("sp", d.yp[1:128, :], Y[r0:r0 + 127, :])
        else:
            P.dma("sp", d.yp, Y[r0 - 1:r0 + 127, :])
        if i == 15 or i == 17:
            P.memset("pool", d.yn, 0.0)
            P.dma("sp", d.yn[0:127, :], Y[r0 + 1:r0 + 128, :])
        else:
            P.dma("sp", d.yn, Y[r0 + 1:r0 + 129, :])

    loads(0)
    for i in range(NT):
        d = sets[i % 2]
        yc, yp, yn, lo, loT, t1, lw, ic, kd, bz = d.yc, d.yp, d.yn, d.lo, d.loT, d.t1, d.lw, d.ic, d.kd, d.bz
        gg, kx, sq, kk, aa, ss8 = d.gg, d.kx, d.sq, d.kk, d.aa, d.ss8
        r0 = i * 128
        rows = slice(r0, r0 + 128)
        if dbuf and i + 1 < NT:
            loads(i + 1)
        if not dbuf and i > 0:
            loads(i)
        P.tt("dve", yp, yp, yc, ALU.subtract)
        P.tt("dve", yn, yn, yc, ALU.subtract)
        P.tt("dve", yp, yp, mu0, ALU.mult)
        P.tt("dve", yn, yn, mu1, ALU.mult)
        P.tt("dve", yc, yc, yp, ALU.add)
        P.tt("dve", yc, yc, yn, ALU.add)
        r_, k_, v_ = yc[:, 0:512], yc[:, 512:1024], yc[:, 1024:1536]
        P.act(lo[:, 0, :], yc[:, 1536:1664], AF.Tanh)
        P.copy("pool", lo[:, 1, :], yc[:, 1664:1792])
        P.act(lo[:, 2, :], yc[:, 1792:1920], AF.Sigmoid)
        pb = S.bank()
        for q in range(3):
            P.tr(pb[:, q * 128:(q + 1) * 128], lo[:, q, :], S.identf)
        P.copy("act", loT.re("p a t -> p (a t)"), pb[:, 0:384])
        for z in range(2):
            ps = S.bank()
            P.mm(ps, loT[z * 64:(z + 1) * 64, 0, :], w2[z * 64:(z + 1) * 64, :])
            P.tt("dve", t1, ps, w0b[:, z, :], ALU.add)
            P.act(t1, t1, AF.Sigmoid)
            P.ts("pool", lw[z], t1, -float(np.exp(-0.5)), op0=ALU.mult)
            P.dma("sp", RS["LW%d" % z][rows, :], lw[z])
            ps = S.bank()
            P.mm(ps, loT[z * 64:(z + 1) * 64, 1, :], a2[z * 64:(z + 1) * 64, :])
            P.tt("dve", ic[z], ps, a0b[:, z, :], ALU.add)
            P.act(ic[z], ic[z], AF.Sigmoid)
        ps = S.bank()
        P.mm(ps, loT[:, 2, :], g2)
        P.copy("act", gg, ps)
        P.dma("sp", RS["G"][rows, :], gg)
        P.tt("dve", kx, k_, kkb, ALU.mult)
        P.tt("pool", sq, kx, kx, ALU.mult)
        P.reduce(ss8, sq.re("p (h d) -> p h d", d=64), ALU.add)
        P.act(ss8, ss8, AF.Sqrt, bias=S.eps[:, 3:4], scale=1.0)
        P.recip(ss8, ss8)
        P.tt("dve", kk.re("p (h d) -> p h d", d=64), kx.re("p (h d) -> p h d", d=64), bc3(ss8, [128, 8, 64]), ALU.mult)
        P.ts("pool", aa, kk, -1.0, op0=ALU.mult)
        P.dma("sp", RS["A"][rows, :], aa)
        for z in range(2):
            P.stt(t1, ic[z], -1.0, kab, ALU.add, ALU.mult)
            P.stt(kd[z], t1, 1.0, k_, ALU.add, ALU.mult)
            P.dma("sp", RS["KD%d" % z][rows, :], kd[z])
            P.tt("pool", bz[z], kk, ic[z], ALU.mult)
            P.dma("sp", RS["B%d" % z][rows, :], bz[z])
        P.dma("sp", RS["R"][rows, :], r_)
        P.dma("sp", RS["K"][rows, :], k_)
        P.dma("sp", RS["V"][rows, :], v_)
        yield
    if own_phase:
        P.phase_end()


def rwkv_scan(S, l, b, nsteps=NT):
    P = S.P
    RS = S.RS
    P.phase_begin()
    ident, SL, SU, IL, IU, ones = [S.cf[:, q, :] for q in range(6)]
    order = [[16, 17] + list(range(16)), [17, 16] + list(range(15, -1, -1))]
    mN = [SL, SU]
    mTs = [SU, SL]
    mTi = [IU, IL]
    mC = [IU, IL]

    def mk(z):
        d = Ctx()
        for n in ["lw", "kd", "bb", "aa", "rr", "vv", "cum", "d1", "d2", "e1", "e2", "At", "Bt", "Kt", "Rt", "Bg", "Kg",
                  "Zs", "Us", "Os"]:
            setattr(d, n, P.sb("%s%d" % (n, z), [128, 512], F32))
        for n in ["AtT", "BtT", "KtT", "RtT"]:
            setattr(d, n, P.sb("%s%d" % (n, z), [64, 8, 128], F32))
        for n in ["Aak", "Arb", "Ark"]:
            setattr(d, n, [P.sb("%s%d_%d" % (n, z, hg), [128, 4, 128], F32) for hg in range(2)])
        for n in ["M", "Mt", "Pt"]:
            setattr(d, n, [[P.sb("%s%d_%d_%d" % (n, z, hg, q), [128, 4, 128], F32) for q in range(2)] for hg in range(2)])
        d.gL = P.sb("gL%d" % z, [64, 8], F32)
        d.H = P.sb("H%d" % z, [64, 8, 64], F32)
        P.memset("pool", d.H, 0.0)
        return d

    Dz = [mk(0), mk(1)]

    def mb(m):
        return T(m.ap.unsqueeze(1).to_broadcast([128, 4, 128]), m.tok)

    def b4(ps):
        return ps.re("p (h t) -> p h t", h=4)

    def opd(XT, h):
        return XT[:, h, :]

    def pre(z, c):
        d = Dz[z]
        rows = slice(c * 128, (c + 1) * 128)
        P.dma("sp", d.lw, RS["LW%d" % z][rows, :])
        P.dma("sp", d.kd, RS["KD%d" % z][rows, :])
        P.dma("sp", d.bb, RS["B%d" % z][rows, :])
        P.dma("sp", d.aa, RS["A"][rows, :])
        P.dma("sp", d.rr, RS["R"][rows, :])
        P.dma("sp", d.vv, RS["V"][rows, :])
        pc = S.bank()
        P.mm(pc, mC[z], d.lw)
        pl = S.bank()
        P.mm(pl, ones, d.lw)
        pg = S.bank()
        for h in range(8):
            P.mm(pg[0:64, 2 * h:2 * h + 2], d.lw[:, h * 64:(h + 1) * 64], ones[:, 0:2])
        P.act(d.gL, pg[0:64, 0:16].re("p (h two) -> p h two", two=2)[:, :, 0], AF.Exp)
        P.copy("act", d.cum, pc)
        P.tt("pool", d.d1, d.cum, d.lw, ALU.subtract)
        P.tt("dve", d.d2, pl, d.cum, ALU.subtract)
        P.act(d.e1, d.cum, AF.Exp)
        P.act(d.e2, d.cum, AF.Exp, scale=-1.0)
        P.act(d.d1, d.d1, AF.Exp)
        P.act(d.d2, d.d2, AF.Exp)
        P.tt("dve", d.At, d.aa, d.d1, ALU.mult)
        P.tt("pool", d.Bt, d.bb, d.e2, ALU.mult)
        P.tt("dve", d.Kt, d.kd, d.e2, ALU.mult)
        P.tt("pool", d.Rt, d.rr, d.e1, ALU.mult)
        P.tt("dve", d.Bg, d.bb, d.d2, ALU.mult)
        P.tt("pool", d.Kg, d.kd, d.d2, ALU.mult)
        qi = 0
        for (src, dst) in [(d.At, d.AtT), (d.Bt, d.BtT), (d.Kt, d.KtT), (d.Rt, d.RtT)]:
            for hg in range(2):
                pb = S.bank()
                for hh in range(4):
                    h = hg * 4 + hh
                    P.tr(pb[0:64, hh * 128:(hh + 1) * 128], src[:, h * 64:(h + 1) * 64], ident)
                P.copy("act" if qi % 2 == 0 else "dve", dst[:, hg * 4:(hg + 1) * 4, :].re("p a t -> p (a t)"), pb[0:64, :])
                qi += 1
        for hg in range(2):
            M, Mt, Pt = d.M[hg], d.Mt[hg], d.Pt[hg]
            specs = [(d.AtT, d.BtT, mN[z], M[0]), (d.BtT, d.AtT, mTs[z], Mt[0]), (d.KtT, d.AtT, mTs[z], d.Aak[hg]),
                     (d.BtT, d.RtT, mTi[z], d.Arb[hg]), (d.KtT, d.RtT, mTi[z], d.Ark[hg])]
            for (LT, RT_, msk, dst) in specs:
                ps = S.bank()
                for hh in range(4):
                    h = hg * 4 + hh
                    P.mm(ps[:, hh * 128:(hh + 1) * 128], opd(LT, h), opd(RT_, h))
                P.tt("dve", dst, b4(ps), mb(msk), ALU.mult)
            P.tt("pool", Pt[0], Mt[0], mb(ident), ALU.add)
            cur = 0
            for j in range(1, 7):
                nxt = 1 - cur
                if j < 6:
                    ps1 = S.bank()
                    for hh in range(4):
                        P.mm(ps1[:, hh * 128:(hh + 1) * 128], M[cur][:, hh, :], Mt[cur][:, hh, :])
                    P.copy("act", Mt[nxt], b4(ps1))
                ps2 = S.bank()
                for hh in range(4):
                    P.mm(ps2[:, hh * 128:(hh + 1) * 128], Mt[cur][:, hh, :], M[cur][:, hh, :])
                P.copy("dve", M[nxt], b4(ps2))
                ps3 = S.bank()
                for hh in range(4):
                    P.mm(ps3[:, hh * 128:(hh + 1) * 128], M[nxt][:, hh, :], Pt[cur][:, hh, :])
                P.tt("dve", Pt[nxt], b4(ps3), Pt[cur], ALU.add)
                cur = nxt
            assert cur == 0

    def seq_stages(z, c):
        d = Dz[z]
        rows = slice(c * 128, (c + 1) * 128)
        H = d.H

        def st_z():
            pz = S.bank()
            for h in range(8):
                hc = slice(h * 64, (h + 1) * 64)
                P.mm(pz[:, hc], d.AtT[:, h, :], H[:, h, :], start=True, stop=False)
                P.mm(pz[:, hc], d.Aak[h // 4][:, h % 4, :], d.vv[:, hc], start=False, stop=True)
            P.copy("act", d.Zs, pz)

        def st_u():
            pu = S.bank()
            for h in range(8):
                hc = slice(h * 64, (h + 1) * 64)
                P.mm(pu[:, hc], d.Pt[h // 4][0][:, h % 4, :], d.Zs[:, hc])
            P.copy("dve", d.Us, pu)

        def st_o():
            po = S.bank()
            for h in range(8):
                hc = slice(h * 64, (h + 1) * 64)
                P.mm(po[:, hc], d.RtT[:, h, :], H[:, h, :], start=True, stop=False)
                P.mm(po[:, hc], d.Arb[h // 4][:, h % 4, :], d.Us[:, hc], start=False, stop=False)
                P.mm(po[:, hc], d.Ark[h // 4][:, h % 4, :], d.vv[:, hc], start=False, stop=True)
            P.copy("act", d.Os, po)
            P.dma("sp", RS["O%d" % z][rows, :], d.Os)

        def st_h():
            ph = S.bank()
            for h in range(8):
                hc = slice(h * 64, (h + 1) * 64)
                P.mm(ph[0:64, hc], d.Bg[:, hc], d.Us[:, hc], start=True, stop=False)
                P.mm(ph[0:64, hc], d.Kg[:, hc], d.vv[:, hc], start=False, stop=True)
            P.tt("dve", H, H, bc3(d.gL, [64, 8, 64]), ALU.mult)
            P.tt("dve", H, H, ph[0:64, :].re("p (h v) -> p h v", v=64), ALU.add)

        return [st_z, st_u, st_o, st_h]

    for step in range(nsteps):
        for z in range(2):
            pre(z, order[z][step])
        stg = [seq_stages(z, order[z][step]) for z in range(2)]
        for k in range(4):
            for z in range(2):
                stg[z][k]()
    P.phase_end()


def rwkv_readout(S, l, b, need_ctx, own_phase=True):
    P = S.P
    RS = S.RS
    W = S.W
    if own_phase:
        P.phase_begin()
    gng = P.sb("gng", [128, 512], F32)
    gnb = P.sb("gnb", [128, 512], F32)
    rkb = P.sb("rkb", [128, 512], F32)
    P.dma("sp", gng, W["rwkv_gn_g"][l:l + 1, :].bc([128, 512]))
    P.dma("sp", gnb, W["rwkv_gn_b"][l:l + 1, :].bc([128, 512]))
    P.dma("sp", rkb, W["rwkv_r_k"][l:l + 1, :].bc([128, 512]))
    names = ["O0", "O1", "R", "K", "V", "G"]
    bufs = [{n: P.sb("ro_%s%d" % (n, q), [128, 512], F32) for n in names} for q in range(2)]
    o = P.sb("o", [128, 512], F32)
    sq = P.sb("sq", [128, 512], F32)
    s1 = P.sb("s1", [128, 8], F32)
    s2 = P.sb("s2", [128, 8], F32)
    m2 = P.sb("m2", [128, 8], F32)
    rk = P.sb("rk", [128, 512], F32)
    sb8 = P.sb("sb8", [128, 8], F32)
    yo = [P.sb("yo%d" % q, [128, 512], F32) for q in range(2)]
    tiles = list(range(16)) + ([16, 17] if need_ctx else [])

    def v3(t):
        return t.re("p (h d) -> p h d", d=64)

    def loads(n, i):
        for nm in names:
            P.dma("sp", bufs[n % 2][nm], RS[nm][i * 128:(i + 1) * 128, :])

    loads(0, tiles[0])
    for n, i in enumerate(tiles):
        if n + 1 < len(tiles):
            loads(n + 1, tiles[n + 1])
        B = bufs[n % 2]
        y = yo[n % 2]
        P.tt("dve", o, B["O0"], B["O1"], ALU.add)
        P.reduce(s1, v3(o), ALU.add)
        P.tt("pool", sq, o, o, ALU.mult)
        P.reduce(s2, v3(sq), ALU.add)
        P.ts("dve", s1, s1, 1.0 / 64, op0=ALU.mult)
        P.tt("dve", m2, s1, s1, ALU.mult)
        P.stt(s2, s2, 1.0 / 64, m2, ALU.mult, ALU.subtract)
        P.act(s2, s2, AF.Sqrt, bias=S.eps[:, 2:3], scale=1.0)
        P.recip(s2, s2)
        P.tt("dve", v3(o), v3(o), bc3(s1, [128, 8, 64]), ALU.subtract)
        P.tt("dve", v3(o), v3(o), bc3(s2, [128, 8, 64]), ALU.mult)
        P.tt("dve", o, o, gng, ALU.mult)
        P.tt("pool", o, o, gnb, ALU.add)
        P.tt("dve", rk, B["R"], B["K"], ALU.mult)
        P.tt("pool", rk, rk, rkb, ALU.mult)
        P.reduce(sb8, v3(rk), ALU.add)
        P.tt("dve", v3(rk), v3(B["V"]), bc3(sb8, [128, 8, 64]), ALU.mult)
        P.tt("dve", o, o, rk, ALU.add)
        P.tt("dve", y, o, B["G"], ALU.mult)
        P.dma("sp", S.YR[b][i * 128:(i + 1) * 128, :], y)
        yield
    if own_phase:
        P.phase_end()
```

```python
import numpy as np
import concourse.bass as bass
import concourse.mybir as mybir
from concourse.bass_utils import run_bass_kernel_spmd

F32 = mybir.dt.float32
BF16 = mybir.dt.bfloat16
AF = mybir.ActivationFunctionType
ALU = mybir.AluOpType
AX = mybir.AxisListType


class Tok:
    __slots__ = ("w", "r", "dsem", "dcount", "name")

    def __init__(self, name=""):
        self.w = None
        self.r = {}
        self.dsem = None
        self.dcount = 0
        self.name = name


class T:
    __slots__ = ("ap", "tok")

    def __init__(self, ap, tok):
        self.ap = ap
        self.tok = tok

    def __getitem__(self, idx):
        return T(self.ap[idx], self.tok)

    def re(self, pat, **kw):
        return T(self.ap.rearrange(pat, **kw), self.tok)

    def bc(self, shape):
        return T(self.ap.to_broadcast(shape), self.tok)

    def bitcast(self, dt):
        return T(self.ap.bitcast(dt), self.tok)

    def wt(self, tok):
        return T(self.ap, tok)

    def r(self):
        return T(self.ap.bitcast(mybir.dt.float32r), self.tok)

    @property
    def shape(self):
        return self.ap.shape


class Prog:
    def __init__(self, nc):
        self.nc = nc
        self.eng = {"pe": nc.tensor, "dve": nc.vector, "act": nc.scalar, "pool": nc.gpsimd, "sp": nc.sync}
        self.sem = {k: nc.alloc_semaphore("sem_" + k) for k in self.eng}
        self.cnt = {k: 0 for k in self.eng}
        self.seen = {k: {} for k in self.eng}
        self.dsems = []
        self.ninst = 0
        self.dpool = []
        self.phase_stack = None

    def dram(self, name, shape, dt, kind="Internal"):
        h = self.nc.dram_tensor(name, list(shape), dt, kind=kind)
        return T(h.ap(), Tok(name))

    def sb(self, name, shape, dt):
        if self.phase_stack is None:
            h = self.nc.alloc_sbuf_tensor(name, list(shape), dt)
            return T(h.ap(), Tok(name))
        self.uid = getattr(self, "uid", 0) + 1
        g = self.nc.sbuf_tensor("%s_%d" % (name, self.uid), list(shape), dt)
        h = g.__enter__()
        t = T(h.ap(), Tok(name))
        self.phase_stack.append((g, t.tok))
        return t

    def phase_begin(self):
        assert self.phase_stack is None
        self.phase_stack = []

    def phase_end(self):
        self.barrier()
        for g, tok in reversed(self.phase_stack):
            if tok.dsem is not None:
                self.dpool.append((tok.dsem, tok.dcount))
                self.dsems.remove(tok)
                tok.dsem = None
            g.__exit__(None, None, None)
        self.phase_stack = None

    def ps(self, name, shape, dt=F32):
        h = self.nc.alloc_psum_tensor(name, list(shape), dt)
        return T(h.ap(), Tok(name))

    def _wait(self, eng, events):
        best = {}
        for ev in events:
            if ev is None:
                continue
            s, v = ev
            if s.num not in best or best[s.num][1] < v:
                best[s.num] = (s, v)
        seen = self.seen[eng]
        e = self.eng[eng]
        for num, (s, v) in best.items():
            if eng == "pe" and s is self.sem["pe"]:
                continue
            if seen.get(num, 0) < v:
                e.wait_ge(s, v)
                seen[num] = v
                self.ninst += 1

    def _deps(self, w, r):
        evs = []
        for t in r:
            evs.append(t.tok.w)
        for t in w:
            evs.append(t.tok.w)
            evs.extend(t.tok.r.values())
        return evs

    def op(self, eng, fn, w, r):
        self._wait(eng, self._deps(w, r))
        inst = fn(self.eng[eng])
        self.cnt[eng] += 1
        self.ninst += 1
        s = self.sem[eng]
        inst.then_inc(s, 1)
        ev = (s, self.cnt[eng])
        for t in r:
            t.tok.r[s.num] = ev
        for t in w:
            t.tok.w = ev
            t.tok.r = {}
        return ev

    def dma(self, eng, out, in_, **kw):
        tok = out.tok
        evs = [in_.tok.w] + list(tok.r.values())
        if tok.w is not None and tok.w[0] is not tok.dsem:
            evs.append(tok.w)
        self._wait(eng, evs)
        if tok.dsem is None:
            if self.dpool:
                tok.dsem, tok.dcount = self.dpool.pop()
            else:
                tok.dsem = self.nc.alloc_semaphore("d%d_%s" % (len(self.dsems), tok.name))
            self.dsems.append(tok)
        inst = self.eng[eng].dma_start(out=out.ap, in_=in_.ap, **kw)
        inst.then_inc(tok.dsem, 16)
        self.ninst += 1
        tok.dcount += 16
        ev = (tok.dsem, tok.dcount)
        in_.tok.r[tok.dsem.num] = ev
        tok.w = ev
        tok.r = {}
        return ev

    def barrier(self):
        evs = [(self.sem[k], self.cnt[k]) for k in self.eng if self.cnt[k] > 0]
        evs += [(t.dsem, t.dcount) for t in self.dsems]
        for k in self.eng:
            self._wait(k, evs)

    def finish(self):
        self.barrier()

    def mm(self, out, lhsT, rhs, start=True, stop=True, r32=False):
        la, ra = lhsT.ap, rhs.ap
        if r32:
            la, ra = la.bitcast(mybir.dt.float32r), ra.bitcast(mybir.dt.float32r)
        return self.op("pe", lambda e: e.matmul(out.ap, la, ra, start=start, stop=stop),
                       [out], [lhsT, rhs])

    def tr(self, out, in_, ident):
        return self.op("pe", lambda e: e.transpose(out.ap, in_.ap, ident.ap), [out], [in_, ident])

    def act(self, out, in_, func, bias=None, scale=None, accum=None, eng="act"):
        kw = {}
        r = [in_]
        w = [out]
        if bias is not None:
            if isinstance(bias, T):
                kw["bias"] = bias.ap
                r.append(bias)
            else:
                kw["bias"] = bias
        if scale is not None:
            if isinstance(scale, T):
                kw["scale"] = scale.ap
                r.append(scale)
            else:
                kw["scale"] = scale
        if accum is not None:
            kw["accum_out"] = accum.ap
            w.append(accum)
        return self.op("act", lambda e: e.activation(out.ap, in_.ap, func, **kw), w, r)

    def copy(self, eng, out, in_):
        if eng == "act":
            return self.op("act", lambda e: e.copy(out.ap, in_.ap), [out], [in_])
        return self.op(eng, lambda e: e.tensor_copy(out.ap, in_.ap), [out], [in_])

    def tt(self, eng, out, a, b, op):
        return self.op(eng, lambda e: e.tensor_tensor(out.ap, a.ap, b.ap, op), [out], [a, b])

    def ts(self, eng, out, a, s1, s2=None, op0=ALU.mult, op1=None, accum=None):
        r = [a]
        w = [out]
        v1 = s1
        v2 = s2
        if isinstance(s1, T):
            r.append(s1)
            v1 = s1.ap
        if isinstance(s2, T):
            r.append(s2)
            v2 = s2.ap
        kw = {}
        if op1 is not None:
            kw["op1"] = op1
        if accum is not None:
            kw["accum_out"] = accum.ap
            w.append(accum)
        return self.op(eng, lambda e: e.tensor_scalar(out.ap, a.ap, v1, v2, op0, **kw), w, r)

    def stt(self, out, in0, scalar, in1, op0, op1, eng="dve"):
        r = [in0, in1]
        v = scalar
        if isinstance(scalar, T):
            r.append(scalar)
            v = scalar.ap
        return self.op(eng, lambda e: e.scalar_tensor_tensor(out.ap, in0.ap, v, in1.ap, op0, op1), [out], r)

    def memset(self, eng, out, val):
        return self.op(eng, lambda e: e.memset(out.ap, val), [out], [])

    def reduce(self, out, in_, op, axis=None, eng="dve"):
        ax = axis if axis is not None else AX.X
        return self.op(eng, lambda e: e.tensor_reduce(out.ap, in_.ap, ax, op), [out], [in_])

    def recip(self, out, in_):
        return self.op("dve", lambda e: e.reciprocal(out.ap, in_.ap), [out], [in_])


NB = 2
TL = 2048
TC = 256
TT = TL + TC
NT = TT // 128
D = 1024
DFF = 2816
DIN = 7296
DEPTH = 2
ALPHA = (2 * DEPTH) ** 0.25
LN_EPS = 1e-5
RMS_EPS = 1e-6
GN_EPS = 64e-5
MASKV = -30000.0
C_AQ, C_AK, C_AV, C_RW, C_NQ, C_NK, C_NV, C_G = 0, 512, 640, 768, 2688, 3200, 3712, 4224


NB = 2
TL = 2048
TC = 256
TT = TL + TC
NT = TT // 128
D = 1024
DFF = 2816
DIN = 7296
DEPTH = 2
ALPHA = (2 * DEPTH) ** 0.25
LN_EPS = 1e-5
RMS_EPS = 1e-6
GN_EPS = 64e-5
MASKV = -30000.0
C_AQ, C_AK, C_AV, C_RW, C_NQ, C_NK, C_NV, C_G = 0, 512, 640, 768, 2688, 3200, 3712, 4224
DRW = 1920

WNAMES = ["w_ada", "b_ada", "ffn1_wi", "ffn1_wo", "ffn2_wi", "ffn2_wo", "ln_g", "ln_b", "w_in",
          "att_q_gain", "att_k_gain", "rwkv_mu", "rwkv_w0", "rwkv_w2", "rwkv_a0", "rwkv_a2", "rwkv_g2",
          "rwkv_k_k", "rwkv_k_a", "rwkv_r_k", "rwkv_gn_g", "rwkv_gn_b", "w_branch", "w_out"]
WSHAPES = {
    "w_ada": [DEPTH, D, 9 * D], "b_ada": [DEPTH, 9 * D], "ffn1_wi": [DEPTH, D, 2 * DFF], "ffn1_wo": [DEPTH, DFF, D],
    "ffn2_wi": [DEPTH, D, 2 * DFF], "ffn2_wo": [DEPTH, DFF, D], "ln_g": [DEPTH, 3, D], "ln_b": [DEPTH, 3, D],
    "w_in": [DEPTH, D, DIN], "att_q_gain": [DEPTH, 64], "att_k_gain": [DEPTH, 64], "rwkv_mu": [DEPTH, 2, DRW],
    "rwkv_w0": [DEPTH, 2, 512], "rwkv_w2": [DEPTH, 2, 64, 512], "rwkv_a0": [DEPTH, 2, 512],
    "rwkv_a2": [DEPTH, 2, 64, 512], "rwkv_g2": [DEPTH, 128, 512], "rwkv_k_k": [DEPTH, 512], "rwkv_k_a": [DEPTH, 512],
    "rwkv_r_k": [DEPTH, 512], "rwkv_gn_g": [DEPTH, 512], "rwkv_gn_b": [DEPTH, 512],
    "w_branch": [DEPTH, 3, 512, D], "w_out": [DEPTH, D, D],
}


def na_tiles(i):
    if 2 <= i <= 13:
        return [(i - 2 + p, p) for p in range(5)]
    if i == 0:
        return [(j, 5 + j) for j in range(4)]
    if i == 1:
        return [(j, 9 + j) for j in range(4)]
    if i == 14:
        return [(12 + j, 13 + j) for j in range(4)]
    return [(12 + j, 17 + j) for j in range(4)]


NPAT = 21


class Ctx:
    pass


def build(dbg=False, stop=None, nlayers=DEPTH):
    nc = bass.Bass("TRN2", target_bir_lowering=False)
    P = Prog(nc)
    S = Ctx()
    S.P = P
    S.dbg = dbg
    import os
    S.nsteps = int(os.environ.get('RWKV_STEPS', NT))
    okind = "ExternalOutput" if dbg else "Internal"
    S.x_in = P.dram("x", [NB, TL, D], F32, kind="ExternalInput")
    S.ctx_in = P.dram("ctx", [NB, TC, D], F32, kind="ExternalInput")
    S.c3_in = P.dram("c3", [3, D], F32, kind="ExternalInput")
    S.W = {n: P.dram(n, WSHAPES[n], F32, kind="ExternalInput") for n in WNAMES}
    S.nab = P.dram("nab", [DEPTH, NPAT, 128, 8, 128], F32, kind="ExternalInput")
    S.cst = P.dram("cst", [128, 6, 128], F32, kind="ExternalInput")
    S.rope = P.dram("rope", [TL, 64], F32, kind="ExternalInput")
    S.out = P.dram("out", [NB, TL, D], F32, kind="ExternalOutput")
    S.XA = [P.dram("XA%d" % b, [TT, D], F32, kind=okind) for b in range(NB)]
    S.XB = [P.dram("XB%d" % b, [TT, D], F32, kind=okind) for b in range(NB)]
    S.mod = P.dram("mod", [3, 9 * D], F32, kind=okind)
    S.QA = [P.dram("QA%d" % b, [TT, 512], BF16) for b in range(NB)]
    S.KA = [P.dram("KA%d" % b, [TT, 128], BF16) for b in range(NB)]
    S.VA = [P.dram("VA%d" % b, [TT, 128], BF16) for b in range(NB)]
    S.QN = [P.dram("QN%d" % b, [TT, 512], BF16) for b in range(NB)]
    S.KN = [P.dram("KN%d" % b, [TT, 512], BF16) for b in range(NB)]
    S.VN = [P.dram("VN%d" % b, [TT, 512], BF16) for b in range(NB)]
    S.GT = [P.dram("GT%d" % b, [TT, 3 * D], BF16) for b in range(NB)]
    S.YRI = [P.dram("YRI%d" % b, [TT, DRW], F32, kind=okind) for b in range(NB)]
    S.YA = [P.dram("YA%d" % b, [TT, 512], F32, kind=okind) for b in range(NB)]
    S.YR = [P.dram("YR%d" % b, [TT, 512], F32, kind=okind) for b in range(NB)]
    S.YN = [P.dram("YN%d" % b, [TT, 512], F32, kind=okind) for b in range(NB)]
    S.RS = {n: P.dram("RS_" + n, [TT, 512], F32, kind=okind) for n in
            ["R", "K", "V", "G", "A", "LW0", "LW1", "KD0", "KD1", "B0", "B1", "O0", "O1"]}
    S.cf = P.sb("cst_f", [128, 6, 128], F32)
    P.dma("sp", S.cf, S.cst)
    S.identb = P.sb("identb", [128, 128], BF16)
    P.copy("dve", S.identb, S.cf[:, 0, :])
    S.identf = S.cf[:, 0, :]
    S.banks = [P.ps("bank%d" % i, [128, 512], F32) for i in range(8)]
    S.bank_i = 0

    def bank():
        b = S.banks[S.bank_i % 8]
        S.bank_i += 1
        return b
    S.bank = bank
    S.eps = P.sb("eps", [128, 4], F32)
    for i, v in enumerate([LN_EPS, RMS_EPS, GN_EPS, 1e-12]):
        P.memset("dve", S.eps[:, i:i + 1], v)

    xsrc = [None] * NB
    for l in range(nlayers):
        last = l == DEPTH - 1
        mod_phase(S, l)
        if stop == ("mod", l):
            break
        src = (lambda b, i: (S.x_in[b, i * 128:(i + 1) * 128, :] if i < 16 else S.ctx_in[b, (i - 16) * 128:(i - 15) * 128, :])) \
            if l == 0 else (lambda b, i: S.XB[b][i * 128:(i + 1) * 128, :])
        dst = lambda b, i: S.XA[b][i * 128:(i + 1) * 128, :]
        ffn_phase(S, l, 0, src, dst, list(range(NT)))
        if stop == ("ffn1", l):
            break
        mixpre_phase(S, l)
        if stop == ("mixpre", l):
            break
        if stop == ("att", l):
            for b in range(NB):
                att_phase(S, l, b, need_ctx=not last)
                na_phase(S, l, b, need_ctx=not last)
            break
        if stop == ("rwkvscan", l):
            for _ in rwkv_prep(S, l, 0):
                pass
            rwkv_scan(S, l, 0, nsteps=S.nsteps)
            break
        for b in range(NB):
            att_phase(S, l, b, need_ctx=not last)
            na_phase(S, l, b, need_ctx=not last)
        for b in range(NB):
            rwkv_phase(S, l, b, need_ctx=not last)
        if stop == ("rwkv", l):
            break
        merge_phase(S, l, list(range(NT)) if not last else list(range(16)))
        if stop == ("merge", l):
            break
        src = lambda b, i: S.XB[b][i * 128:(i + 1) * 128, :]
        if last:
            dst = lambda b, i: S.out[b, i * 128:(i + 1) * 128, :]
            ffn_phase(S, l, 1, src, dst, list(range(16)))
        else:
            dst = lambda b, i: S.XA[b][i * 128:(i + 1) * 128, :]
            ffn_phase(S, l, 1, src, dst, list(range(NT)))
            S.XA, S.XB = S.XB, S.XA
    P.finish()
    S.nc = nc
    return S


def bc3(t, shape):
    return T(t.ap.unsqueeze(2).to_broadcast(shape), t.tok)


def transpose_bf(S, src, dst, nch, eng_cycle=("act", "dve")):
    P = S.P
    gi = 0
    for g0 in range(0, nch, 8):
        n = min(8, nch - g0)
        pb = S.bank().bitcast(BF16)
        for k in range(n):
            P.tr(pb[:, k * 128:(k + 1) * 128], src[:, (g0 + k) * 128:(g0 + k + 1) * 128], S.identb)
        P.copy(eng_cycle[gi % len(eng_cycle)], dst[:, g0 * 128:(g0 + n) * 128], pb[:, 0:n * 128])
        gi += 1


def layernorm(S, z, g_bc, b_bc, out, wk):
    P = S.P
    stats, mv, rstd, xn = wk
    for j in range(2):
        P.op("dve", lambda e: e.bn_stats(stats[:, j, :].ap, z[:, j * 512:(j + 1) * 512].ap), [stats], [z])
    P.op("dve", lambda e: e.bn_aggr(mv.ap, stats.re("p a b -> p (a b)").ap), [mv], [stats])
    P.act(rstd, mv[:, 1:2], AF.Sqrt, bias=S.eps[:, 0:1], scale=1.0)
    P.recip(rstd, rstd)
    P.ts("dve", xn, z, mv[:, 0:1], rstd, op0=ALU.subtract, op1=ALU.mult)
    P.tt("dve", xn, xn, g_bc, ALU.mult)
    P.tt("dve", out, xn, b_bc, ALU.add)


def ln_work(P):
    return (P.sb("ln_stats", [128, 2, 6], F32), P.sb("ln_mv", [128, 2], F32),
            P.sb("ln_rstd", [128, 1], F32), P.sb("ln_xn", [128, D], F32))


def row_tiles(row, tiles):
    if row < 2:
        return [(row, i) for i in tiles if i < 16]
    return [(b, i) for b in range(NB) for i in tiles if i >= 16]


def load_mod(S, row, slots, dsts):
    for s, d in zip(slots, dsts):
        S.P.dma("sp", d, S.mod[row:row + 1, s * D:(s + 1) * D].bc([128, D]))


def mod_phase(S, l):
    P = S.P
    P.phase_begin()
    c3 = P.sb("c3", [3, D], F32)
    P.dma("sp", c3, S.c3_in)
    sc = P.sb("sc", [3, D], F32)
    P.act(sc, c3, AF.Silu)
    scT = P.sb("scT", [128, 8, 3], F32)
    pb = S.bank()
    for k in range(8):
        P.tr(pb[:, k * 4:k * 4 + 3], sc[:, k * 128:(k + 1) * 128], S.identf[0:3, 0:3])
    P.copy("dve", scT, pb[:, 0:32].re("p (k c) -> p k c", c=4)[:, :, 0:3])
    ba = P.sb("ba", [3, 9 * D], F32)
    P.dma("sp", ba, S.W["b_ada"][l:l + 1, :].bc([3, 9 * D]))
    m_sb = P.sb("m_sb", [3, 9 * D], F32)
    wch = [P.sb("wch%d" % i, [128, 8, 512], F32) for i in range(2)]
    wsrc = S.W["w_ada"][l].re("(k p) n -> p k n", p=128)
    for j in range(18):
        w = wch[j % 2]
        P.dma("sp", w, wsrc[:, :, j * 512:(j + 1) * 512])
        pb = S.bank()
        for k in range(8):
            P.mm(pb[0:3, :], scT[:, k, :], w[:, k, :], start=(k == 0), stop=(k == 7))
        P.tt("dve", m_sb[:, j * 512:(j + 1) * 512], pb[0:3, :], ba[:, j * 512:(j + 1) * 512], ALU.add)
    P.dma("sp", S.mod, m_sb)
    P.phase_end()


def ffn_phase(S, l, f, src, dst, tiles):
    P = S.P
    P.phase_begin()
    wi_d = S.W["ffn%d_wi" % (f + 1)]
    wo_d = S.W["ffn%d_wo" % (f + 1)]
    wi = P.sb("wi", [128, 8, 2 * DFF], BF16)
    wo = P.sb("wo", [128, 22, D], BF16)
    for k in range(8):
        P.dma("pool", wi[:, k, :], wi_d[l, k * 128:(k + 1) * 128, :])
    for k in range(22):
        P.dma("pool", wo[:, k, :], wo_d[l, k * 128:(k + 1) * 128, :])
    lni = 0 if f == 0 else 2
    lng = P.sb("lng", [128, D], F32)
    lnb = P.sb("lnb", [128, D], F32)
    P.dma("sp", lng, S.W["ln_g"][l, lni:lni + 1, :].bc([128, D]))
    P.dma("sp", lnb, S.W["ln_b"][l, lni:lni + 1, :].bc([128, D]))
    s0 = 0 if f == 0 else 6
    sh = P.sb("sh", [128, D], F32)
    sc = P.sb("sc", [128, D], F32)
    gt = P.sb("gt", [128, D], F32)
    xb = [P.sb("xb%d" % i, [128, D], F32) for i in range(2)]
    u32 = P.sb("u32", [128, D], F32)
    ub = P.sb("ub", [128, D], BF16)
    uT = P.sb("uT", [128, D], BF16)
    sg = [P.sb("sg%d" % i, [128, 512], F32) for i in range(2)]
    act = P.sb("act", [128, DFF], BF16)
    actT = P.sb("actT", [128, DFF], BF16)
    tmp = P.sb("tmp", [128, D], F32)
    z = P.sb("z", [128, D], F32)
    xo = P.sb("xo", [128, D], F32)
    wk = ln_work(P)
    for row in range(3):
        rt = row_tiles(row, tiles)
        if not rt:
            continue
        load_mod(S, row, [s0, s0 + 1, s0 + 2], [sh, sc, gt])
        P.ts("pool", sc, sc, 1.0, op0=ALU.add)
        P.ts("pool", gt, gt, 0.5, op0=ALU.mult)
        P.dma("sp", xb[0], src(*rt[0]))

        def front(n):
            P.tt("dve", u32, xb[n % 2], sc, ALU.mult)
            P.tt("dve", ub, u32, sh, ALU.add)
            transpose_bf(S, ub, uT, 8)

        front(0)
        for n, (b, i) in enumerate(rt):
            xt = xb[n % 2]
            if n + 1 < len(rt):
                P.dma("sp", xb[(n + 1) % 2], src(*rt[n + 1]))
            for ci, c0 in enumerate(range(0, DFF, 512)):
                w = min(512, DFF - c0)
                pg = S.bank()
                pu = S.bank()
                for k in range(8):
                    P.mm(pg[:, 0:w], uT[:, k * 128:(k + 1) * 128], wi[:, k, c0:c0 + w], start=(k == 0), stop=(k == 7))
                for k in range(8):
                    P.mm(pu[:, 0:w], uT[:, k * 128:(k + 1) * 128], wi[:, k, DFF + c0:DFF + c0 + w], start=(k == 0), stop=(k == 7))
                s_ = sg[ci % 2]
                P.act(s_[:, 0:w], pg[:, 0:w], AF.Silu)
                P.tt("dve", act[:, c0:c0 + w], s_[:, 0:w], pu[:, 0:w], ALU.mult)
            transpose_bf(S, act, actT, 22)
            pos = []
            for n2 in range(2):
                po = S.bank()
                for hc in range(22):
                    P.mm(po, actT[:, hc * 128:(hc + 1) * 128], wo[:, hc, n2 * 512:(n2 + 1) * 512], start=(hc == 0), stop=(hc == 21))
                pos.append(po)
            if n + 1 < len(rt):
                front(n + 1)
            for n2 in range(2):
                P.tt("dve", tmp[:, n2 * 512:(n2 + 1) * 512], pos[n2], gt[:, n2 * 512:(n2 + 1) * 512], ALU.mult)
            P.stt(z, xt, ALPHA, tmp, ALU.mult, ALU.add)
            layernorm(S, z, lng, lnb, xo, wk)
            P.dma("sp", dst(b, i), xo)
    P.phase_end()


def _consts():
    p = np.arange(128)[:, None]
    f = np.arange(128)[None, :]
    cst = np.stack([(p == f), (p > f), (p < f), (p >= f), (p <= f), np.ones((128, 128), bool)], 1).astype(np.float32)
    t = np.arange(TL)
    row = (t // 64).astype(np.float32)
    col = (t % 64).astype(np.float32)
    inv = (np.float32(10000.0) ** (-np.arange(0, 32, 2, dtype=np.float32) / np.float32(32))).astype(np.float32)
    ang = np.stack([row[:, None] * inv, col[:, None] * inv], 1).astype(np.float32)
    rope = np.concatenate([np.cos(ang).reshape(TL, 32), np.sin(ang).reshape(TL, 32)], 1).astype(np.float32)
    return np.ascontiguousarray(cst), np.ascontiguousarray(rope)


def _na_bias(rpb):
    L = rpb.shape[0]
    out = np.full((L, NPAT, 128, 8, 128), MASKV, np.float32)
    reps = {}
    for i in [2, 0, 1, 14, 15]:
        for (j, pat) in na_tiles(i):
            if pat not in reps:
                reps[pat] = (i, j)
    kk, kc = np.divmod(np.arange(128), 64)
    qq, qc = np.divmod(np.arange(128), 64)
    for pat, (i, j) in reps.items():
        r = 2 * i + qq
        kr = 2 * j + kk
        r0 = np.clip(r - 4, 0, 32 - 8)
        c0 = np.clip(qc - 8, 0, 64 - 16)
        ok = (kr[:, None] >= r0[None, :]) & (kr[:, None] < r0[None, :] + 8) & \
             (kc[:, None] >= c0[None, :]) & (kc[:, None] < c0[None, :] + 16)
        dr = np.clip(kr[:, None] - r[None, :] + 7, 0, 14)
        dc = np.clip(kc[:, None] - qc[None, :] + 15, 0, 30)
        g = rpb[:, :, dr, dc]
        g = np.where(ok[None, None], g, np.float32(MASKV))
        out[:, pat] = np.transpose(g, (0, 2, 1, 3))
    return out


def make_in_maps(inp, cores=range(8)):
    cst, rope = _consts()
    shared = {n: np.ascontiguousarray(np.asarray(inp[n], np.float32).reshape(WSHAPES[n])) for n in WNAMES}
    shared["nab"] = _na_bias(np.asarray(inp["na_rpb"], np.float32))
    shared["cst"] = cst
    shared["rope"] = rope
    maps = []
    for c in cores:
        m = dict(shared)
        m["x"] = np.ascontiguousarray(inp["x"][c * NB:(c + 1) * NB])
        m["ctx"] = np.ascontiguousarray(inp["ctx"][c * NB:(c + 1) * NB])
        m["c3"] = np.ascontiguousarray(np.concatenate([inp["c"][c * NB:(c + 1) * NB], inp["c_ctx"][None, :]], 0))
        maps.append(m)
    return maps


_CACHE = {}


def kernel(**inputs):
    if "S" not in _CACHE:
        _CACHE["S"] = build()
    S = _CACHE["S"]
    maps = make_in_maps(inputs)
    res = run_bass_kernel_spmd(S.nc, maps, core_ids=list(range(8)))
    return np.concatenate([np.asarray(r["out"], np.float32) for r in res.results], 0)


def mixpre_phase(S, l):
    P = S.P
    P.phase_begin()
    win = P.sb("win", [128, 8, DIN], BF16)
    for k in range(8):
        P.dma("pool", win[:, k, :], S.W["w_in"][l, k * 128:(k + 1) * 128, :])
    gain = P.sb("gain", [128, 10, 64], F32)
    for h in range(10):
        gsrc = S.W["att_q_gain"] if h < 8 else S.W["att_k_gain"]
        P.dma("sp", gain[:, h, :], gsrc[l:l + 1, :].bc([128, 64]))
    sh = P.sb("sh", [128, D], F32)
    sc = P.sb("sc", [128, D], F32)
    xb = [P.sb("xb%d" % i, [128, D], F32) for i in range(2)]
    u32 = P.sb("u32", [128, D], F32)
    ub = P.sb("ub", [128, D], BF16)
    uT = P.sb("uT", [128, D], BF16)
    NPX = C_NQ
    pxs = [P.sb("px%d" % i, [128, NPX], F32) for i in range(2)]
    gtss = [P.sb("gts%d" % i, [128, 3 * D], BF16) for i in range(2)]
    nbs = [P.sb("nb%d" % i, [128, 1536], BF16) for i in range(2)]
    sq = P.sb("sq", [128, 640], F32)
    ss = P.sb("ss", [128, 10], F32)
    rs = P.sb("rs", [128, 10], F32)
    qn = P.sb("qn", [128, 640], F32)
    rw = P.sb("rw", [128, 4, 320], F32)
    qrs = [P.sb("qr%d" % i, [128, 640], BF16) for i in range(2)]
    vas = [P.sb("va%d" % i, [128, 128], BF16) for i in range(2)]
    cs = P.sb("cs", [128, 64], F32)
    chunks = [(c0, 512) for c0 in range(0, 2560, 512)] + [(2560, 128)] + [(c0, 512) for c0 in range(C_NQ, DIN, 512)]
    cnt = 0
    for row in range(3):
        rt = row_tiles(row, list(range(NT)))
        load_mod(S, row, [3, 4], [sh, sc])
        P.ts("pool", sc, sc, 1.0, op0=ALU.add)
        P.dma("sp", xb[0], S.XA[rt[0][0]][rt[0][1] * 128:(rt[0][1] + 1) * 128, :])

        def front(n):
            P.tt("dve", u32, xb[n % 2], sc, ALU.mult)
            P.tt("dve", ub, u32, sh, ALU.add)
            transpose_bf(S, ub, uT, 8)

        front(0)
        for n, (b, i) in enumerate(rt):
            rows = slice(i * 128, (i + 1) * 128)
            if n + 1 < len(rt):
                b2, i2 = rt[n + 1]
                P.dma("sp", xb[(n + 1) % 2], S.XA[b2][i2 * 128:(i2 + 1) * 128, :])
            px = pxs[cnt % 2]
            gts = gtss[cnt % 2]
            qr = qrs[cnt % 2]
            va = vas[cnt % 2]
            nb = nbs[cnt % 2]
            cnt += 1
            for ci, (c0, w) in enumerate(chunks):
                pb = S.bank()
                for k in range(8):
                    P.mm(pb[:, 0:w], uT[:, k * 128:(k + 1) * 128], win[:, k, c0:c0 + w], start=(k == 0), stop=(k == 7))
                if c0 < C_NQ:
                    P.copy("act" if ci % 2 == 0 else "dve", px[:, c0:c0 + w], pb[:, 0:w])
                elif c0 < C_G:
                    P.copy("dve" if ci % 2 == 0 else "act", nb[:, c0 - C_NQ:c0 - C_NQ + w], pb[:, 0:w])
                else:
                    P.act(gts[:, c0 - C_G:c0 - C_G + w], pb[:, 0:w], AF.Sigmoid)
            if n + 1 < len(rt):
                front(n + 1)
            P.tt("dve", sq, px[:, 0:640], px[:, 0:640], ALU.mult)
            P.reduce(ss, sq.re("p (h d) -> p h d", d=64), ALU.add)
            P.act(rs, ss, AF.Sqrt, bias=S.eps[:, 1:2], scale=1.0 / 64)
            P.recip(rs, rs)
            qn3 = qn.re("p (h d) -> p h d", d=64)
            P.tt("dve", qn3, px[:, 0:640].re("p (h d) -> p h d", d=64), bc3(rs, [128, 10, 64]), ALU.mult)
            P.tt("dve", qn3, qn3, gain, ALU.mult)
            if i < 16:
                P.dma("sp", cs, S.rope[rows, :])
                q5 = qn.re("p (h a j d) -> p h a j d", h=10, a=2, j=2, d=16)
                o5 = qr.re("p (h a j d) -> p h a j d", h=10, a=2, j=2, d=16)
                x1, x2 = q5[:, :, :, 0, :], q5[:, :, :, 1, :]
                cosb = T(cs[:, 0:32].re("p (a d) -> p a d", a=2).ap.unsqueeze(1).to_broadcast([128, 10, 2, 16]), cs.tok)
                sinb = T(cs[:, 32:64].re("p (a d) -> p a d", a=2).ap.unsqueeze(1).to_broadcast([128, 10, 2, 16]), cs.tok)
                tw = [rw[:, t, :].re("p (h a d) -> p h a d", h=10, a=2) for t in range(4)]
                P.tt("dve", tw[0], x1, cosb, ALU.mult)
                P.tt("pool", tw[1], x2, sinb, ALU.mult)
                P.tt("dve", tw[2], x2, cosb, ALU.mult)
                P.tt("dve", tw[3], x1, sinb, ALU.mult)
                P.tt("dve", o5[:, :, :, 0, :], tw[0], tw[1], ALU.subtract)
                P.tt("pool", o5[:, :, :, 1, :], tw[2], tw[3], ALU.add)
            else:
                P.copy("dve", qr, qn)
            P.dma("sp", S.QA[b][rows, :], qr[:, 0:512])
            P.dma("sp", S.KA[b][rows, :], qr[:, 512:640])
            P.copy("act", va, px[:, C_AV:C_AV + 128])
            P.dma("sp", S.VA[b][rows, :], va)
            P.dma("sp", S.YRI[b][rows, :], px[:, C_RW:C_RW + DRW])
            P.dma("sp", S.QN[b][rows, :], nb[:, 0:512])
            P.dma("sp", S.KN[b][rows, :], nb[:, 512:1024])
            P.dma("sp", S.VN[b][rows, :], nb[:, 1024:1536])
            P.dma("sp", S.GT[b][rows, :], gts)
    P.phase_end()


def load_kT(S, ksrc, KT, nh):
    P = S.P
    ktok = P.sb("ktok", [128, NT, nh * 64], BF16)
    P.dma("sp", ktok, ksrc.re("(i p) c -> p i c", p=128))
    per = 8 // nh
    for i0 in range(0, NT, per):
        n = min(per, NT - i0)
        pb = S.bank().bitcast(BF16)
        for ii in range(n):
            for h in range(nh):
                slot = h * per + ii
                P.tr(pb[0:64, slot * 128:(slot + 1) * 128], ktok[:, i0 + ii, h * 64:(h + 1) * 64], S.identb)
        src = pb[0:64, :].re("p (h i t) -> p h i t", h=nh, i=per)[:, :, 0:n, :]
        dst = KT[:, :, i0 * 128:(i0 + n) * 128].re("p h (i t) -> p h i t", t=128)
        P.copy("act", dst, src)


def load_v1(S, vsrc, V1, nh):
    P = S.P
    P.memset("pool", V1, 1.0)
    for i in range(NT):
        P.dma("sp", V1[:, i, :, 0:64], vsrc[i * 128:(i + 1) * 128, :].re("p (g d) -> p g d", g=nh))


def pv_and_store(S, PT, V1, keyl, kvh_of, ya, dstrows, pt_of=None):
    P = S.P
    nk = len(keyl)
    for hg in range(2):
        po = S.bank()
        for hh in range(4):
            h = hg * 4 + hh
            for idx, j in enumerate(keyl):
                pt = pt_of(idx, h) if pt_of is not None else PT[:, idx, h * 128:(h + 1) * 128]
                P.mm(po[:, hh * 65:(hh + 1) * 65], pt, V1[:, j, kvh_of(h), :],
                     start=(idx == 0), stop=(idx == nk - 1))
        po3 = po[:, 0:260].re("p (h e) -> p h e", e=65)
        rec = S.rec
        P.recip(rec, po3[:, :, 64])
        P.tt("dve", ya[:, hg * 256:(hg + 1) * 256].re("p (h d) -> p h d", d=64), po3[:, :, 0:64],
             bc3(rec, [128, 4, 64]), ALU.mult)
    P.dma("sp", dstrows, ya)


def load_qT(S, qsrc_rows, qtok, QT):
    P = S.P
    P.dma("sp", qtok, qsrc_rows)
    pb = S.bank().bitcast(BF16)
    for h in range(8):
        P.tr(pb[0:64, h * 128:(h + 1) * 128], qtok[:, h * 64:(h + 1) * 64], S.identb)
    P.copy("dve", QT, pb[0:64, :])


def att_phase(S, l, b, need_ctx, side=None):
    P = S.P
    P.phase_begin()
    KT = P.sb("KT", [64, 2, TT], BF16)
    load_kT(S, S.KA[b], KT, 2)
    V1 = P.sb("V1", [128, NT, 2, 65], BF16)
    load_v1(S, S.VA[b], V1, 2)
    S.rec = P.sb("rec", [128, 4], F32)
    qtoks = [P.sb("qtok%d" % i, [128, 512], BF16) for i in range(2)]
    QTs = [P.sb("QT%d" % i, [64, 1024], BF16) for i in range(2)]
    PTgs = [[P.sb("PTg%d_%d" % (g, q), [128, NT, 512], BF16) for g in range(2)] for q in range(2)]
    yas = [P.sb("ya%d" % i, [128, 512], F32) for i in range(2)]
    qtiles = list(range(16)) + ([16, 17] if need_ctx else [])
    sidegen = side() if side is not None else None
    for n, i in enumerate(qtiles):
        keyl = list(range(NT)) if i < 16 else [16, 17]
        QT, ya, PTg = QTs[n % 2], yas[n % 2], PTgs[n % 2]
        rows = slice(i * 128, (i + 1) * 128)
        load_qT(S, S.QA[b][rows, :], qtoks[n % 2], QT)
        for idx, j in enumerate(keyl):
            for g in range(2):
                ps = S.bank()
                P.mm(ps, KT[:, g, j * 128:(j + 1) * 128], QT[:, g * 512:(g + 1) * 512])
                P.act(PTg[g][:, idx, :], ps, AF.Exp, scale=0.125)
        pv_and_store(S, None, V1, keyl, lambda h: h // 4, ya, S.YA[b][rows, :],
                     pt_of=lambda idx, h: PTg[h // 4][:, idx, (h % 4) * 128:(h % 4 + 1) * 128])
        if sidegen is not None:
            next(sidegen, None)
    if sidegen is not None:
        for _ in sidegen:
            pass
    P.phase_end()


def na_phase(S, l, b, need_ctx, side=None):
    P = S.P
    P.phase_begin()
    KT = P.sb("KTn", [64, 8, TT], BF16)
    load_kT(S, S.KN[b], KT, 8)
    V1 = P.sb("V1n", [128, NT, 8, 65], BF16)
    load_v1(S, S.VN[b], V1, 8)
    S.rec = P.sb("rec", [128, 4], F32)
    qtoks = [P.sb("qtok%d" % i, [128, 512], BF16) for i in range(2)]
    QTs = [P.sb("QT%d" % i, [64, 1024], BF16) for i in range(2)]
    PTs = [P.sb("PT%d" % i, [128, 7, 1024], BF16) for i in range(2)]
    yas = [P.sb("ya%d" % i, [128, 512], F32) for i in range(2)]
    nbs = [P.sb("nab%d" % i, [128, 5, 8, 128], F32) for i in range(2)]
    tmps = [P.sb("natmp%d" % i, [128, 512], F32) for i in range(2)]
    qtiles = list(range(16)) + ([16, 17] if need_ctx else [])
    sidegen = side() if side is not None else None
    tc = 0
    for n, i in enumerate(qtiles):
        kl = (na_tiles(i) if i < 16 else []) + [(16, None), (17, None)]
        QT, PT, ya, nb = QTs[n % 2], PTs[n % 2], yas[n % 2], nbs[n % 2]
        rows = slice(i * 128, (i + 1) * 128)
        for idx, (j, pat) in enumerate(kl):
            if pat is not None:
                P.dma("sp", nb[:, idx], S.nab[l, pat])
        load_qT(S, S.QN[b][rows, :], qtoks[n % 2], QT)
        for idx, (j, pat) in enumerate(kl):
            for hg in range(2):
                ps = S.bank()
                for hh in range(4):
                    h = hg * 4 + hh
                    P.mm(ps[:, hh * 128:(hh + 1) * 128], KT[:, h, j * 128:(j + 1) * 128], QT[:, h * 128:(h + 1) * 128])
                if pat is not None:
                    tmp = tmps[tc % 2]
                    tc += 1
                    P.stt(tmp, ps, 0.125, nb[:, idx, hg * 4:(hg + 1) * 4, :].re("p h q -> p (h q)"), ALU.mult, ALU.add)
                    P.act(PT[:, idx, hg * 512:(hg + 1) * 512], tmp, AF.Exp)
                else:
                    P.act(PT[:, idx, hg * 512:(hg + 1) * 512], ps, AF.Exp, scale=0.125)
        pv_and_store(S, PT, V1, [j for j, _ in kl], lambda h: h, ya, S.YN[b][rows, :])
        if sidegen is not None:
            next(sidegen, None)
    if sidegen is not None:
        for _ in sidegen:
            pass
    P.phase_end()


def merge_phase(S, l, tiles):
    P = S.P
    P.phase_begin()
    wb = P.sb("wbr", [128, 12, D], BF16)
    for zz in range(3):
        for k in range(4):
            P.dma("pool", wb[:, zz * 4 + k, :], S.W["w_branch"][l, zz, k * 128:(k + 1) * 128, :])
    wout = P.sb("wout", [128, 8, D], BF16)
    for k in range(8):
        P.dma("pool", wout[:, k, :], S.W["w_out"][l, k * 128:(k + 1) * 128, :])
    lng = P.sb("lng", [128, D], F32)
    lnb = P.sb("lnb", [128, D], F32)
    P.dma("sp", lng, S.W["ln_g"][l, 1:2, :].bc([128, D]))
    P.dma("sp", lnb, S.W["ln_b"][l, 1:2, :].bc([128, D]))
    gt = P.sb("gt", [128, D], F32)
    xb = [P.sb("xb%d" % i, [128, D], F32) for i in range(2)]
    y32 = [P.sb("y32_%d" % i, [128, 1536], F32) for i in range(2)]
    gts = [P.sb("gtl%d" % i, [128, 3 * D], BF16) for i in range(2)]
    yb = P.sb("yb", [128, 1536], BF16)
    yT = P.sb("yT", [128, 1536], BF16)
    mg = P.sb("mg", [128, D], F32)
    tmp = P.sb("tmp", [128, 512], F32)
    mgb = P.sb("mgb", [128, D], BF16)
    mT = P.sb("mT", [128, D], BF16)
    tmp2 = P.sb("tmp2", [128, D], F32)
    z = P.sb("z", [128, D], F32)
    xo = P.sb("xo", [128, D], F32)
    wk = ln_work(P)

    def loads(n, b, i):
        rows = slice(i * 128, (i + 1) * 128)
        P.dma("sp", xb[n % 2], S.XA[b][rows, :])
        P.dma("sp", y32[n % 2][:, 0:512], S.YA[b][rows, :])
        P.dma("sp", y32[n % 2][:, 512:1024], S.YR[b][rows, :])
        P.dma("sp", y32[n % 2][:, 1024:1536], S.YN[b][rows, :])
        P.dma("sp", gts[n % 2], S.GT[b][rows, :])

    for row in range(3):
        rt = row_tiles(row, tiles)
        if not rt:
            continue
        load_mod(S, row, [5], [gt])
        loads(0, *rt[0])
        for n, (b, i) in enumerate(rt):
            rows = slice(i * 128, (i + 1) * 128)
            if n + 1 < len(rt):
                loads(n + 1, *rt[n + 1])
            xt, yy, gg = xb[n % 2], y32[n % 2], gts[n % 2]
            P.copy("act", yb[:, 0:768], yy[:, 0:768])
            P.copy("dve", yb[:, 768:1536], yy[:, 768:1536])
            transpose_bf(S, yb, yT, 12)
            for n2 in range(2):
                cs_ = slice(n2 * 512, (n2 + 1) * 512)
                for zz in range(3):
                    ps = S.bank()
                    for k in range(4):
                        P.mm(ps, yT[:, (zz * 4 + k) * 128:(zz * 4 + k + 1) * 128], wb[:, zz * 4 + k, cs_], start=(k == 0), stop=(k == 3))
                    gsl = gg[:, zz * D + n2 * 512: zz * D + (n2 + 1) * 512]
                    if zz == 0:
                        P.tt("dve", mg[:, cs_], ps, gsl, ALU.mult)
                    else:
                        P.tt("dve", tmp, ps, gsl, ALU.mult)
                        P.tt("dve", mg[:, cs_], mg[:, cs_], tmp, ALU.add)
            P.copy("act", mgb, mg)
            transpose_bf(S, mgb, mT, 8)
            for n2 in range(2):
                cs_ = slice(n2 * 512, (n2 + 1) * 512)
                po = S.bank()
                for k in range(8):
                    P.mm(po, mT[:, k * 128:(k + 1) * 128], wout[:, k, cs_], start=(k == 0), stop=(k == 7))
                P.tt("dve", tmp2[:, cs_], po, gt[:, cs_], ALU.mult)
            P.stt(z, xt, ALPHA, tmp2, ALU.mult, ALU.add)
            layernorm(S, z, lng, lnb, xo, wk)
            P.dma("sp", S.XB[b][rows, :], xo)
    P.phase_end()


def rwkv_phase(S, l, b, need_ctx):
    for _ in rwkv_prep(S, l, b):
        pass
    rwkv_scan(S, l, b)
    for _ in rwkv_readout(S, l, b, need_ctx):
        pass


def rwkv_prep(S, l, b, own_phase=True, dbuf=True):
    P = S.P
    W = S.W
    RS = S.RS
    if own_phase:
        P.phase_begin()
    mu0 = P.sb("mu0", [128, DRW], F32)
    mu1 = P.sb("mu1", [128, DRW], F32)
    P.dma("sp", mu0, W["rwkv_mu"][l, 0:1, :].bc([128, DRW]))
    P.dma("sp", mu1, W["rwkv_mu"][l, 1:2, :].bc([128, DRW]))
    w2 = P.sb("w2", [128, 512], F32)
    a2 = P.sb("a2", [128, 512], F32)
    g2 = P.sb("g2", [128, 512], F32)
    P.dma("sp", w2, W["rwkv_w2"][l].re("z r c -> (z r) c"))
    P.dma("sp", a2, W["rwkv_a2"][l].re("z r c -> (z r) c"))
    P.dma("sp", g2, W["rwkv_g2"][l])
    w0b = P.sb("w0b", [128, 2, 512], F32)
    a0b = P.sb("a0b", [128, 2, 512], F32)
    for z in range(2):
        P.dma("sp", w0b[:, z, :], W["rwkv_w0"][l, z:z + 1, :].bc([128, 512]))
        P.dma("sp", a0b[:, z, :], W["rwkv_a0"][l, z:z + 1, :].bc([128, 512]))
    kkb = P.sb("kkb", [128, 512], F32)
    kab = P.sb("kab", [128, 512], F32)
    P.dma("sp", kkb, W["rwkv_k_k"][l:l + 1, :].bc([128, 512]))
    P.dma("sp", kab, W["rwkv_k_a"][l:l + 1, :].bc([128, 512]))
    def mky(q):
        d = Ctx()
        d.yc = P.sb("yc%d" % q, [128, DRW], F32)
        d.yp = P.sb("yp%d" % q, [128, DRW], F32)
        d.yn = P.sb("yn%d" % q, [128, DRW], F32)
        return d

    def mkw(q):
        d = Ctx()
        d.lo = P.sb("lo%d" % q, [128, 3, 128], F32)
        d.loT = P.sb("loT%d" % q, [128, 3, 128], F32)
        d.t1 = P.sb("t1%d" % q, [128, 512], F32)
        d.lw = [P.sb("lw%d_%d" % (z, q), [128, 512], F32) for z in range(2)]
        d.ic = [P.sb("ic%d_%d" % (z, q), [128, 512], F32) for z in range(2)]
        d.kd = [P.sb("kd%d_%d" % (z, q), [128, 512], F32) for z in range(2)]
        d.bz = [P.sb("bz%d_%d" % (z, q), [128, 512], F32) for z in range(2)]
        for n in ["gg", "kx", "sq", "kk", "aa"]:
            setattr(d, n, P.sb("%s%d" % (n, q), [128, 512], F32))
        d.ss8 = P.sb("ss8%d" % q, [128, 8], F32)
        return d

    ysets = [mky(q) for q in range(3)]
    wsets = [mkw(q) for q in range(2)]
    Y = S.YRI[b]

    def loads(i):
        d = ysets[i % 3]
        r0 = i * 128
        rows = slice(r0, r0 + 128)
        P.dma("sp", d.yc, Y[rows, :])
        if i == 0 or i == 16:
            P.memset("pool", d.yp, 0.0)
            P.dma("sp", d.yp[1:128, :], Y[r0:r0 + 127, :])
        else:
            P.dma("sp", d.yp, Y[r0 - 1:r0 + 127, :])
        if i == 15 or i == 17:
            P.memset("pool", d.yn, 0.0)
            P.dma("sp", d.yn[0:127, :], Y[r0 + 1:r0 + 128, :])
        else:
            P.dma("sp", d.yn, Y[r0 + 1:r0 + 129, :])

    def stage_a(i):
        dy, dw = ysets[i % 3], wsets[i % 2]
        yc, yp, yn, lo, loT = dy.yc, dy.yp, dy.yn, dw.lo, dw.loT
        P.tt("dve", yp, yp, yc, ALU.subtract)
        P.tt("dve", yn, yn, yc, ALU.subtract)
        P.tt("dve", yp, yp, mu0, ALU.mult)
        P.tt("dve", yn, yn, mu1, ALU.mult)
        P.tt("dve", yc, yc, yp, ALU.add)
        P.tt("dve", yc, yc, yn, ALU.add)
        P.act(lo[:, 0, :], yc[:, 1536:1664], AF.Tanh)
        P.copy("act", lo[:, 1, :], yc[:, 1664:1792])
        P.act(lo[:, 2, :], yc[:, 1792:1920], AF.Sigmoid)
        pb = S.bank()
        for q in range(3):
            P.tr(pb[:, q * 128:(q + 1) * 128], lo[:, q, :], S.identf)
        P.copy("act", loT.re("p a t -> p (a t)"), pb[:, 0:384])

    def stage_b(i):
        dy, d = ysets[i % 3], wsets[i % 2]
        yc, loT, t1, lw, ic, kd, bz = dy.yc, d.loT, d.t1, d.lw, d.ic, d.kd, d.bz
        gg, kx, sq, kk, aa, ss8 = d.gg, d.kx, d.sq, d.kk, d.aa, d.ss8
        rows = slice(i * 128, (i + 1) * 128)
        r_, k_, v_ = yc[:, 0:512], yc[:, 512:1024], yc[:, 1024:1536]
        for z in range(2):
            ps = S.bank()
            P.mm(ps, loT[z * 64:(z + 1) * 64, 0, :], w2[z * 64:(z + 1) * 64, :])
            P.tt("dve", t1, ps, w0b[:, z, :], ALU.add)
            P.act(t1, t1, AF.Sigmoid)
            P.ts("dve", lw[z], t1, -float(np.exp(-0.5)), op0=ALU.mult)
            P.dma("sp", RS["LW%d" % z][rows, :], lw[z])
            ps = S.bank()
            P.mm(ps, loT[z * 64:(z + 1) * 64, 1, :], a2[z * 64:(z + 1) * 64, :])
            P.tt("dve", ic[z], ps, a0b[:, z, :], ALU.add)
            P.act(ic[z], ic[z], AF.Sigmoid)
        ps = S.bank()
        P.mm(ps, loT[:, 2, :], g2)
        P.copy("act", gg, ps)
        P.dma("sp", RS["G"][rows, :], gg)
        P.tt("dve", kx, k_, kkb, ALU.mult)
        P.tt("pool", sq, kx, kx, ALU.mult)
        P.reduce(ss8, sq.re("p (h d) -> p h d", d=64), ALU.add)
        P.act(ss8, ss8, AF.Sqrt, bias=S.eps[:, 3:4], scale=1.0)
        P.recip(ss8, ss8)
        P.tt("dve", kk.re("p (h d) -> p h d", d=64), kx.re("p (h d) -> p h d", d=64), bc3(ss8, [128, 8, 64]), ALU.mult)
        P.act(aa, kk, AF.Identity, scale=-1.0)
        P.dma("sp", RS["A"][rows, :], aa)
        for z in range(2):
            P.stt(t1, ic[z], -1.0, kab, ALU.add, ALU.mult)
            P.stt(kd[z], t1, 1.0, k_, ALU.add, ALU.mult)
            P.dma("sp", RS["KD%d" % z][rows, :], kd[z])
            P.tt("pool", bz[z], kk, ic[z], ALU.mult)
            P.dma("sp", RS["B%d" % z][rows, :], bz[z])
        P.dma("sp", RS["R"][rows, :], r_)
        P.dma("sp", RS["K"][rows, :], k_)
        P.dma("sp", RS["V"][rows, :], v_)

    loads(0)
    loads(1)
    stage_a(0)
    for i in range(NT):
        if i + 2 < NT:
            loads(i + 2)
        if i + 1 < NT:
            stage_a(i + 1)
        stage_b(i)
        yield
    if own_phase:
        P.phase_end()


def rwkv_scan(S, l, b, nsteps=NT):
    P = S.P
    RS = S.RS
    P.phase_begin()
    ident, SL, SU, IL, IU, ones = [S.cf[:, q, :] for q in range(6)]
    order = [[16, 17] + list(range(16)), [17, 16] + list(range(15, -1, -1))]
    mN = [SL, SU]
    mTs = [SU, SL]
    mTi = [IU, IL]
    mC = [IU, IL]

    def mk(z):
        d = Ctx()
        for n in ["lw", "kd", "bb", "aa", "rr", "vv", "cum", "d1", "d2", "e1", "e2", "At", "Bt", "Kt", "Rt", "Bg", "Kg",
                  "Zs", "Us", "Os"]:
            setattr(d, n, P.sb("%s%d" % (n, z), [128, 512], F32))
        for n in ["AtT", "BtT", "KtT", "RtT"]:
            setattr(d, n, P.sb("%s%d" % (n, z), [64, 8, 128], F32))
        for n in ["Aak", "Arb", "Ark"]:
            setattr(d, n, [P.sb("%s%d_%d" % (n, z, hg), [128, 4, 128], F32) for hg in range(2)])
        for n in ["M", "Mt", "Pt"]:
            setattr(d, n, [[P.sb("%s%d_%d_%d" % (n, z, hg, q), [128, 4, 128], F32) for q in range(2)] for hg in range(2)])
        d.gL = P.sb("gL%d" % z, [64, 8], F32)
        d.H = P.sb("H%d" % z, [64, 8, 64], F32)
        P.memset("pool", d.H, 0.0)
        return d

    Dz = [mk(0), mk(1)]

    def mb(m):
        return T(m.ap.unsqueeze(1).to_broadcast([128, 4, 128]), m.tok)

    def b4(ps):
        return ps.re("p (h t) -> p h t", h=4)

    def opd(XT, h):
        return XT[:, h, :]

    def pre(z, c):
        d = Dz[z]
        rows = slice(c * 128, (c + 1) * 128)
        P.dma("sp", d.lw, RS["LW%d" % z][rows, :])
        P.dma("sp", d.kd, RS["KD%d" % z][rows, :])
        P.dma("sp", d.bb, RS["B%d" % z][rows, :])
        P.dma("sp", d.aa, RS["A"][rows, :])
        P.dma("sp", d.rr, RS["R"][rows, :])
        P.dma("sp", d.vv, RS["V"][rows, :])
        pc = S.bank()
        P.mm(pc, mC[z], d.lw)
        pl = S.bank()
        P.mm(pl, ones, d.lw)
        pg = S.bank()
        for h in range(8):
            P.mm(pg[0:64, 2 * h:2 * h + 2], d.lw[:, h * 64:(h + 1) * 64], ones[:, 0:2])
        P.act(d.gL, pg[0:64, 0:16].re("p (h two) -> p h two", two=2)[:, :, 0], AF.Exp)
        P.copy("act", d.cum, pc)
        P.tt("dve", d.d1, d.cum, d.lw, ALU.subtract)
        P.tt("dve", d.d2, pl, d.cum, ALU.subtract)
        P.act(d.e1, d.cum, AF.Exp)
        P.act(d.e2, d.cum, AF.Exp, scale=-1.0)
        P.act(d.d1, d.d1, AF.Exp)
        P.act(d.d2, d.d2, AF.Exp)
        P.tt("dve", d.At, d.aa, d.d1, ALU.mult)
        P.tt("dve", d.Bt, d.bb, d.e2, ALU.mult)
        P.tt("dve", d.Kt, d.kd, d.e2, ALU.mult)
        P.tt("dve", d.Rt, d.rr, d.e1, ALU.mult)
        P.tt("dve", d.Bg, d.bb, d.d2, ALU.mult)
        P.tt("pool", d.Kg, d.kd, d.d2, ALU.mult)
        qi = 0
        for (src, dst) in [(d.At, d.AtT), (d.Bt, d.BtT), (d.Kt, d.KtT), (d.Rt, d.RtT)]:
            for hg in range(2):
                pb = S.bank()
                for hh in range(4):
                    h = hg * 4 + hh
                    P.tr(pb[0:64, hh * 128:(hh + 1) * 128], src[:, h * 64:(h + 1) * 64], ident)
                P.copy("act" if qi % 2 == 0 else "dve", dst[:, hg * 4:(hg + 1) * 4, :].re("p a t -> p (a t)"), pb[0:64, :])
                qi += 1
        for hg in range(2):
            M, Mt, Pt = d.M[hg], d.Mt[hg], d.Pt[hg]
            specs = [(d.AtT, d.BtT, mN[z], M[0]), (d.BtT, d.AtT, mTs[z], Mt[0]), (d.KtT, d.AtT, mTs[z], d.Aak[hg]),
                     (d.BtT, d.RtT, mTi[z], d.Arb[hg]), (d.KtT, d.RtT, mTi[z], d.Ark[hg])]
            for (LT, RT_, msk, dst) in specs:
                ps = S.bank()
                for hh in range(4):
                    h = hg * 4 + hh
                    P.mm(ps[:, hh * 128:(hh + 1) * 128], opd(LT, h), opd(RT_, h))
                P.tt("dve", dst, b4(ps), mb(msk), ALU.mult)
            P.tt("dve", Pt[0], Mt[0], mb(ident), ALU.add)
            cur = 0
            for j in range(1, 7):
                nxt = 1 - cur
                if j < 6:
                    ps1 = S.bank()
                    for hh in range(4):
                        P.mm(ps1[:, hh * 128:(hh + 1) * 128], M[cur][:, hh, :], Mt[cur][:, hh, :])
                    P.copy("act", Mt[nxt], b4(ps1))
                ps2 = S.bank()
                for hh in range(4):
                    P.mm(ps2[:, hh * 128:(hh + 1) * 128], Mt[cur][:, hh, :], M[cur][:, hh, :])
                P.copy("dve", M[nxt], b4(ps2))
                ps3 = S.bank()
                for hh in range(4):
                    P.mm(ps3[:, hh * 128:(hh + 1) * 128], M[nxt][:, hh, :], Pt[cur][:, hh, :])
                P.tt("dve", Pt[nxt], b4(ps3), Pt[cur], ALU.add)
                cur = nxt
            assert cur == 0

    def seq_stages(z, c):
        d = Dz[z]
        rows = slice(c * 128, (c + 1) * 128)
        H = d.H

        def st_z():
            pz = S.bank()
            for h in range(8):
                hc = slice(h * 64, (h + 1) * 64)
                P.mm(pz[:, hc], d.AtT[:, h, :], H[:, h, :], start=True, stop=False)
                P.mm(pz[:, hc], d.Aak[h // 4][:, h % 4, :], d.vv[:, hc], start=False, stop=True)
            P.copy("act", d.Zs, pz)

        def st_u():
            pu = S.bank()
            for h in range(8):
                hc = slice(h * 64, (h + 1) * 64)
                P.mm(pu[:, hc], d.Pt[h // 4][0][:, h % 4, :], d.Zs[:, hc])
            P.copy("dve", d.Us, pu)

        def st_o():
            po = S.bank()
            for h in range(8):
                hc = slice(h * 64, (h + 1) * 64)
                P.mm(po[:, hc], d.RtT[:, h, :], H[:, h, :], start=True, stop=False)
                P.mm(po[:, hc], d.Arb[h // 4][:, h % 4, :], d.Us[:, hc], start=False, stop=False)
                P.mm(po[:, hc], d.Ark[h // 4][:, h % 4, :], d.vv[:, hc], start=False, stop=True)
            P.copy("act", d.Os, po)
            P.dma("sp", RS["O%d" % z][rows, :], d.Os)

        def st_h():
            ph = S.bank()
            for h in range(8):
                hc = slice(h * 64, (h + 1) * 64)
                P.mm(ph[0:64, hc], d.Bg[:, hc], d.Us[:, hc], start=True, stop=False)
                P.mm(ph[0:64, hc], d.Kg[:, hc], d.vv[:, hc], start=False, stop=True)
            P.tt("dve", H, H, bc3(d.gL, [64, 8, 64]), ALU.mult)
            P.tt("dve", H, H, ph[0:64, :].re("p (h v) -> p h v", v=64), ALU.add)

        return [st_z, st_u, st_o, st_h]

    for step in range(nsteps):
        for z in range(2):
            pre(z, order[z][step])
        stg = [seq_stages(z, order[z][step]) for z in range(2)]
        for k in range(4):
            for z in range(2):
                stg[z][k]()
    P.phase_end()


def rwkv_readout(S, l, b, need_ctx, own_phase=True):
    P = S.P
    RS = S.RS
    W = S.W
    if own_phase:
        P.phase_begin()
    gng = P.sb("gng", [128, 512], F32)
    gnb = P.sb("gnb", [128, 512], F32)
    rkb = P.sb("rkb", [128, 512], F32)
    P.dma("sp", gng, W["rwkv_gn_g"][l:l + 1, :].bc([128, 512]))
    P.dma("sp", gnb, W["rwkv_gn_b"][l:l + 1, :].bc([128, 512]))
    P.dma("sp", rkb, W["rwkv_r_k"][l:l + 1, :].bc([128, 512]))
    names = ["O0", "O1", "R", "K", "V", "G"]
    bufs = [{n: P.sb("ro_%s%d" % (n, q), [128, 512], F32) for n in names} for q in range(2)]
    o = P.sb("o", [128, 512], F32)
    sq = P.sb("sq", [128, 512], F32)
    s1 = P.sb("s1", [128, 8], F32)
    s2 = P.sb("s2", [128, 8], F32)
    m2 = P.sb("m2", [128, 8], F32)
    rk = P.sb("rk", [128, 512], F32)
    sb8 = P.sb("sb8", [128, 8], F32)
    yo = [P.sb("yo%d" % q, [128, 512], F32) for q in range(2)]
    tiles = list(range(16)) + ([16, 17] if need_ctx else [])

    def v3(t):
        return t.re("p (h d) -> p h d", d=64)

    def loads(n, i):
        for nm in names:
            P.dma("sp", bufs[n % 2][nm], RS[nm][i * 128:(i + 1) * 128, :])

    loads(0, tiles[0])
    for n, i in enumerate(tiles):
        if n + 1 < len(tiles):
            loads(n + 1, tiles[n + 1])
        B = bufs[n % 2]
        y = yo[n % 2]
        P.tt("dve", o, B["O0"], B["O1"], ALU.add)
        P.reduce(s1, v3(o), ALU.add)
        P.tt("dve", sq, o, o, ALU.mult)
        P.reduce(s2, v3(sq), ALU.add)
        P.ts("dve", s1, s1, 1.0 / 64, op0=ALU.mult)
        P.tt("dve", m2, s1, s1, ALU.mult)
        P.stt(s2, s2, 1.0 / 64, m2, ALU.mult, ALU.subtract)
        P.act(s2, s2, AF.Sqrt, bias=S.eps[:, 2:3], scale=1.0)
        P.recip(s2, s2)
        P.tt("dve", v3(o), v3(o), bc3(s1, [128, 8, 64]), ALU.subtract)
        P.tt("dve", v3(o), v3(o), bc3(s2, [128, 8, 64]), ALU.mult)
        P.tt("dve", o, o, gng, ALU.mult)
        P.tt("dve", o, o, gnb, ALU.add)
        P.tt("dve", rk, B["R"], B["K"], ALU.mult)
        P.tt("dve", rk, rk, rkb, ALU.mult)
        P.reduce(sb8, v3(rk), ALU.add)
        P.tt("dve", v3(rk), v3(B["V"]), bc3(sb8, [128, 8, 64]), ALU.mult)
        P.tt("dve", o, o, rk, ALU.add)
        P.tt("dve", y, o, B["G"], ALU.mult)
        P.dma("sp", S.YR[b][i * 128:(i + 1) * 128, :], y)
        yield
    if own_phase:
        P.phase_end()
```

```python
import numpy as np
import concourse.bass as bass
import concourse.mybir as mybir
from concourse.bass_utils import run_bass_kernel_spmd

F32 = mybir.dt.float32
BF16 = mybir.dt.bfloat16
AF = mybir.ActivationFunctionType
ALU = mybir.AluOpType
AX = mybir.AxisListType


class Tok:
    __slots__ = ("w", "r", "dsem", "dcount", "name")

    def __init__(self, name=""):
        self.w = None
        self.r = {}
        self.dsem = None
        self.dcount = 0
        self.name = name


class T:
    __slots__ = ("ap", "tok")

    def __init__(self, ap, tok):
        self.ap = ap
        self.tok = tok

    def __getitem__(self, idx):
        return T(self.ap[idx], self.tok)

    def re(self, pat, **kw):
        return T(self.ap.rearrange(pat, **kw), self.tok)

    def bc(self, shape):
        return T(self.ap.to_broadcast(shape), self.tok)

    def bitcast(self, dt):
        return T(self.ap.bitcast(dt), self.tok)

    def wt(self, tok):
        return T(self.ap, tok)

    def r(self):
        return T(self.ap.bitcast(mybir.dt.float32r), self.tok)

    @property
    def shape(self):
        return self.ap.shape


class Prog:
    def __init__(self, nc):
        self.nc = nc
        self.eng = {"pe": nc.tensor, "dve": nc.vector, "act": nc.scalar, "pool": nc.gpsimd, "sp": nc.sync}
        self.sem = {k: nc.alloc_semaphore("sem_" + k) for k in self.eng}
        self.cnt = {k: 0 for k in self.eng}
        self.seen = {k: {} for k in self.eng}
        self.dsems = []
        self.ninst = 0
        self.dpool = []
        self.phase_stack = None

    def dram(self, name, shape, dt, kind="Internal"):
        h = self.nc.dram_tensor(name, list(shape), dt, kind=kind)
        return T(h.ap(), Tok(name))

    def sb(self, name, shape, dt):
        if self.phase_stack is None:
            h = self.nc.alloc_sbuf_tensor(name, list(shape), dt)
            return T(h.ap(), Tok(name))
        self.uid = getattr(self, "uid", 0) + 1
        g = self.nc.sbuf_tensor("%s_%d" % (name, self.uid), list(shape), dt)
        h = g.__enter__()
        t = T(h.ap(), Tok(name))
        self.phase_stack.append((g, t.tok))
        return t

    def phase_begin(self):
        assert self.phase_stack is None
        self.phase_stack = []

    def phase_end(self):
        self.barrier()
        for g, tok in reversed(self.phase_stack):
            if tok.dsem is not None:
                self.dpool.append((tok.dsem, tok.dcount))
                self.dsems.remove(tok)
                tok.dsem = None
            g.__exit__(None, None, None)
        self.phase_stack = None

    def ps(self, name, shape, dt=F32):
        h = self.nc.alloc_psum_tensor(name, list(shape), dt)
        return T(h.ap(), Tok(name))

    def _wait(self, eng, events):
        best = {}
        for ev in events:
            if ev is None:
                continue
            s, v = ev
            if s.num not in best or best[s.num][1] < v:
                best[s.num] = (s, v)
        seen = self.seen[eng]
        e = self.eng[eng]
        for num, (s, v) in best.items():
            if eng == "pe" and s is self.sem["pe"]:
                continue
            if seen.get(num, 0) < v:
                e.wait_ge(s, v)
                seen[num] = v
                self.ninst += 1

    def _deps(self, w, r):
        evs = []
        for t in r:
            evs.append(t.tok.w)
        for t in w:
            evs.append(t.tok.w)
            evs.extend(t.tok.r.values())
        return evs

    def op(self, eng, fn, w, r):
        self._wait(eng, self._deps(w, r))
        inst = fn(self.eng[eng])
        self.cnt[eng] += 1
        self.ninst += 1
        s = self.sem[eng]
        inst.then_inc(s, 1)
        ev = (s, self.cnt[eng])
        for t in r:
            t.tok.r[s.num] = ev
        for t in w:
            t.tok.w = ev
            t.tok.r = {}
        return ev

    def dma(self, eng, out, in_, **kw):
        tok = out.tok
        evs = [in_.tok.w] + list(tok.r.values())
        if tok.w is not None and tok.w[0] is not tok.dsem:
            evs.append(tok.w)
        self._wait(eng, evs)
        if tok.dsem is None:
            if self.dpool:
                tok.dsem, tok.dcount = self.dpool.pop()
            else:
                tok.dsem = self.nc.alloc_semaphore("d%d_%s" % (len(self.dsems), tok.name))
            self.dsems.append(tok)
        inst = self.eng[eng].dma_start(out=out.ap, in_=in_.ap, **kw)
        inst.then_inc(tok.dsem, 16)
        self.ninst += 1
        tok.dcount += 16
        ev = (tok.dsem, tok.dcount)
        in_.tok.r[tok.dsem.num] = ev
        tok.w = ev
        tok.r = {}
        return ev

    def barrier(self):
        evs = [(self.sem[k], self.cnt[k]) for k in self.eng if self.cnt[k] > 0]
        evs += [(t.dsem, t.dcount) for t in self.dsems]
        for k in self.eng:
            self._wait(k, evs)

    def finish(self):
        self.barrier()

    def mm(self, out, lhsT, rhs, start=True, stop=True, r32=False):
        la, ra = lhsT.ap, rhs.ap
        if r32:
            la, ra = la.bitcast(mybir.dt.float32r), ra.bitcast(mybir.dt.float32r)
        return self.op("pe", lambda e: e.matmul(out.ap, la, ra, start=start, stop=stop),
                       [out], [lhsT, rhs])

    def tr(self, out, in_, ident):
        return self.op("pe", lambda e: e.transpose(out.ap, in_.ap, ident.ap), [out], [in_, ident])

    def act(self, out, in_, func, bias=None, scale=None, accum=None, eng="act"):
        kw = {}
        r = [in_]
        w = [out]
        if bias is not None:
            if isinstance(bias, T):
                kw["bias"] = bias.ap
                r.append(bias)
            else:
                kw["bias"] = bias
        if scale is not None:
            if isinstance(scale, T):
                kw["scale"] = scale.ap
                r.append(scale)
            else:
                kw["scale"] = scale
        if accum is not None:
            kw["accum_out"] = accum.ap
            w.append(accum)
        return self.op("act", lambda e: e.activation(out.ap, in_.ap, func, **kw), w, r)

    def copy(self, eng, out, in_):
        if eng == "act":
            return self.op("act", lambda e: e.copy(out.ap, in_.ap), [out], [in_])
        return self.op(eng, lambda e: e.tensor_copy(out.ap, in_.ap), [out], [in_])

    def tt(self, eng, out, a, b, op):
        return self.op(eng, lambda e: e.tensor_tensor(out.ap, a.ap, b.ap, op), [out], [a, b])

    def ts(self, eng, out, a, s1, s2=None, op0=ALU.mult, op1=None, accum=None):
        r = [a]
        w = [out]
        v1 = s1
        v2 = s2
        if isinstance(s1, T):
            r.append(s1)
            v1 = s1.ap
        if isinstance(s2, T):
            r.append(s2)
            v2 = s2.ap
        kw = {}
        if op1 is not None:
            kw["op1"] = op1
        if accum is not None:
            kw["accum_out"] = accum.ap
            w.append(accum)
        return self.op(eng, lambda e: e.tensor_scalar(out.ap, a.ap, v1, v2, op0, **kw), w, r)

    def stt(self, out, in0, scalar, in1, op0, op1, eng="dve"):
        r = [in0, in1]
        v = scalar
        if isinstance(scalar, T):
            r.append(scalar)
            v = scalar.ap
        return self.op(eng, lambda e: e.scalar_tensor_tensor(out.ap, in0.ap, v, in1.ap, op0, op1), [out], r)

    def memset(self, eng, out, val):
        return self.op(eng, lambda e: e.memset(out.ap, val), [out], [])

    def reduce(self, out, in_, op, axis=None, eng="dve"):
        ax = axis if axis is not None else AX.X
        return self.op(eng, lambda e: e.tensor_reduce(out.ap, in_.ap, ax, op), [out], [in_])

    def recip(self, out, in_):
        return self.op("dve", lambda e: e.reciprocal(out.ap, in_.ap), [out], [in_])


NB = 2
TL = 2048
TC = 256
TT = TL + TC
NT = TT // 128
D = 1024
DFF = 2816
DIN = 7296
DEPTH = 2
ALPHA = (2 * DEPTH) ** 0.25
LN_EPS = 1e-5
RMS_EPS = 1e-6
GN_EPS = 64e-5
MASKV = -30000.0
C_AQ, C_AK, C_AV, C_RW, C_NQ, C_NK, C_NV, C_G = 0, 512, 640, 768, 2688, 3200, 3712, 4224


NB = 2
TL = 2048
TC = 256
TT = TL + TC
NT = TT // 128
D = 1024
DFF = 2816
DIN = 7296
DEPTH = 2
ALPHA = (2 * DEPTH) ** 0.25
LN_EPS = 1e-5
RMS_EPS = 1e-6
GN_EPS = 64e-5
MASKV = -30000.0
C_AQ, C_AK, C_AV, C_RW, C_NQ, C_NK, C_NV, C_G = 0, 512, 640, 768, 2688, 3200, 3712, 4224
DRW = 1920

WNAMES = ["w_ada", "b_ada", "ffn1_wi", "ffn1_wo", "ffn2_wi", "ffn2_wo", "ln_g", "ln_b", "w_in",
          "att_q_gain", "att_k_gain", "rwkv_mu", "rwkv_w0", "rwkv_w2", "rwkv_a0", "rwkv_a2", "rwkv_g2",
          "rwkv_k_k", "rwkv_k_a", "rwkv_r_k", "rwkv_gn_g", "rwkv_gn_b", "w_branch", "w_out"]
WSHAPES = {
    "w_ada": [DEPTH, D, 9 * D], "b_ada": [DEPTH, 9 * D], "ffn1_wi": [DEPTH, D, 2 * DFF], "ffn1_wo": [DEPTH, DFF, D],
    "ffn2_wi": [DEPTH, D, 2 * DFF], "ffn2_wo": [DEPTH, DFF, D], "ln_g": [DEPTH, 3, D], "ln_b": [DEPTH, 3, D],
    "w_in": [DEPTH, D, DIN], "att_q_gain": [DEPTH, 64], "att_k_gain": [DEPTH, 64], "rwkv_mu": [DEPTH, 2, DRW],
    "rwkv_w0": [DEPTH, 2, 512], "rwkv_w2": [DEPTH, 2, 64, 512], "rwkv_a0": [DEPTH, 2, 512],
    "rwkv_a2": [DEPTH, 2, 64, 512], "rwkv_g2": [DEPTH, 128, 512], "rwkv_k_k": [DEPTH, 512], "rwkv_k_a": [DEPTH, 512],
    "rwkv_r_k": [DEPTH, 512], "rwkv_gn_g": [DEPTH, 512], "rwkv_gn_b": [DEPTH, 512],
    "w_branch": [DEPTH, 3, 512, D], "w_out": [DEPTH, D, D],
}


def na_tiles(i):
    if 2 <= i <= 13:
        return [(i - 2 + p, p) for p in range(5)]
    if i == 0:
        return [(j, 5 + j) for j in range(4)]
    if i == 1:
        return [(j, 9 + j) for j in range(4)]
    if i == 14:
        return [(12 + j, 13 + j) for j in range(4)]
    return [(12 + j, 17 + j) for j in range(4)]


NPAT = 21


class Ctx:
    pass


def build(dbg=False, stop=None, nlayers=DEPTH):
    nc = bass.Bass("TRN2", target_bir_lowering=False)
    P = Prog(nc)
    S = Ctx()
    S.P = P
    S.dbg = dbg
    import os
    S.nsteps = int(os.environ.get('RWKV_STEPS', NT))
    okind = "ExternalOutput" if dbg else "Internal"
    S.x_in = P.dram("x", [NB, TL, D], F32, kind="ExternalInput")
    S.ctx_in = P.dram("ctx", [NB, TC, D], F32, kind="ExternalInput")
    S.c3_in = P.dram("c3", [3, D], F32, kind="ExternalInput")
    S.W = {n: P.dram(n, WSHAPES[n], F32, kind="ExternalInput") for n in WNAMES}
    S.nab = P.dram("nab", [DEPTH, NPAT, 128, 8, 128], F32, kind="ExternalInput")
    S.cst = P.dram("cst", [128, 6, 128], F32, kind="ExternalInput")
    S.rope = P.dram("rope", [TL, 64], F32, kind="ExternalInput")
    S.out = P.dram("out", [NB, TL, D], F32, kind="ExternalOutput")
    S.XA = [P.dram("XA%d" % b, [TT, D], F32, kind=okind) for b in range(NB)]
    S.XB = [P.dram("XB%d" % b, [TT, D], F32, kind=okind) for b in range(NB)]
    S.mod = P.dram("mod", [3, 9 * D], F32, kind=okind)
    S.QA = [P.dram("QA%d" % b, [TT, 512], BF16) for b in range(NB)]
    S.KA = [P.dram("KA%d" % b, [TT, 128], BF16) for b in range(NB)]
    S.VA = [P.dram("VA%d" % b, [TT, 128], BF16) for b in range(NB)]
    S.QN = [P.dram("QN%d" % b, [TT, 512], BF16) for b in range(NB)]
    S.KN = [P.dram("KN%d" % b, [TT, 512], BF16) for b in range(NB)]
    S.VN = [P.dram("VN%d" % b, [TT, 512], BF16) for b in range(NB)]
    S.GT = [P.dram("GT%d" % b, [TT, 3 * D], BF16) for b in range(NB)]
    S.YRI = [P.dram("YRI%d" % b, [TT, DRW], F32, kind=okind) for b in range(NB)]
    S.YA = [P.dram("YA%d" % b, [TT, 512], F32, kind=okind) for b in range(NB)]
    S.YR = [P.dram("YR%d" % b, [TT, 512], F32, kind=okind) for b in range(NB)]
    S.YN = [P.dram("YN%d" % b, [TT, 512], F32, kind=okind) for b in range(NB)]
    S.RS = {n: P.dram("RS_" + n, [TT, 512], F32, kind=okind) for n in
            ["R", "K", "V", "G", "A", "LW0", "LW1", "KD0", "KD1", "B0", "B1", "O0", "O1"]}
    S.cf = P.sb("cst_f", [128, 6, 128], F32)
    P.dma("sp", S.cf, S.cst)
    S.identb = P.sb("identb", [128, 128], BF16)
    P.copy("dve", S.identb, S.cf[:, 0, :])
    S.identf = S.cf[:, 0, :]
    S.banks = [P.ps("bank%d" % i, [128, 512], F32) for i in range(8)]
    S.bank_i = 0

    def bank():
        b = S.banks[S.bank_i % 8]
        S.bank_i += 1
        return b
    S.bank = bank
    S.eps = P.sb("eps", [128, 4], F32)
    for i, v in enumerate([LN_EPS, RMS_EPS, GN_EPS, 1e-12]):
        P.memset("dve", S.eps[:, i:i + 1], v)

    xsrc = [None] * NB
    for l in range(nlayers):
        last = l == DEPTH - 1
        mod_phase(S, l)
        if stop == ("mod", l):
            break
        src = (lambda b, i: (S.x_in[b, i * 128:(i + 1) * 128, :] if i < 16 else S.ctx_in[b, (i - 16) * 128:(i - 15) * 128, :])) \
            if l == 0 else (lambda b, i: S.XB[b][i * 128:(i + 1) * 128, :])
        dst = lambda b, i: S.XA[b][i * 128:(i + 1) * 128, :]
        ffn_phase(S, l, 0, src, dst, list(range(NT)))
        if stop == ("ffn1", l):
            break
        mixpre_phase(S, l)
        if stop == ("mixpre", l):
            break
        if stop == ("att", l):
            for b in range(NB):
                att_phase(S, l, b, need_ctx=not last)
                na_phase(S, l, b, need_ctx=not last)
            break
        if stop == ("rwkvscan", l):
            for _ in rwkv_prep(S, l, 0):
                pass
            rwkv_scan(S, l, 0, nsteps=S.nsteps)
            break
        for b in range(NB):
            att_phase(S, l, b, need_ctx=not last)
            na_phase(S, l, b, need_ctx=not last)
        for b in range(NB):
            rwkv_phase(S, l, b, need_ctx=not last)
        if stop == ("rwkv", l):
            break
        merge_phase(S, l, list(range(NT)) if not last else list(range(16)))
        if stop == ("merge", l):
            break
        src = lambda b, i: S.XB[b][i * 128:(i + 1) * 128, :]
        if last:
            dst = lambda b, i: S.out[b, i * 128:(i + 1) * 128, :]
            ffn_phase(S, l, 1, src, dst, list(range(16)))
        else:
            dst = lambda b, i: S.XA[b][i * 128:(i + 1) * 128, :]
            ffn_phase(S, l, 1, src, dst, list(range(NT)))
            S.XA, S.XB = S.XB, S.XA
    P.finish()
    S.nc = nc
    return S


def bc3(t, shape):
    return T(t.ap.unsqueeze(2).to_broadcast(shape), t.tok)


def transpose_bf(S, src, dst, nch, eng_cycle=("act", "dve")):
    P = S.P
    gi = 0
    for g0 in range(0, nch, 8):
        n = min(8, nch - g0)
        pb = S.bank().bitcast(BF16)
        for k in range(n):
            P.tr(pb[:, k * 128:(k + 1) * 128], src[:, (g0 + k) * 128:(g0 + k + 1) * 128], S.identb)
        P.copy(eng_cycle[gi % len(eng_cycle)], dst[:, g0 * 128:(g0 + n) * 128], pb[:, 0:n * 128])
        gi += 1


def layernorm(S, z, g_bc, b_bc, out, wk):
    P = S.P
    stats, mv, rstd, xn = wk
    for j in range(2):
        P.op("dve", lambda e: e.bn_stats(stats[:, j, :].ap, z[:, j * 512:(j + 1) * 512].ap), [stats], [z])
    P.op("dve", lambda e: e.bn_aggr(mv.ap, stats.re("p a b -> p (a b)").ap), [mv], [stats])
    P.act(rstd, mv[:, 1:2], AF.Sqrt, bias=S.eps[:, 0:1], scale=1.0)
    P.recip(rstd, rstd)
    P.ts("dve", xn, z, mv[:, 0:1], rstd, op0=ALU.subtract, op1=ALU.mult)
    P.tt("dve", xn, xn, g_bc, ALU.mult)
    P.tt("dve", out, xn, b_bc, ALU.add)


def ln_work(P):
    return (P.sb("ln_stats", [128, 2, 6], F32), P.sb("ln_mv", [128, 2], F32),
            P.sb("ln_rstd", [128, 1], F32), P.sb("ln_xn", [128, D], F32))


def row_tiles(row, tiles):
    if row < 2:
        return [(row, i) for i in tiles if i < 16]
    return [(b, i) for b in range(NB) for i in tiles if i >= 16]


def load_mod(S, row, slots, dsts):
    for s, d in zip(slots, dsts):
        S.P.dma("sp", d, S.mod[row:row + 1, s * D:(s + 1) * D].bc([128, D]))


def mod_phase(S, l):
    P = S.P
    P.phase_begin()
    c3 = P.sb("c3", [3, D], F32)
    P.dma("sp", c3, S.c3_in)
    sc = P.sb("sc", [3, D], F32)
    P.act(sc, c3, AF.Silu)
    scT = P.sb("scT", [128, 8, 3], F32)
    pb = S.bank()
    for k in range(8):
        P.tr(pb[:, k * 4:k * 4 + 3], sc[:, k * 128:(k + 1) * 128], S.identf[0:3, 0:3])
    P.copy("dve", scT, pb[:, 0:32].re("p (k c) -> p k c", c=4)[:, :, 0:3])
    ba = P.sb("ba", [3, 9 * D], F32)
    P.dma("sp", ba, S.W["b_ada"][l:l + 1, :].bc([3, 9 * D]))
    m_sb = P.sb("m_sb", [3, 9 * D], F32)
    wch = [P.sb("wch%d" % i, [128, 8, 512], F32) for i in range(2)]
    wsrc = S.W["w_ada"][l].re("(k p) n -> p k n", p=128)
    for j in range(18):
        w = wch[j % 2]
        P.dma("sp", w, wsrc[:, :, j * 512:(j + 1) * 512])
        pb = S.bank()
        for k in range(8):
            P.mm(pb[0:3, :], scT[:, k, :], w[:, k, :], start=(k == 0), stop=(k == 7))
        P.tt("dve", m_sb[:, j * 512:(j + 1) * 512], pb[0:3, :], ba[:, j * 512:(j + 1) * 512], ALU.add)
    P.dma("sp", S.mod, m_sb)
    P.phase_end()


def ffn_phase(S, l, f, src, dst, tiles):
    P = S.P
    P.phase_begin()
    wi_d = S.W["ffn%d_wi" % (f + 1)]
    wo_d = S.W["ffn%d_wo" % (f + 1)]
    wi = P.sb("wi", [128, 8, 2 * DFF], BF16)
    wo = P.sb("wo", [128, 22, D], BF16)
    for k in range(8):
        P.dma("pool", wi[:, k, :], wi_d[l, k * 128:(k + 1) * 128, :])
    for k in range(22):
        P.dma("pool", wo[:, k, :], wo_d[l, k * 128:(k + 1) * 128, :])
    lni = 0 if f == 0 else 2
    lng = P.sb("lng", [128, D], F32)
    lnb = P.sb("lnb", [128, D], F32)
    P.dma("sp", lng, S.W["ln_g"][l, lni:lni + 1, :].bc([128, D]))
    P.dma("sp", lnb, S.W["ln_b"][l, lni:lni + 1, :].bc([128, D]))
    s0 = 0 if f == 0 else 6
    sh = P.sb("sh", [128, D], F32)
    sc = P.sb("sc", [128, D], F32)
    gt = P.sb("gt", [128, D], F32)
    xb = [P.sb("xb%d" % i, [128, D], F32) for i in range(2)]
    u32 = P.sb("u32", [128, D], F32)
    ub = P.sb("ub", [128, D], BF16)
    uT = P.sb("uT", [128, D], BF16)
    sg = [P.sb("sg%d" % i, [128, 512], F32) for i in range(2)]
    act = P.sb("act", [128, DFF], BF16)
    actT = P.sb("actT", [128, DFF], BF16)
    tmp = P.sb("tmp", [128, D], F32)
    z = P.sb("z", [128, D], F32)
    xo = P.sb("xo", [128, D], F32)
    wk = ln_work(P)
    for row in range(3):
        rt = row_tiles(row, tiles)
        if not rt:
            continue
        load_mod(S, row, [s0, s0 + 1, s0 + 2], [sh, sc, gt])
        P.ts("pool", sc, sc, 1.0, op0=ALU.add)
        P.ts("pool", gt, gt, 0.5, op0=ALU.mult)
        P.dma("sp", xb[0], src(*rt[0]))

        def front(n):
            P.tt("dve", u32, xb[n % 2], sc, ALU.mult)
            P.tt("dve", ub, u32, sh, ALU.add)
            transpose_bf(S, ub, uT, 8)

        front(0)
        for n, (b, i) in enumerate(rt):
            xt = xb[n % 2]
            if n + 1 < len(rt):
                P.dma("sp", xb[(n + 1) % 2], src(*rt[n + 1]))
            for ci, c0 in enumerate(range(0, DFF, 512)):
                w = min(512, DFF - c0)
                pg = S.bank()
                pu = S.bank()
                for k in range(8):
                    P.mm(pg[:, 0:w], uT[:, k * 128:(k + 1) * 128], wi[:, k, c0:c0 + w], start=(k == 0), stop=(k == 7))
                for k in range(8):
                    P.mm(pu[:, 0:w], uT[:, k * 128:(k + 1) * 128], wi[:, k, DFF + c0:DFF + c0 + w], start=(k == 0), stop=(k == 7))
                s_ = sg[ci % 2]
                P.act(s_[:, 0:w], pg[:, 0:w], AF.Silu)
                P.tt("dve", act[:, c0:c0 + w], s_[:, 0:w], pu[:, 0:w], ALU.mult)
            transpose_bf(S, act, actT, 22)
            pos = []
            for n2 in range(2):
                po = S.bank()
                for hc in range(22):
                    P.mm(po, actT[:, hc * 128:(hc + 1) * 128], wo[:, hc, n2 * 512:(n2 + 1) * 512], start=(hc == 0), stop=(hc == 21))
                pos.append(po)
            if n + 1 < len(rt):
                front(n + 1)
            for n2 in range(2):
                P.tt("dve", tmp[:, n2 * 512:(n2 + 1) * 512], pos[n2], gt[:, n2 * 512:(n2 + 1) * 512], ALU.mult)
            P.stt(z, xt, ALPHA, tmp, ALU.mult, ALU.add)
            layernorm(S, z, lng, lnb, xo, wk)
            P.dma("sp", dst(b, i), xo)
    P.phase_end()


def _consts():
    p = np.arange(128)[:, None]
    f = np.arange(128)[None, :]
    cst = np.stack([(p == f), (p > f), (p < f), (p >= f), (p <= f), np.ones((128, 128), bool)], 1).astype(np.float32)
    t = np.arange(TL)
    row = (t // 64).astype(np.float32)
    col = (t % 64).astype(np.float32)
    inv = (np.float32(10000.0) ** (-np.arange(0, 32, 2, dtype=np.float32) / np.float32(32))).astype(np.float32)
    ang = np.stack([row[:, None] * inv, col[:, None] * inv], 1).astype(np.float32)
    rope = np.concatenate([np.cos(ang).reshape(TL, 32), np.sin(ang).reshape(TL, 32)], 1).astype(np.float32)
    return np.ascontiguousarray(cst), np.ascontiguousarray(rope)


def _na_bias(rpb):
    L = rpb.shape[0]
    out = np.full((L, NPAT, 128, 8, 128), MASKV, np.float32)
    reps = {}
    for i in [2, 0, 1, 14, 15]:
        for (j, pat) in na_tiles(i):
            if pat not in reps:
                reps[pat] = (i, j)
    kk, kc = np.divmod(np.arange(128), 64)
    qq, qc = np.divmod(np.arange(128), 64)
    for pat, (i, j) in reps.items():
        r = 2 * i + qq
        kr = 2 * j + kk
        r0 = np.clip(r - 4, 0, 32 - 8)
        c0 = np.clip(qc - 8, 0, 64 - 16)
        ok = (kr[:, None] >= r0[None, :]) & (kr[:, None] < r0[None, :] + 8) & \
             (kc[:, None] >= c0[None, :]) & (kc[:, None] < c0[None, :] + 16)
        dr = np.clip(kr[:, None] - r[None, :] + 7, 0, 14)
        dc = np.clip(kc[:, None] - qc[None, :] + 15, 0, 30)
        g = rpb[:, :, dr, dc]
        g = np.where(ok[None, None], g, np.float32(MASKV))
        out[:, pat] = np.transpose(g, (0, 2, 1, 3))
    return out


def make_in_maps(inp, cores=range(8)):
    cst, rope = _consts()
    shared = {n: np.ascontiguousarray(np.asarray(inp[n], np.float32).reshape(WSHAPES[n])) for n in WNAMES}
    shared["nab"] = _na_bias(np.asarray(inp["na_rpb"], np.float32))
    shared["cst"] = cst
    shared["rope"] = rope
    maps = []
    for c in cores:
        m = dict(shared)
        m["x"] = np.ascontiguousarray(inp["x"][c * NB:(c + 1) * NB])
        m["ctx"] = np.ascontiguousarray(inp["ctx"][c * NB:(c + 1) * NB])
        m["c3"] = np.ascontiguousarray(np.concatenate([inp["c"][c * NB:(c + 1) * NB], inp["c_ctx"][None, :]], 0))
        maps.append(m)
    return maps


_CACHE = {}


def kernel(**inputs):
    if "S" not in _CACHE:
        _CACHE["S"] = build()
    S = _CACHE["S"]
    maps = make_in_maps(inputs)
    res = run_bass_kernel_spmd(S.nc, maps, core_ids=list(range(8)))
    return np.concatenate([np.asarray(r["out"], np.float32) for r in res.results], 0)


def mixpre_phase(S, l):
    P = S.P
    P.phase_begin()
    win = P.sb("win", [128, 8, DIN], BF16)
    for k in range(8):
        P.dma("pool", win[:, k, :], S.W["w_in"][l, k * 128:(k + 1) * 128, :])
    gain = P.sb("gain", [128, 10, 64], F32)
    for h in range(10):
        gsrc = S.W["att_q_gain"] if h < 8 else S.W["att_k_gain"]
        P.dma("sp", gain[:, h, :], gsrc[l:l + 1, :].bc([128, 64]))
    sh = P.sb("sh", [128, D], F32)
    sc = P.sb("sc", [128, D], F32)
    xb = [P.sb("xb%d" % i, [128, D], F32) for i in range(2)]
    u32 = P.sb("u32", [128, D], F32)
    ub = P.sb("ub", [128, D], BF16)
    uT = P.sb("uT", [128, D], BF16)
    NPX = C_NQ
    pxs = [P.sb("px%d" % i, [128, NPX], F32) for i in range(2)]
    gtss = [P.sb("gts%d" % i, [128, 3 * D], BF16) for i in range(2)]
    nbs = [P.sb("nb%d" % i, [128, 1536], BF16) for i in range(2)]
    sq = P.sb("sq", [128, 640], F32)
    ss = P.sb("ss", [128, 10], F32)
    rs = P.sb("rs", [128, 10], F32)
    qn = P.sb("qn", [128, 640], F32)
    rw = P.sb("rw", [128, 4, 320], F32)
    qrs = [P.sb("qr%d" % i, [128, 640], BF16) for i in range(2)]
    vas = [P.sb("va%d" % i, [128, 128], BF16) for i in range(2)]
    cs = P.sb("cs", [128, 64], F32)
    chunks = [(c0, 512) for c0 in range(0, 2560, 512)] + [(2560, 128)] + [(c0, 512) for c0 in range(C_NQ, DIN, 512)]
    cnt = 0
    for row in range(3):
        rt = row_tiles(row, list(range(NT)))
        load_mod(S, row, [3, 4], [sh, sc])
        P.ts("pool", sc, sc, 1.0, op0=ALU.add)
        P.dma("sp", xb[0], S.XA[rt[0][0]][rt[0][1] * 128:(rt[0][1] + 1) * 128, :])

        def front(n):
            P.tt("dve", u32, xb[n % 2], sc, ALU.mult)
            P.tt("dve", ub, u32, sh, ALU.add)
            transpose_bf(S, ub, uT, 8)

        front(0)
        for n, (b, i) in enumerate(rt):
            rows = slice(i * 128, (i + 1) * 128)
            if n + 1 < len(rt):
                b2, i2 = rt[n + 1]
                P.dma("sp", xb[(n + 1) % 2], S.XA[b2][i2 * 128:(i2 + 1) * 128, :])
            px = pxs[cnt % 2]
            gts = gtss[cnt % 2]
            qr = qrs[cnt % 2]
            va = vas[cnt % 2]
            nb = nbs[cnt % 2]
            cnt += 1
            for ci, (c0, w) in enumerate(chunks):
                pb = S.bank()
                for k in range(8):
                    P.mm(pb[:, 0:w], uT[:, k * 128:(k + 1) * 128], win[:, k, c0:c0 + w], start=(k == 0), stop=(k == 7))
                if c0 < C_NQ:
                    P.copy("act" if ci % 2 == 0 else "dve", px[:, c0:c0 + w], pb[:, 0:w])
                elif c0 < C_G:
                    P.copy("dve" if ci % 2 == 0 else "act", nb[:, c0 - C_NQ:c0 - C_NQ + w], pb[:, 0:w])
                else:
                    P.act(gts[:, c0 - C_G:c0 - C_G + w], pb[:, 0:w], AF.Sigmoid)
            if n + 1 < len(rt):
                front(n + 1)
            P.tt("dve", sq, px[:, 0:640], px[:, 0:640], ALU.mult)
            P.reduce(ss, sq.re("p (h d) -> p h d", d=64), ALU.add)
            P.act(rs, ss, AF.Sqrt, bias=S.eps[:, 1:2], scale=1.0 / 64)
            P.recip(rs, rs)
            qn3 = qn.re("p (h d) -> p h d", d=64)
            P.tt("dve", qn3, px[:, 0:640].re("p (h d) -> p h d", d=64), bc3(rs, [128, 10, 64]), ALU.mult)
            P.tt("dve", qn3, qn3, gain, ALU.mult)
            if i < 16:
                P.dma("sp", cs, S.rope[rows, :])
                q5 = qn.re("p (h a j d) -> p h a j d", h=10, a=2, j=2, d=16)
                o5 = qr.re("p (h a j d) -> p h a j d", h=10, a=2, j=2, d=16)
                x1, x2 = q5[:, :, :, 0, :], q5[:, :, :, 1, :]
                cosb = T(cs[:, 0:32].re("p (a d) -> p a d", a=2).ap.unsqueeze(1).to_broadcast([128, 10, 2, 16]), cs.tok)
                sinb = T(cs[:, 32:64].re("p (a d) -> p a d", a=2).ap.unsqueeze(1).to_broadcast([128, 10, 2, 16]), cs.tok)
                tw = [rw[:, t, :].re("p (h a d) -> p h a d", h=10, a=2) for t in range(4)]
                P.tt("dve", tw[0], x1, cosb, ALU.mult)
                P.tt("pool", tw[1], x2, sinb, ALU.mult)
                P.tt("dve", tw[2], x2, cosb, ALU.mult)
                P.tt("dve", tw[3], x1, sinb, ALU.mult)
                P.tt("dve", o5[:, :, :, 0, :], tw[0], tw[1], ALU.subtract)
                P.tt("pool", o5[:, :, :, 1, :], tw[2], tw[3], ALU.add)
            else:
                P.copy("dve", qr, qn)
            P.dma("sp", S.QA[b][rows, :], qr[:, 0:512])
            P.dma("sp", S.KA[b][rows, :], qr[:, 512:640])
            P.copy("act", va, px[:, C_AV:C_AV + 128])
            P.dma("sp", S.VA[b][rows, :], va)
            P.dma("sp", S.YRI[b][rows, :], px[:, C_RW:C_RW + DRW])
            P.dma("sp", S.QN[b][rows, :], nb[:, 0:512])
            P.dma("sp", S.KN[b][rows, :], nb[:, 512:1024])
            P.dma("sp", S.VN[b][rows, :], nb[:, 1024:1536])
            P.dma("sp", S.GT[b][rows, :], gts)
    P.phase_end()


def load_kT(S, ksrc, KT, nh):
    P = S.P
    ktok = P.sb("ktok", [128, NT, nh * 64], BF16)
    P.dma("sp", ktok, ksrc.re("(i p) c -> p i c", p=128))
    per = 8 // nh
    for i0 in range(0, NT, per):
        n = min(per, NT - i0)
        pb = S.bank().bitcast(BF16)
        for ii in range(n):
            for h in range(nh):
                slot = h * per + ii
                P.tr(pb[0:64, slot * 128:(slot + 1) * 128], ktok[:, i0 + ii, h * 64:(h + 1) * 64], S.identb)
        src = pb[0:64, :].re("p (h i t) -> p h i t", h=nh, i=per)[:, :, 0:n, :]
        dst = KT[:, :, i0 * 128:(i0 + n) * 128].re("p h (i t) -> p h i t", t=128)
        P.copy("act", dst, src)


def load_v1(S, vsrc, V1, nh):
    P = S.P
    P.memset("pool", V1, 1.0)
    for i in range(NT):
        P.dma("sp", V1[:, i, :, 0:64], vsrc[i * 128:(i + 1) * 128, :].re("p (g d) -> p g d", g=nh))


def pv_and_store(S, PT, V1, keyl, kvh_of, ya, dstrows, pt_of=None):
    P = S.P
    nk = len(keyl)
    for hg in range(2):
        po = S.bank()
        for hh in range(4):
            h = hg * 4 + hh
            for idx, j in enumerate(keyl):
                pt = pt_of(idx, h) if pt_of is not None else PT[:, idx, h * 128:(h + 1) * 128]
                P.mm(po[:, hh * 65:(hh + 1) * 65], pt, V1[:, j, kvh_of(h), :],
                     start=(idx == 0), stop=(idx == nk - 1))
        po3 = po[:, 0:260].re("p (h e) -> p h e", e=65)
        rec = S.rec
        P.recip(rec, po3[:, :, 64])
        P.tt("dve", ya[:, hg * 256:(hg + 1) * 256].re("p (h d) -> p h d", d=64), po3[:, :, 0:64],
             bc3(rec, [128, 4, 64]), ALU.mult)
    P.dma("sp", dstrows, ya)


def load_qT(S, qsrc_rows, qtok, QT):
    P = S.P
    P.dma("sp", qtok, qsrc_rows)
    pb = S.bank().bitcast(BF16)
    for h in range(8):
        P.tr(pb[0:64, h * 128:(h + 1) * 128], qtok[:, h * 64:(h + 1) * 64], S.identb)
    P.copy("dve", QT, pb[0:64, :])


def att_phase(S, l, b, need_ctx, side=None):
    P = S.P
    P.phase_begin()
    KT = P.sb("KT", [64, 2, TT], BF16)
    load_kT(S, S.KA[b], KT, 2)
    V1 = P.sb("V1", [128, NT, 2, 65], BF16)
    load_v1(S, S.VA[b], V1, 2)
    S.rec = P.sb("rec", [128, 4], F32)
    qtoks = [P.sb("qtok%d" % i, [128, 512], BF16) for i in range(2)]
    QTs = [P.sb("QT%d" % i, [64, 1024], BF16) for i in range(2)]
    PTgs = [[P.sb("PTg%d_%d" % (g, q), [128, NT, 512], BF16) for g in range(2)] for q in range(2)]
    yas = [P.sb("ya%d" % i, [128, 512], F32) for i in range(2)]
    qtiles = list(range(16)) + ([16, 17] if need_ctx else [])
    sidegen = side() if side is not None else None
    for n, i in enumerate(qtiles):
        keyl = list(range(NT)) if i < 16 else [16, 17]
        QT, ya, PTg = QTs[n % 2], yas[n % 2], PTgs[n % 2]
        rows = slice(i * 128, (i + 1) * 128)
        load_qT(S, S.QA[b][rows, :], qtoks[n % 2], QT)
        for idx, j in enumerate(keyl):
            for g in range(2):
                ps = S.bank()
                P.mm(ps, KT[:, g, j * 128:(j + 1) * 128], QT[:, g * 512:(g + 1) * 512])
                P.act(PTg[g][:, idx, :], ps, AF.Exp, scale=0.125)
        pv_and_store(S, None, V1, keyl, lambda h: h // 4, ya, S.YA[b][rows, :],
                     pt_of=lambda idx, h: PTg[h // 4][:, idx, (h % 4) * 128:(h % 4 + 1) * 128])
        if sidegen is not None:
            next(sidegen, None)
    if sidegen is not None:
        for _ in sidegen:
            pass
    P.phase_end()


def na_phase(S, l, b, need_ctx, side=None):
    P = S.P
    P.phase_begin()
    KT = P.sb("KTn", [64, 8, TT], BF16)
    load_kT(S, S.KN[b], KT, 8)
    V1 = P.sb("V1n", [128, NT, 8, 65], BF16)
    load_v1(S, S.VN[b], V1, 8)
    S.rec = P.sb("rec", [128, 4], F32)
    qtoks = [P.sb("qtok%d" % i, [128, 512], BF16) for i in range(2)]
    QTs = [P.sb("QT%d" % i, [64, 1024], BF16) for i in range(2)]
    PTs = [P.sb("PT%d" % i, [128, 7, 1024], BF16) for i in range(2)]
    yas = [P.sb("ya%d" % i, [128, 512], F32) for i in range(2)]
    nbs = [P.sb("nab%d" % i, [128, 5, 8, 128], F32) for i in range(2)]
    tmps = [P.sb("natmp%d" % i, [128, 512], F32) for i in range(2)]
    qtiles = list(range(16)) + ([16, 17] if need_ctx else [])
    sidegen = side() if side is not None else None
    tc = 0
    for n, i in enumerate(qtiles):
        kl = (na_tiles(i) if i < 16 else []) + [(16, None), (17, None)]
        QT, PT, ya, nb = QTs[n % 2], PTs[n % 2], yas[n % 2], nbs[n % 2]
        rows = slice(i * 128, (i + 1) * 128)
        for idx, (j, pat) in enumerate(kl):
            if pat is not None:
                P.dma("sp", nb[:, idx], S.nab[l, pat])
        load_qT(S, S.QN[b][rows, :], qtoks[n % 2], QT)
        for idx, (j, pat) in enumerate(kl):
            for hg in range(2):
                ps = S.bank()
                for hh in range(4):
                    h = hg * 4 + hh
                    P.mm(ps[:, hh * 128:(hh + 1) * 128], KT[:, h, j * 128:(j + 1) * 128], QT[:, h * 128:(h + 1) * 128])
                if pat is not None:
                    tmp = tmps[tc % 2]
                    tc += 1
                    P.stt(tmp, ps, 0.125, nb[:, idx, hg * 4:(hg + 1) * 4, :].re("p h q -> p (h q)"), ALU.mult, ALU.add)
                    P.act(PT[:, idx, hg * 512:(hg + 1) * 512], tmp, AF.Exp)
                else:
                    P.act(PT[:, idx, hg * 512:(hg + 1) * 512], ps, AF.Exp, scale=0.125)
        pv_and_store(S, PT, V1, [j for j, _ in kl], lambda h: h, ya, S.YN[b][rows, :])
        if sidegen is not None:
            next(sidegen, None)
    if sidegen is not None:
        for _ in sidegen:
            pass
    P.phase_end()


def merge_phase(S, l, tiles):
    P = S.P
    P.phase_begin()
    wb = P.sb("wbr", [128, 12, D], BF16)
    for zz in range(3):
        for k in range(4):
            P.dma("pool", wb[:, zz * 4 + k, :], S.W["w_branch"][l, zz, k * 128:(k + 1) * 128, :])
    wout = P.sb("wout", [128, 8, D], BF16)
    for k in range(8):
        P.dma("pool", wout[:, k, :], S.W["w_out"][l, k * 128:(k + 1) * 128, :])
    lng = P.sb("lng", [128, D], F32)
    lnb = P.sb("lnb", [128, D], F32)
    P.dma("sp", lng, S.W["ln_g"][l, 1:2, :].bc([128, D]))
    P.dma("sp", lnb, S.W["ln_b"][l, 1:2, :].bc([128, D]))
    gt = P.sb("gt", [128, D], F32)
    xb = [P.sb("xb%d" % i, [128, D], F32) for i in range(2)]
    y32 = [P.sb("y32_%d" % i, [128, 1536], F32) for i in range(2)]
    gts = [P.sb("gtl%d" % i, [128, 3 * D], BF16) for i in range(2)]
    yb = P.sb("yb", [128, 1536], BF16)
    yT = P.sb("yT", [128, 1536], BF16)
    mg = P.sb("mg", [128, D], F32)
    tmp = P.sb("tmp", [128, 512], F32)
    mgb = P.sb("mgb", [128, D], BF16)
    mT = P.sb("mT", [128, D], BF16)
    tmp2 = P.sb("tmp2", [128, D], F32)
    z = P.sb("z", [128, D], F32)
    xo = P.sb("xo", [128, D], F32)
    wk = ln_work(P)

    def loads(n, b, i):
        rows = slice(i * 128, (i + 1) * 128)
        P.dma("sp", xb[n % 2], S.XA[b][rows, :])
        P.dma("sp", y32[n % 2][:, 0:512], S.YA[b][rows, :])
        P.dma("sp", y32[n % 2][:, 512:1024], S.YR[b][rows, :])
        P.dma("sp", y32[n % 2][:, 1024:1536], S.YN[b][rows, :])
        P.dma("sp", gts[n % 2], S.GT[b][rows, :])

    def mfront(n):
        yy = y32[n % 2]
        P.copy("act", yb[:, 0:768], yy[:, 0:768])
        P.copy("dve", yb[:, 768:1536], yy[:, 768:1536])
        transpose_bf(S, yb, yT, 12)

    for row in range(3):
        rt = row_tiles(row, tiles)
        if not rt:
            continue
        load_mod(S, row, [5], [gt])
        loads(0, *rt[0])
        for n, (b, i) in enumerate(rt):
            rows = slice(i * 128, (i + 1) * 128)
            if n + 1 < len(rt):
                loads(n + 1, *rt[n + 1])
            xt, yy, gg = xb[n % 2], y32[n % 2], gts[n % 2]
            if n == 0:
                mfront(0)
            for n2 in range(2):
                cs_ = slice(n2 * 512, (n2 + 1) * 512)
                for zz in range(3):
                    ps = S.bank()
                    for k in range(4):
                        P.mm(ps, yT[:, (zz * 4 + k) * 128:(zz * 4 + k + 1) * 128], wb[:, zz * 4 + k, cs_], start=(k == 0), stop=(k == 3))
                    gsl = gg[:, zz * D + n2 * 512: zz * D + (n2 + 1) * 512]
                    if zz == 0:
                        P.tt("dve", mg[:, cs_], ps, gsl, ALU.mult)
                    else:
                        P.tt("dve", tmp, ps, gsl, ALU.mult)
                        P.tt("dve", mg[:, cs_], mg[:, cs_], tmp, ALU.add)
            P.copy("act", mgb, mg)
            transpose_bf(S, mgb, mT, 8)
            pos = []
            for n2 in range(2):
                cs_ = slice(n2 * 512, (n2 + 1) * 512)
                po = S.bank()
                for k in range(8):
                    P.mm(po, mT[:, k * 128:(k + 1) * 128], wout[:, k, cs_], start=(k == 0), stop=(k == 7))
                pos.append(po)
            if n + 1 < len(rt):
                mfront(n + 1)
            for n2 in range(2):
                cs_ = slice(n2 * 512, (n2 + 1) * 512)
                P.tt("dve", tmp2[:, cs_], pos[n2], gt[:, cs_], ALU.mult)
            P.stt(z, xt, ALPHA, tmp2, ALU.mult, ALU.add)
            layernorm(S, z, lng, lnb, xo, wk)
            P.dma("sp", S.XB[b][rows, :], xo)
    P.phase_end()


def rwkv_phase(S, l, b, need_ctx):
    for _ in rwkv_prep(S, l, b):
        pass
    rwkv_scan(S, l, b)
    for _ in rwkv_readout(S, l, b, need_ctx):
        pass


def rwkv_prep(S, l, b, own_phase=True, dbuf=True):
    P = S.P
    W = S.W
    RS = S.RS
    if own_phase:
        P.phase_begin()
    mu0 = P.sb("mu0", [128, DRW], F32)
    mu1 = P.sb("mu1", [128, DRW], F32)
    P.dma("sp", mu0, W["rwkv_mu"][l, 0:1, :].bc([128, DRW]))
    P.dma("sp", mu1, W["rwkv_mu"][l, 1:2, :].bc([128, DRW]))
    w2 = P.sb("w2", [128, 512], F32)
    a2 = P.sb("a2", [128, 512], F32)
    g2 = P.sb("g2", [128, 512], F32)
    P.dma("sp", w2, W["rwkv_w2"][l].re("z r c -> (z r) c"))
    P.dma("sp", a2, W["rwkv_a2"][l].re("z r c -> (z r) c"))
    P.dma("sp", g2, W["rwkv_g2"][l])
    w0b = P.sb("w0b", [128, 2, 512], F32)
    a0b = P.sb("a0b", [128, 2, 512], F32)
    for z in range(2):
        P.dma("sp", w0b[:, z, :], W["rwkv_w0"][l, z:z + 1, :].bc([128, 512]))
        P.dma("sp", a0b[:, z, :], W["rwkv_a0"][l, z:z + 1, :].bc([128, 512]))
    kkb = P.sb("kkb", [128, 512], F32)
    kab = P.sb("kab", [128, 512], F32)
    P.dma("sp", kkb, W["rwkv_k_k"][l:l + 1, :].bc([128, 512]))
    P.dma("sp", kab, W["rwkv_k_a"][l:l + 1, :].bc([128, 512]))
    def mky(q):
        d = Ctx()
        d.yc = P.sb("yc%d" % q, [128, DRW], F32)
        d.yp = P.sb("yp%d" % q, [128, DRW], F32)
        d.yn = P.sb("yn%d" % q, [128, DRW], F32)
        return d

    def mkw(q):
        d = Ctx()
        d.lo = P.sb("lo%d" % q, [128, 3, 128], F32)
        d.loT = P.sb("loT%d" % q, [128, 3, 128], F32)
        d.t1 = P.sb("t1%d" % q, [128, 512], F32)
        d.lw = [P.sb("lw%d_%d" % (z, q), [128, 512], F32) for z in range(2)]
        d.ic = [P.sb("ic%d_%d" % (z, q), [128, 512], F32) for z in range(2)]
        d.kd = [P.sb("kd%d_%d" % (z, q), [128, 512], F32) for z in range(2)]
        d.bz = [P.sb("bz%d_%d" % (z, q), [128, 512], F32) for z in range(2)]
        for n in ["gg", "kx", "sq", "kk", "aa"]:
            setattr(d, n, P.sb("%s%d" % (n, q), [128, 512], F32))
        d.ss8 = P.sb("ss8%d" % q, [128, 8], F32)
        return d

    ysets = [mky(q) for q in range(3)]
    wsets = [mkw(q) for q in range(2)]
    Y = S.YRI[b]

    def loads(i):
        d = ysets[i % 3]
        r0 = i * 128
        rows = slice(r0, r0 + 128)
        P.dma("sp", d.yc, Y[rows, :])
        if i == 0 or i == 16:
            P.memset("pool", d.yp, 0.0)
            P.dma("sp", d.yp[1:128, :], Y[r0:r0 + 127, :])
        else:
            P.dma("sp", d.yp, Y[r0 - 1:r0 + 127, :])
        if i == 15 or i == 17:
            P.memset("pool", d.yn, 0.0)
            P.dma("sp", d.yn[0:127, :], Y[r0 + 1:r0 + 128, :])
        else:
            P.dma("sp", d.yn, Y[r0 + 1:r0 + 129, :])

    def stage_a(i):
        dy, dw = ysets[i % 3], wsets[i % 2]
        yc, yp, yn, lo, loT = dy.yc, dy.yp, dy.yn, dw.lo, dw.loT
        P.tt("dve", yp, yp, yc, ALU.subtract)
        P.tt("dve", yn, yn, yc, ALU.subtract)
        P.tt("dve", yp, yp, mu0, ALU.mult)
        P.tt("dve", yn, yn, mu1, ALU.mult)
        P.tt("dve", yc, yc, yp, ALU.add)
        P.tt("dve", yc, yc, yn, ALU.add)
        P.act(lo[:, 0, :], yc[:, 1536:1664], AF.Tanh)
        P.copy("act", lo[:, 1, :], yc[:, 1664:1792])
        P.act(lo[:, 2, :], yc[:, 1792:1920], AF.Sigmoid)
        pb = S.bank()
        for q in range(3):
            P.tr(pb[:, q * 128:(q + 1) * 128], lo[:, q, :], S.identf)
        P.copy("act", loT.re("p a t -> p (a t)"), pb[:, 0:384])

    def stage_b(i):
        dy, d = ysets[i % 3], wsets[i % 2]
        yc, loT, t1, lw, ic, kd, bz = dy.yc, d.loT, d.t1, d.lw, d.ic, d.kd, d.bz
        gg, kx, sq, kk, aa, ss8 = d.gg, d.kx, d.sq, d.kk, d.aa, d.ss8
        rows = slice(i * 128, (i + 1) * 128)
        r_, k_, v_ = yc[:, 0:512], yc[:, 512:1024], yc[:, 1024:1536]
        for z in range(2):
            ps = S.bank()
            P.mm(ps, loT[z * 64:(z + 1) * 64, 0, :], w2[z * 64:(z + 1) * 64, :])
            P.tt("dve", t1, ps, w0b[:, z, :], ALU.add)
            P.act(t1, t1, AF.Sigmoid)
            P.ts("dve", lw[z], t1, -float(np.exp(-0.5)), op0=ALU.mult)
            P.dma("sp", RS["LW%d" % z][rows, :], lw[z])
            ps = S.bank()
            P.mm(ps, loT[z * 64:(z + 1) * 64, 1, :], a2[z * 64:(z + 1) * 64, :])
            P.tt("dve", ic[z], ps, a0b[:, z, :], ALU.add)
            P.act(ic[z], ic[z], AF.Sigmoid)
        ps = S.bank()
        P.mm(ps, loT[:, 2, :], g2)
        P.copy("act", gg, ps)
        P.dma("sp", RS["G"][rows, :], gg)
        P.tt("dve", kx, k_, kkb, ALU.mult)
        P.tt("pool", sq, kx, kx, ALU.mult)
        P.reduce(ss8, sq.re("p (h d) -> p h d", d=64), ALU.add)
        P.act(ss8, ss8, AF.Sqrt, bias=S.eps[:, 3:4], scale=1.0)
        P.recip(ss8, ss8)
        P.tt("dve", kk.re("p (h d) -> p h d", d=64), kx.re("p (h d) -> p h d", d=64), bc3(ss8, [128, 8, 64]), ALU.mult)
        P.act(aa, kk, AF.Identity, scale=-1.0)
        P.dma("sp", RS["A"][rows, :], aa)
        for z in range(2):
            P.stt(t1, ic[z], -1.0, kab, ALU.add, ALU.mult)
            P.stt(kd[z], t1, 1.0, k_, ALU.add, ALU.mult)
            P.dma("sp", RS["KD%d" % z][rows, :], kd[z])
            P.tt("pool", bz[z], kk, ic[z], ALU.mult)
            P.dma("sp", RS["B%d" % z][rows, :], bz[z])
        P.dma("sp", RS["R"][rows, :], r_)
        P.dma("sp", RS["K"][rows, :], k_)
        P.dma("sp", RS["V"][rows, :], v_)

    loads(0)
    loads(1)
    stage_a(0)
    for i in range(NT):
        if i + 2 < NT:
            loads(i + 2)
        if i + 1 < NT:
            stage_a(i + 1)
        stage_b(i)
        yield
    if own_phase:
        P.phase_end()


def rwkv_scan(S, l, b, nsteps=NT):
    P = S.P
    RS = S.RS
    P.phase_begin()
    ident, SL, SU, IL, IU, ones = [S.cf[:, q, :] for q in range(6)]
    order = [[16, 17] + list(range(16)), [17, 16] + list(range(15, -1, -1))]
    mN = [SL, SU]
    mTs = [SU, SL]
    mTi = [IU, IL]
    mC = [IU, IL]

    def mk(z):
        d = Ctx()
        for n in ["lw", "kd", "bb", "aa", "rr", "vv", "cum", "d1", "d2", "e1", "e2", "At", "Bt", "Kt", "Rt", "Bg", "Kg",
                  "Zs", "Us", "Os"]:
            setattr(d, n, P.sb("%s%d" % (n, z), [128, 512], F32))
        for n in ["AtT", "BtT", "KtT", "RtT"]:
            setattr(d, n, P.sb("%s%d" % (n, z), [64, 8, 128], F32))
        for n in ["Aak", "Arb", "Ark"]:
            setattr(d, n, [P.sb("%s%d_%d" % (n, z, hg), [128, 4, 128], F32) for hg in range(2)])
        for n in ["M", "Mt", "Pt"]:
            setattr(d, n, [[P.sb("%s%d_%d_%d" % (n, z, hg, q), [128, 4, 128], F32) for q in range(2)] for hg in range(2)])
        d.gL = P.sb("gL%d" % z, [64, 8], F32)
        d.H = P.sb("H%d" % z, [64, 8, 64], F32)
        P.memset("pool", d.H, 0.0)
        return d

    Dz = [mk(0), mk(1)]

    def mb(m):
        return T(m.ap.unsqueeze(1).to_broadcast([128, 4, 128]), m.tok)

    def b4(ps):
        return ps.re("p (h t) -> p h t", h=4)

    def opd(XT, h):
        return XT[:, h, :]

    def pre(z, c):
        d = Dz[z]
        rows = slice(c * 128, (c + 1) * 128)
        P.dma("sp", d.lw, RS["LW%d" % z][rows, :])
        P.dma("sp", d.kd, RS["KD%d" % z][rows, :])
        P.dma("sp", d.bb, RS["B%d" % z][rows, :])
        P.dma("sp", d.aa, RS["A"][rows, :])
        P.dma("sp", d.rr, RS["R"][rows, :])
        P.dma("sp", d.vv, RS["V"][rows, :])
        pc = S.bank()
        P.mm(pc, mC[z], d.lw)
        pl = S.bank()
        P.mm(pl, ones, d.lw)
        pg = S.bank()
        for h in range(8):
            P.mm(pg[0:64, 2 * h:2 * h + 2], d.lw[:, h * 64:(h + 1) * 64], ones[:, 0:2])
        P.act(d.gL, pg[0:64, 0:16].re("p (h two) -> p h two", two=2)[:, :, 0], AF.Exp)
        P.copy("act", d.cum, pc)
        P.tt("dve", d.d1, d.cum, d.lw, ALU.subtract)
        P.tt("dve", d.d2, pl, d.cum, ALU.subtract)
        P.act(d.e1, d.cum, AF.Exp)
        P.act(d.e2, d.cum, AF.Exp, scale=-1.0)
        P.act(d.d1, d.d1, AF.Exp)
        P.act(d.d2, d.d2, AF.Exp)
        P.tt("dve", d.At, d.aa, d.d1, ALU.mult)
        P.tt("dve", d.Bt, d.bb, d.e2, ALU.mult)
        P.tt("dve", d.Kt, d.kd, d.e2, ALU.mult)
        P.tt("dve", d.Rt, d.rr, d.e1, ALU.mult)
        P.tt("dve", d.Bg, d.bb, d.d2, ALU.mult)
        P.tt("pool", d.Kg, d.kd, d.d2, ALU.mult)
        qi = 0
        for (src, dst) in [(d.At, d.AtT), (d.Bt, d.BtT), (d.Kt, d.KtT), (d.Rt, d.RtT)]:
            for hg in range(2):
                pb = S.bank()
                for hh in range(4):
                    h = hg * 4 + hh
                    P.tr(pb[0:64, hh * 128:(hh + 1) * 128], src[:, h * 64:(h + 1) * 64], ident)
                P.copy("act" if qi % 2 == 0 else "dve", dst[:, hg * 4:(hg + 1) * 4, :].re("p a t -> p (a t)"), pb[0:64, :])
                qi += 1
        return [inv_chain(z, d, hg) for hg in range(2)]

    def inv_chain(z, d, hg):
        if True:
            M, Mt, Pt = d.M[hg], d.Mt[hg], d.Pt[hg]
            specs = [(d.AtT, d.BtT, mN[z], M[0]), (d.BtT, d.AtT, mTs[z], Mt[0]), (d.KtT, d.AtT, mTs[z], d.Aak[hg]),
                     (d.BtT, d.RtT, mTi[z], d.Arb[hg]), (d.KtT, d.RtT, mTi[z], d.Ark[hg])]
            for (LT, RT_, msk, dst) in specs:
                ps = S.bank()
                for hh in range(4):
                    h = hg * 4 + hh
                    P.mm(ps[:, hh * 128:(hh + 1) * 128], opd(LT, h), opd(RT_, h))
                P.tt("dve", dst, b4(ps), mb(msk), ALU.mult)
            P.tt("dve", Pt[0], Mt[0], mb(ident), ALU.add)
            yield
            cur = 0
            for j in range(1, 7):
                nxt = 1 - cur
                if j < 6:
                    ps1 = S.bank()
                    for hh in range(4):
                        P.mm(ps1[:, hh * 128:(hh + 1) * 128], M[cur][:, hh, :], Mt[cur][:, hh, :])
                    P.copy("act", Mt[nxt], b4(ps1))
                ps2 = S.bank()
                for hh in range(4):
                    P.mm(ps2[:, hh * 128:(hh + 1) * 128], Mt[cur][:, hh, :], M[cur][:, hh, :])
                P.copy("dve", M[nxt], b4(ps2))
                ps3 = S.bank()
                for hh in range(4):
                    P.mm(ps3[:, hh * 128:(hh + 1) * 128], M[nxt][:, hh, :], Pt[cur][:, hh, :])
                P.tt("dve", Pt[nxt], b4(ps3), Pt[cur], ALU.add)
                cur = nxt
                yield
            assert cur == 0

    def seq_stages(z, c):
        d = Dz[z]
        rows = slice(c * 128, (c + 1) * 128)
        H = d.H

        def st_z():
            pz = S.bank()
            for h in range(8):
                hc = slice(h * 64, (h + 1) * 64)
                P.mm(pz[:, hc], d.AtT[:, h, :], H[:, h, :], start=True, stop=False)
                P.mm(pz[:, hc], d.Aak[h // 4][:, h % 4, :], d.vv[:, hc], start=False, stop=True)
            P.copy("act", d.Zs, pz)

        def st_u():
            pu = S.bank()
            for h in range(8):
                hc = slice(h * 64, (h + 1) * 64)
                P.mm(pu[:, hc], d.Pt[h // 4][0][:, h % 4, :], d.Zs[:, hc])
            P.copy("dve", d.Us, pu)

        def st_o():
            po = S.bank()
            for h in range(8):
                hc = slice(h * 64, (h + 1) * 64)
                P.mm(po[:, hc], d.RtT[:, h, :], H[:, h, :], start=True, stop=False)
                P.mm(po[:, hc], d.Arb[h // 4][:, h % 4, :], d.Us[:, hc], start=False, stop=False)
                P.mm(po[:, hc], d.Ark[h // 4][:, h % 4, :], d.vv[:, hc], start=False, stop=True)
            P.copy("act", d.Os, po)
            P.dma("sp", RS["O%d" % z][rows, :], d.Os)

        def st_h():
            ph = S.bank()
            for h in range(8):
                hc = slice(h * 64, (h + 1) * 64)
                P.mm(ph[0:64, hc], d.Bg[:, hc], d.Us[:, hc], start=True, stop=False)
                P.mm(ph[0:64, hc], d.Kg[:, hc], d.vv[:, hc], start=False, stop=True)
            P.tt("dve", H, H, bc3(d.gL, [64, 8, 64]), ALU.mult)
            P.tt("dve", H, H, ph[0:64, :].re("p (h v) -> p h v", v=64), ALU.add)

        return [st_z, st_u, st_o, st_h]

    for step in range(nsteps):
        chains = []
        for z in range(2):
            chains += pre(z, order[z][step])
        live = list(chains)
        while live:
            for g in list(live):
                try:
                    next(g)
                except StopIteration:
                    live.remove(g)
        stg = [seq_stages(z, order[z][step]) for z in range(2)]
        for k in range(4):
            for z in range(2):
                stg[z][k]()
    P.phase_end()


def rwkv_readout(S, l, b, need_ctx, own_phase=True):
    P = S.P
    RS = S.RS
    W = S.W
    if own_phase:
        P.phase_begin()
    gng = P.sb("gng", [128, 512], F32)
    gnb = P.sb("gnb", [128, 512], F32)
    rkb = P.sb("rkb", [128, 512], F32)
    P.dma("sp", gng, W["rwkv_gn_g"][l:l + 1, :].bc([128, 512]))
    P.dma("sp", gnb, W["rwkv_gn_b"][l:l + 1, :].bc([128, 512]))
    P.dma("sp", rkb, W["rwkv_r_k"][l:l + 1, :].bc([128, 512]))
    names = ["O0", "O1", "R", "K", "V", "G"]
    bufs = [{n: P.sb("ro_%s%d" % (n, q), [128, 512], F32) for n in names} for q in range(2)]
    o = P.sb("o", [128, 512], F32)
    sq = P.sb("sq", [128, 512], F32)
    s1 = P.sb("s1", [128, 8], F32)
    s2 = P.sb("s2", [128, 8], F32)
    m2 = P.sb("m2", [128, 8], F32)
    rk = P.sb("rk", [128, 512], F32)
    sb8 = P.sb("sb8", [128, 8], F32)
    yo = [P.sb("yo%d" % q, [128, 512], F32) for q in range(2)]
    tiles = list(range(16)) + ([16, 17] if need_ctx else [])

    def v3(t):
        return t.re("p (h d) -> p h d", d=64)

    def loads(n, i):
        for nm in names:
            P.dma("sp", bufs[n % 2][nm], RS[nm][i * 128:(i + 1) * 128, :])

    loads(0, tiles[0])
    for n, i in enumerate(tiles):
        if n + 1 < len(tiles):
            loads(n + 1, tiles[n + 1])
        B = bufs[n % 2]
        y = yo[n % 2]
        P.tt("dve", o, B["O0"], B["O1"], ALU.add)
        P.reduce(s1, v3(o), ALU.add)
        P.tt("dve", sq, o, o, ALU.mult)
        P.reduce(s2, v3(sq), ALU.add)
        P.ts("dve", s1, s1, 1.0 / 64, op0=ALU.mult)
        P.tt("dve", m2, s1, s1, ALU.mult)
        P.stt(s2, s2, 1.0 / 64, m2, ALU.mult, ALU.subtract)
        P.act(s2, s2, AF.Sqrt, bias=S.eps[:, 2:3], scale=1.0)
        P.recip(s2, s2)
        P.tt("dve", v3(o), v3(o), bc3(s1, [128, 8, 64]), ALU.subtract)
        P.tt("dve", v3(o), v3(o), bc3(s2, [128, 8, 64]), ALU.mult)
        P.tt("dve", o, o, gng, ALU.mult)
        P.tt("dve", o, o, gnb, ALU.add)
        P.tt("dve", rk, B["R"], B["K"], ALU.mult)
        P.tt("dve", rk, rk, rkb, ALU.mult)
        P.reduce(sb8, v3(rk), ALU.add)
        P.tt("dve", v3(rk), v3(B["V"]), bc3(sb8, [128, 8, 64]), ALU.mult)
        P.tt("dve", o, o, rk, ALU.add)
        P.tt("dve", y, o, B["G"], ALU.mult)
        P.dma("sp", S.YR[b][i * 128:(i + 1) * 128, :], y)
        yield
    if own_phase:
        P.phase_end()
```
